# Optimizing a Trainium2 kernel written in Bass

```python
import jax, jax.numpy as jnp
from jax import lax
import numpy as np

D_MODEL = 1024
BATCH = 1
SEQ = 16384
DEPTH = 1

PLE_DIM = 256
EPS = 1e-6
RET_HEADS = 4
RET_QK_DIM = 256
RET_V_DIM = 512
RET_CHUNK = 128
RET_ROPE_BASE = 10000.0
DIL_GROUPS = ((128, 1), (512, 4), (2048, 16))
DIL_SLOTS = 8
DIL_HEAD_DIM = 64
DIL_BLOCK = 128
ROPE_THETA = 500000.0
ROPE_DIM = DIL_HEAD_DIM // 4
D_FF = 4 * D_MODEL

RET_QK_W = RET_HEADS * RET_QK_DIM
RET_V_W = RET_HEADS * RET_V_DIM
DIL_HEADS = len(DIL_GROUPS) * DIL_SLOTS
DIL_W = DIL_HEADS * DIL_HEAD_DIM
DIL_OUT_W = DIL_SLOTS * DIL_HEAD_DIM
SPLITS = (RET_QK_W, RET_QK_W, RET_V_W, RET_V_W, DIL_W, DIL_W, DIL_W, D_MODEL, D_MODEL)
IN_W = RET_QK_W * 2 + RET_V_W * 2 + DIL_W * 3 + D_MODEL * 2

kernel_name = "hybrid_retention_dilated_attn_block"


def rmsnorm(x, g=None):
    xf = x.astype(jnp.float32)
    y = xf * lax.rsqrt(jnp.mean(xf * xf, axis=-1, keepdims=True) + EPS)
    if g is not None:
        y = y * g.astype(jnp.float32)
    return y.astype(x.dtype)


def rope(x, cos, sin, rot_dim):
    half = rot_dim // 2
    x1 = x[..., :half]
    x2 = x[..., half:rot_dim]
    return jnp.concatenate([x1 * cos - x2 * sin, x2 * cos + x1 * sin, x[..., rot_dim:]], axis=-1)


def retention(q, k, v, pos):
    B, S, H, dk = q.shape
    dv = v.shape[-1]
    f32 = jnp.float32
    half = dk // 2
    inv_freq = 1.0 / (RET_ROPE_BASE ** jnp.linspace(0.0, 1.0, half, dtype=f32))
    ang = pos.astype(f32)[:, :, None, None] * inv_freq
    cos, sin = jnp.cos(ang), jnp.sin(ang)
    q = rope(q.astype(f32), cos, sin, dk)
    k = rope(k.astype(f32), cos, sin, dk) * (dk ** -0.5)
    v = v.astype(f32)
    C = RET_CHUNK
    Sp = -(-S // C) * C
    padw = ((0, 0), (0, Sp - S), (0, 0), (0, 0))
    nc = Sp // C
    qc = jnp.pad(q, padw).reshape(B, nc, C, H, dk)
    kc = jnp.pad(k, padw).reshape(B, nc, C, H, dk)
    vc = jnp.pad(v, padw).reshape(B, nc, C, H, dv)
    log_gamma = jnp.log1p(-(2.0 ** (-5.0 - jnp.arange(H, dtype=f32))))
    idx = jnp.arange(C, dtype=f32)
    diff = idx[:, None] - idx[None, :]
    decay = jnp.where(diff[None] >= 0, jnp.exp(jnp.maximum(diff, 0.0)[None] * log_gamma[:, None, None]), 0.0)
    scores = jnp.einsum('bnihd,bnjhd->bnhij', qc, kc) * decay[None, None]
    inner = jnp.einsum('bnhij,bnjhe->bnihe', scores, vc)
    q_dec = jnp.exp((idx + 1.0)[:, None] * log_gamma[None, :])
    k_dec = jnp.exp((C - 1.0 - idx)[:, None] * log_gamma[None, :])
    chunk_dec = jnp.exp(C * log_gamma)

    def step(R, inp):
        qn, kn, vn = inp
        cross = jnp.einsum('bihd,bhde->bihe', qn * q_dec[None, :, :, None], R)
        R = R * chunk_dec[None, :, None, None] + jnp.einsum('bjhd,bjhe->bhde', kn * k_dec[None, :, :, None], vn)
        return R, cross

    R0 = jnp.zeros((B, H, dk, dv), f32)
    _, cross = lax.scan(step, R0, (jnp.moveaxis(qc, 1, 0), jnp.moveaxis(kc, 1, 0), jnp.moveaxis(vc, 1, 0)))
    y = inner + jnp.moveaxis(cross, 0, 1)
    return y.reshape(B, Sp, H, dv)[:, :S]


def dilated_group(q, k, v, window, dilation):
    B, S, Hg, hd = q.shape
    f32 = jnp.float32
    band = window // dilation
    QB = DIL_BLOCK
    seg = dilation * QB
    Sp = -(-S // seg) * seg
    L = Sp // dilation
    nb = L // QB
    padw = ((0, 0), (0, Sp - S), (0, 0), (0, 0))

    def to_streams(t):
        t = jnp.pad(t.astype(f32), padw).reshape(B, L, dilation, Hg, hd)
        return t.transpose(0, 2, 1, 3, 4).reshape(B, dilation, nb, QB, Hg, hd)

    def with_prev(t):
        prev = jnp.pad(t[:, :, :-1], ((0, 0), (0, 0), (1, 0), (0, 0), (0, 0), (0, 0)))
        return jnp.concatenate([prev, t], axis=3)

    qs = to_streams(q)
    kk = with_prev(to_streams(k))
    vv = with_prev(to_streams(v))
    s = jnp.einsum('bcnqhe,bcnkhe->bcnhqk', qs, kk) * (hd ** -0.5)
    qi = jnp.arange(QB)[:, None]
    kj = jnp.arange(2 * QB)[None, :] - QB
    dist = qi - kj
    in_band = (dist >= 0) & (dist <= band)
    key_exists = (jnp.arange(nb)[:, None, None] * QB + kj[None]) >= 0
    mask = in_band[None] & key_exists
    s = jnp.where(mask[None, None, :, None], s, -1e30)
    lse = jax.nn.logsumexp(s, axis=-1)
    pr = jnp.exp(s - lse[..., None])
    o = jnp.einsum('bcnhqk,bcnkhe->bcnqhe', pr, vv)
    o = o.reshape(B, dilation, L, Hg, hd).transpose(0, 2, 1, 3, 4).reshape(B, Sp, Hg, hd)[:, :S]
    lse = lse.transpose(0, 1, 2, 4, 3).reshape(B, dilation, L, Hg).transpose(0, 2, 1, 3).reshape(B, Sp, Hg)[:, :S]
    return o, lse


def dilated_attention(q, k, v, pos):
    f32 = jnp.float32
    freqs = ROPE_THETA ** (-jnp.arange(0, ROPE_DIM, 2, dtype=f32) / ROPE_DIM)
    ang = pos.astype(f32)[:, :, None, None] * freqs
    cos, sin = jnp.cos(ang), jnp.sin(ang)
    q = rope(q.astype(f32), cos, sin, ROPE_DIM)
    k = rope(k.astype(f32), cos, sin, ROPE_DIM)
    outs, lses = [], []
    for g, (window, dilation) in enumerate(DIL_GROUPS):
        sl = slice(g * DIL_SLOTS, (g + 1) * DIL_SLOTS)
        o, l = dilated_group(q[:, :, sl], k[:, :, sl], v[:, :, sl], window, dilation)
        outs.append(o)
        lses.append(l)
    w = jax.nn.softmax(jnp.stack(lses, axis=0), axis=0)
    return jnp.sum(w[..., None] * jnp.stack(outs, axis=0), axis=0)


def setup_inputs(seed: int = 0) -> dict:
    key = jax.random.key(seed)
    ks = jax.random.split(key, 24)
    f32 = jnp.float32
    nrm = lambda k, shape, fan_in: jax.random.normal(k, shape, f32) * (fan_in ** -0.5)
    gain = lambda k: 1.0 + 0.02 * jax.random.normal(k, (DEPTH, D_MODEL), f32)
    x = jax.random.normal(ks[0], (BATCH, SEQ, D_MODEL), f32)
    p = jax.random.normal(ks[1], (DEPTH, BATCH, SEQ, PLE_DIM), f32)
    positions = jnp.broadcast_to(jnp.arange(SEQ, dtype=jnp.int32)[None, :], (BATCH, SEQ))
    return {
        "x": x,
        "p": p,
        "positions": positions,
        "w_in": nrm(ks[2], (DEPTH, D_MODEL, IN_W), D_MODEL),
        "b_gate": 0.01 * jax.random.normal(ks[3], (DEPTH, 2, D_MODEL), f32),
        "w_ret_out": nrm(ks[4], (DEPTH, RET_V_W, D_MODEL), RET_V_W),
        "w_dil_out": nrm(ks[5], (DEPTH, DIL_OUT_W, D_MODEL), DIL_OUT_W),
        "w_o": nrm(ks[6], (DEPTH, D_MODEL, D_MODEL), D_MODEL),
        "g_pre_mix": gain(ks[7]),
        "g_post_mix": gain(ks[8]),
        "g_pre_mlp": gain(ks[9]),
        "g_post_mlp": gain(ks[10]),
        "w_up": nrm(ks[11], (DEPTH, D_MODEL, D_FF), D_MODEL),
        "w_down": nrm(ks[12], (DEPTH, D_FF, D_MODEL), D_FF),
        "g_pre_ple": gain(ks[13]),
        "w_ple_gate": nrm(ks[14], (DEPTH, D_MODEL, D_MODEL), D_MODEL),
        "b_ple_gate": 0.01 * jax.random.normal(ks[15], (DEPTH, D_MODEL), f32),
        "w_ple_in": nrm(ks[16], (DEPTH, PLE_DIM, D_MODEL), PLE_DIM),
        "g_post_ple": gain(ks[17]),
    }


def reference(x, p, positions, w_in, b_gate, w_ret_out, w_dil_out, w_o, g_pre_mix, g_post_mix,
              g_pre_mlp, g_post_mlp, w_up, w_down, g_pre_ple, w_ple_gate, b_ple_gate, w_ple_in,
              g_post_ple):
    B, S, _ = x.shape
    split_points = [int(s) for s in np.cumsum(SPLITS)[:-1]]
    h = x
    for i in range(DEPTH):
        u = rmsnorm(h, g_pre_mix[i])
        proj = u @ w_in[i]
        rq, rk, rv, rg, aq, ak, av, gr, ga = jnp.split(proj, split_points, axis=-1)
        yr = retention(rq.reshape(B, S, RET_HEADS, RET_QK_DIM), rk.reshape(B, S, RET_HEADS, RET_QK_DIM),
                       rv.reshape(B, S, RET_HEADS, RET_V_DIM), positions)
        yr = rmsnorm(yr) * jax.nn.silu(rg.reshape(B, S, RET_HEADS, RET_V_DIM).astype(jnp.float32))
        ya_branch = yr.reshape(B, S, RET_V_W).astype(h.dtype) @ w_ret_out[i]
        ya = dilated_attention(aq.reshape(B, S, DIL_HEADS, DIL_HEAD_DIM), ak.reshape(B, S, DIL_HEADS, DIL_HEAD_DIM),
                               av.reshape(B, S, DIL_HEADS, DIL_HEAD_DIM), positions)
        yb_branch = ya.reshape(B, S, DIL_OUT_W).astype(h.dtype) @ w_dil_out[i]
        mixed = jax.nn.sigmoid(gr + b_gate[i, 0]) * ya_branch + jax.nn.sigmoid(ga + b_gate[i, 1]) * yb_branch
        h = h + rmsnorm(mixed @ w_o[i], g_post_mix[i])
        v2 = rmsnorm(h, g_pre_mlp[i])
        f = jnp.square(jax.nn.relu(v2 @ w_up[i])) @ w_down[i]
        h = h + rmsnorm(f, g_post_mlp[i])
        gate = jax.nn.sigmoid(rmsnorm(h, g_pre_ple[i]) @ w_ple_gate[i] + b_ple_gate[i])
        e = p[i].astype(h.dtype) @ w_ple_in[i]
        h = h + rmsnorm(gate * e, g_post_ple[i])
    return h
```

```python
import math
import numpy as np
import concourse.bass as bass
import concourse.mybir as mb
from concourse.bass_utils import run_bass_kernel_spmd

F32 = mb.dt.float32
BF16 = mb.dt.bfloat16
I32 = mb.dt.int32
AF = mb.ActivationFunctionType
ALU = mb.AluOpType

NCORES = 8
TOK = 2048
EPS = 1e-6
SAME_ENG_SYNC = True
import os
ROPE_ENG = os.environ.get('ROPE_ENG', 'pool')
RSTOP = os.environ.get('RSTOP', '')
ARENA_BASE = 16640
ARENA_END = 229376

INV2PI = float(np.float32(1.0 / (2.0 * math.pi)))
C1 = 6.28125
C2 = 2.0 * math.pi - 6.28125
PI = math.pi
PIC = 3.14159
HALFPI = math.pi / 2.0
TWOPI = 2.0 * math.pi

O_INVF = 0
O_DT = 128
O_KSC = 640
O_QDEC = 644
O_COEF = 648
O_FREQD = 680
O_SSIGN = 681
O_BGT = 682
NCST = 704
O_ID = 0
O_PERM = 128
O_MK = 256
O_MK0 = 512
O_ONES = 768
NC16 = 1024


class Op:
    __slots__ = ("eng", "fn", "r", "w", "stream", "inc", "epoch", "sig", "val", "deps")


class Prog:
    def __init__(self):
        self.ops = []
        self.epoch = 0

    def add(self, eng, fn, r=(), w=(), stream=None, inc=16):
        op = Op()
        op.eng = eng
        op.fn = fn
        r = tuple(r)
        w = tuple(w)
        if eng != "pe":
            extra = tuple(k for k in r if k.startswith("ps") and k[2:].isdigit() and k not in w)
            w = w + extra
        op.r = r
        op.w = w
        op.stream = stream
        op.inc = inc
        op.epoch = self.epoch
        op.sig = stream is not None
        op.val = 0
        op.deps = ()
        self.ops.append(op)
        return op

    def barrier(self):
        self.epoch += 1

    def mm(self, out, lhsT, rhs, start, stop, r, w):
        self.add("pe", lambda e: e.matmul(out, lhsT, rhs, start=start, stop=stop), r, w)

    def tr(self, out, in_, ident, r, w):
        self.add("pe", lambda e: e.transpose(out, in_, ident), r, w)

    def act(self, out, in_, func, r, w, bias=None, scale=None, accum=None):
        kw = {}
        if bias is not None:
            kw["bias"] = bias
        if scale is not None:
            kw["scale"] = scale
        if accum is not None:
            kw["accum_out"] = accum
        self.add("act", lambda e: e.activation(out, in_, func, **kw), r, w)

    def tt(self, eng, out, in0, in1, op, r, w):
        self.add(eng, lambda e: e.tensor_tensor(out, in0, in1, op), r, w)

    def stt(self, eng, out, in0, scalar, in1, op0, op1, r, w):
        self.add(eng, lambda e: e.scalar_tensor_tensor(out, in0, scalar, in1, op0, op1), r, w)

    def ts(self, eng, out, in0, s1, s2, op0, op1, r, w):
        if op1 is None:
            self.add(eng, lambda e: e.tensor_scalar(out, in0, s1, None, op0), r, w)
        else:
            self.add(eng, lambda e: e.tensor_scalar(out, in0, s1, s2, op0, op1), r, w)

    def cp(self, eng, out, in_, r, w):
        if eng == "act":
            self.add(eng, lambda e: e.copy(out, in_), r, w)
        else:
            self.add(eng, lambda e: e.tensor_copy(out, in_), r, w)

    def dma(self, eng, out, in_, stream, r, w):
        self.add(eng, lambda e: e.dma_start(out=out, in_=in_), r, w, stream=stream)

    def analyse(self):
        ops = self.ops
        last_w, readers, last_by_prod = {}, {}, {}
        bar_deps = frozenset()
        cur_epoch = 0
        for i, op in enumerate(ops):
            if op.epoch != cur_epoch:
                cur_epoch = op.epoch
                bar_deps = frozenset(last_by_prod.values())
                last_w, readers = {}, {}
            deps = set(bar_deps)
            for k in op.r:
                j = last_w.get(k)
                if j is not None:
                    deps.add(j)
            for k in op.w:
                j = last_w.get(k)
                rd = readers.get(k)
                if j is not None:
                    if not (op.stream is not None and ops[j].stream == op.stream and not rd):
                        deps.add(j)
                if rd:
                    deps.update(rd.values())
            pid = (op.stream, i) if op.stream else op.eng
            for k in op.r:
                readers.setdefault(k, {})[pid] = i
            for k in op.w:
                last_w[k] = i
                readers[k] = {}
            last_by_prod[op.stream or op.eng] = i
            fdeps = []
            for j in deps:
                pj = ops[j]
                if pj.stream is None and op.stream is None and pj.eng == op.eng:
                    if op.eng == "pe" or not SAME_ENG_SYNC:
                        continue
                fdeps.append(j)
                pj.sig = True
            op.deps = fdeps
        cnt = {}
        for op in ops:
            if op.stream:
                cnt[op.stream] = cnt.get(op.stream, 0) + (op.inc if op.inc else 1)
                op.val = cnt[op.stream]
            elif op.sig:
                cnt[op.eng] = cnt.get(op.eng, 0) + 1
                op.val = cnt[op.eng]
        return cnt

    def emit(self, nc):
        import contextlib
        cnt = self.analyse()
        ops = self.ops
        names = sorted(cnt.keys())
        with contextlib.ExitStack() as st:
            sems = {n: st.enter_context(nc.semaphore("s_" + n)) for n in names}
            block = st.enter_context(nc.Block())

            def run(engname):
                def body(e):
                    waited = {}
                    for op in ops:
                        if op.eng != engname:
                            continue
                        need = {}
                        for j in op.deps:
                            pj = ops[j]
                            key = pj.stream or pj.eng
                            if pj.val > need.get(key, 0):
                                need[key] = pj.val
                        for key, val in need.items():
                            if waited.get(key, 0) < val:
                                e.wait_ge(sems[key], val)
                                waited[key] = val
                        if op.fn is None:
                            continue
                        ins = op.fn(e)
                        if op.stream:
                            if op.inc:
                                ins.then_inc(sems[op.stream], op.inc)
                            else:
                                ins.then_inc(sems[op.stream])
                        elif op.sig:
                            ins.then_inc(sems[op.eng], 1)
                return body

            block.tensor(run("pe"))
            block.scalar(run("act"))
            block.vector(run("dve"))
            block.gpsimd(run("pool"))
            block.sync(run("sp"))


def bcast(ap, axis, n):
    dims = [list(d) for d in ap.ap]
    dims.insert(axis, [0, n])
    return bass.AP(ap.tensor, ap.offset, dims)


def build(dbg=(), stop=None, fake_ag=False):
    nc = bass.Bass("TRN2", target_bir_lowering=False)
    P = Prog()

    def din(name, shape, dt=F32):
        return nc.dram_tensor(name, list(shape), dt, kind="ExternalInput")

    dbgkeys = []

    def dbgs():
        k = f"dbg{len(dbgkeys)}"
        dbgkeys.append(k)
        return k

    def finalize():
        P.add("sp", None, list(dbgkeys), [])
        P.emit(nc)
        return nc

    x_all = din("x_all", [4096, 1024])
    p_own = din("p_own", [2048, 256])
    pos_bc = din("pos_bc", [128, 4096], I32)
    pos_tm = din("pos_tm", [128, 16], I32)
    w_in = din("w_in", [1024, 12800])
    w_ret_out = din("w_ret_out", [2048, 1024])
    w_dil_out = din("w_dil_out", [512, 1024])
    w_o = din("w_o", [1024, 1024])
    w_up = din("w_up", [1024, 4096])
    w_down = din("w_down", [4096, 1024])
    w_ple_gate = din("w_ple_gate", [1024, 1024])
    w_ple_in = din("w_ple_in", [256, 1024])
    gbc = din("gbc", [6, 128, 1024])
    bple_bc = din("bple_bc", [128, 1024])
    cst = din("cst", [128, NCST])
    cst16 = din("cst16", [128, NC16])
    out = nc.dram_tensor("out", [2048, 1024], F32, kind="ExternalOutput")
    dbg_t = {}
    for name, shape in dbg:
        dbg_t[name] = nc.dram_tensor("dbg_" + name, list(shape), F32, kind="ExternalOutput")

    tblC = nc.dram_tensor("tblC", [128, 4096], F32)
    tblS = nc.dram_tensor("tblS", [128, 4096], F32)
    yrT_d = nc.dram_tensor("yrT_d", [128, 16, 2048], BF16)
    mixT_d = nc.dram_tensor("mixT_d", [128, 8, 2048], BF16)
    h1_d = nc.dram_tensor("h1_d", [2048, 1024], F32)
    loc_d = [nc.dram_tensor(f"loc{h}", [128, 1024], F32) for h in range(4)]
    gath_d = [nc.dram_tensor(f"gath{h}", [1024, 1024], F32) for h in range(4)]

    w_in_v = w_in.ap().rearrange("(k p) c -> p k c", p=128)

    class Arena:
        def __init__(s):
            s.cur = ARENA_BASE
            s.stack = []
            s.n = 0

        def alloc(s, name, fshape, dt):
            nbytes = int(np.prod(fshape)) * mb.dt.size(dt)
            off = s.cur
            s.cur += (nbytes + 31) // 32 * 32
            assert s.cur <= ARENA_END, (name, s.cur)
            s.n += 1
            return nc.alloc_sbuf_tensor_at(f"{name}_{s.n}", [128] + list(fshape), dt, offset=off)

        def push(s):
            s.stack.append(s.cur)

        def pop(s):
            s.cur = s.stack.pop()

    A = Arena()
    PS = nc.alloc_psum_tensor("ps", [128, 4096], F32)
    PSB = PS.bitcast(BF16)

    def ps(b, n=512, off=0):
        return PS[:, b * 512 + off: b * 512 + off + n]

    def psb(b, n=1024, off=0):
        return PSB[:, b * 1024 + off: b * 1024 + off + n]

    def psk(b):
        return f"ps{b}"

    c32 = A.alloc("c32", [NCST], F32)
    c16 = A.alloc("c16", [NC16], BF16)
    epst = A.alloc("eps", [8], F32)
    stat = A.alloc("stat", [3, 16], F32)
    wsl = [A.alloc(f"wsl{i}", [8, 512], BF16) for i in range(3)]
    A.push()
    uTo = A.alloc("uTo", [8, 2048], BF16)
    yaT = A.alloc("yaT", [4, 2048], BF16)
    cos_r = A.alloc("cos_r", [16, 128], F32)
    sin_r = A.alloc("sin_r", [16, 128], F32)

    P.dma("sp", c32[:], cst.ap(), "c32", [], ["c32"])
    P.dma("pool", c16[:], cst16.ap(), "c16", [], ["c16"])
    P.add("dve", lambda e: e.memset(epst[:], EPS), [], ["eps"])

    invf = c32[:, O_INVF:O_INVF + 128]
    ident = c16[:, O_ID:O_ID + 128]
    permT = c16[:, O_PERM:O_PERM + 128]
    freq_d = c32[:, O_FREQD:O_FREQD + 1]
    ssign = c32[:, O_SSIGN:O_SSIGN + 1]

    stat_ctr = [0]

    def rms_rstd(src, n, rkeys, junk):
        i = stat_ctr[0] % 16
        stat_ctr[0] += 1
        k = f"st{i}"
        ss, sq, rs = stat[:, 0, i:i + 1], stat[:, 1, i:i + 1], stat[:, 2, i:i + 1]
        P.add("dve", lambda e: e.memset(ss, 0.0), [], [k])
        P.act(junk[:, 0:n], src, AF.Square, list(rkeys) + [k], ["junk", k], accum=ss)
        P.act(sq, ss, AF.Sqrt, [k, "eps"], [k], bias=epst[:, 0:1], scale=1.0 / n)
        P.add("dve", lambda e: e.reciprocal(rs, sq), [k], [k])
        return rs, k

    def sincos(ang, n, T, sin_out, cos_out, kin, kout, sin_scale=None):
        e = "dve"
        kt = "sctmp"
        P.ts(e, T["ki"][:, 0:n], ang, INV2PI, None, ALU.mult, None, [kin], [kt])
        P.cp(e, T["kf"][:, 0:n], T["ki"][:, 0:n], [kt], [kt])
        r_, rc, tf, kf = T["r"][:, 0:n], T["rc"][:, 0:n], T["tf"][:, 0:n], T["kf"][:, 0:n]
        P.stt(e, r_, kf, -C1, ang, ALU.mult, ALU.add, [kt, kin], [kt])
        P.stt(e, r_, kf, -C2, r_, ALU.mult, ALU.add, [kt], [kt])
        P.ts(e, tf, r_, PI, None, ALU.is_gt, None, [kt], [kt])
        P.stt(e, r_, tf, -TWOPI, r_, ALU.mult, ALU.add, [kt], [kt])
        P.ts(e, rc, r_, HALFPI, None, ALU.add, None, [kt], [kt])
        P.ts(e, tf, rc, PI, None, ALU.is_gt, None, [kt], [kt])
        P.stt(e, rc, tf, -TWOPI, rc, ALU.mult, ALU.add, [kt], [kt])
        P.ts(e, r_, r_, -PIC, PIC, ALU.max, ALU.min, [kt], [kt])
        P.ts(e, rc, rc, -PIC, PIC, ALU.max, ALU.min, [kt], [kt])
        if sin_scale is not None:
            P.act(sin_out, r_, AF.Sin, [kt, "c32"], [kout], scale=sin_scale)
        else:
            P.act(sin_out, r_, AF.Sin, [kt], [kout])
        P.act(cos_out, rc, AF.Sin, [kt], [kout])

    def dbg_dump(name, src_ap, rkeys, dst_ap=None):
        if name in dbg_t:
            d = dbg_t[name].ap() if dst_ap is None else dst_ap
            k = dbgs()
            P.dma("sp", d, src_ap, k, rkeys, [k])

    A.push()
    uTh = A.alloc("uTh", [8, 2048], BF16)
    tbs = [A.alloc(f"tbs{i}", [2, 512], F32) for i in range(2)]
    A.push()
    xsl = [A.alloc(f"xs{i}", [1024], F32) for i in range(3)]
    u16 = [A.alloc(f"u16{i}", [1024], BF16) for i in range(2)]
    junk = A.alloc("junk", [1024], BF16)
    g0 = A.alloc("g0", [1024], F32)
    T = {"ki": A.alloc("ki", [1024], I32), "kf": A.alloc("kf", [1024], F32), "r": A.alloc("r", [1024], F32),
         "rc": A.alloc("rc", [1024], F32), "tf": A.alloc("tf", [1024], F32)}
    angb = A.alloc("angb", [1024], F32)
    posi = A.alloc("posi", [1024], I32)
    csl = A.alloc("csl", [2, 1024], F32)
    ptm_i = A.alloc("ptm_i", [16], I32)
    ptm_f = A.alloc("ptm_f", [16], F32)

    P.dma("sp", g0[:], gbc.ap()[0], "g0", [], ["g0"])
    for t in range(32):
        s3, s2 = t % 3, t % 2
        xs, u = xsl[s3], u16[s2]
        P.dma("sp", xs[:], x_all.ap()[t * 128:(t + 1) * 128, :], f"xs{s3}", [], [f"xs{s3}"])
        rs, k = rms_rstd(xs[:], 1024, [f"xs{s3}"], junk)
        P.stt("dve", u[:], xs[:], rs, g0[:], ALU.mult, ALU.mult, [f"xs{s3}", k, "g0"], [f"u16{s2}"])
        for kk in range(8):
            P.tr(psb(s2)[:, kk * 128:(kk + 1) * 128], u[:, kk * 128:(kk + 1) * 128], ident,
                 [f"u16{s2}", "c16"], [psk(s2)])
        dst = (uTh if t < 16 else uTo)[:, :, (t % 16) * 128:(t % 16 + 1) * 128]
        P.cp("act" if t % 2 else "dve", dst, psb(s2).rearrange("p (k c) -> p k c", k=8), [psk(s2)], [f"uT{t}"])

    for c in range(4):
        P.dma("sp", posi[:], pos_bc.ap()[:, c * 1024:(c + 1) * 1024], "posi", [], ["posi"])
        P.cp("dve", angb[:], posi[:], ["posi"], ["angb"])
        P.ts("dve", angb[:], angb[:], freq_d, None, ALU.mult, None, ["angb", "c32"], ["angb"])
        sincos(angb[:], 1024, T, csl[:, 1, :], csl[:, 0, :], "angb", "csl", sin_scale=ssign)
        P.dma("sp", tblC.ap()[:, c * 1024:(c + 1) * 1024], csl[:, 0, :], "tblCw", ["csl"], ["tblC"])
        P.dma("sp", tblS.ap()[:, c * 1024:(c + 1) * 1024], csl[:, 1, :], "tblSw", ["csl"], ["tblS"])
    P.dma("sp", ptm_i[:], pos_tm.ap(), "ptm", [], ["ptm_i"])
    P.cp("dve", ptm_f[:], ptm_i[:], ["ptm_i"], ["ptm_f"])
    for c in range(2):
        for t in range(8):
            P.ts("dve", angb[:, t * 128:(t + 1) * 128], invf, ptm_f[:, c * 8 + t:c * 8 + t + 1], None, ALU.mult, None,
                 ["c32", "ptm_f"], ["angb"])
        sincos(angb[:], 1024, T, sin_r[:, c * 8:(c + 1) * 8, :].rearrange("p a b -> p (a b)"),
               cos_r[:, c * 8:(c + 1) * 8, :].rearrange("p a b -> p (a b)"), "angb", "csr")
    if "uT" in dbg_t:
        dd = dbg_t["uT"].ap().rearrange("(k p) t -> p k t", p=128)
        k = dbgs()
        P.dma("pool", dd[:, :, 0:2048], uTh[:], k, [f"uT{t}" for t in range(16)], [k])
        k = dbgs()
        P.dma("pool", dd[:, :, 2048:4096], uTo[:], k, [f"uT{t}" for t in range(16, 32)], [k])
    dbg_dump("cosr", cos_r[:].rearrange("p a b -> p (a b)"), ["csr"])
    dbg_dump("sinr", sin_r[:].rearrange("p a b -> p (a b)"), ["csr"])
    if stop == "A":
        return finalize()
    P.barrier()
    A.pop()

    qT = A.alloc("qT", [2048], BF16)
    kTp = A.alloc("kTp", [2, 4096], BF16)
    vT = A.alloc("vT", [2048], BF16)
    vpad = A.alloc("vpad", [32, 2, 128], BF16)
    acc = A.alloc("acc", [2, 2048], F32)
    x16 = [A.alloc(f"x16{i}", [512], BF16) for i in range(2)]
    t1 = [A.alloc(f"t1{i}", [512], F32) for i in range(2)]
    t2 = [A.alloc(f"t2{i}", [512], F32) for i in range(2)]
    pexp = [A.alloc(f"pexp{i}", [512], BF16) for i in range(2)]
    pmk = [A.alloc(f"pmk{i}", [512], BF16) for i in range(2)]
    P.add("pool", lambda e: e.memset(vpad[:], 0.0), [], ["vpad"])
    P.add("pool", lambda e: e.memset(kTp[:], 0.0), [], ["kT"])
    mk = c16[:, O_MK:O_MK + 256]
    mk0 = c16[:, O_MK0:O_MK0 + 256]

    ctr = {"pb": 0, "rope": 0, "sc": 0, "pu": 0, "ws": 0}

    def utok(k, tok0, n):
        if tok0 < 2048:
            assert tok0 + n <= 2048
            return uTh[:, k, tok0:tok0 + n], [f"uT{t}" for t in range(tok0 // 128, (tok0 + n + 127) // 128)]
        o = tok0 - 2048
        return uTo[:, k, o:o + n], [f"uT{16 + t}" for t in range(o // 128, (o + n + 127) // 128)]

    def proj_fm(wcols, wkey, tok0, n):
        bank = ctr["pb"] % 2
        ctr["pb"] += 1
        for k in range(8):
            rhs, rk = utok(k, tok0, n)
            P.mm(ps(bank, n), wcols[:, k, :], rhs, k == 0, k == 7, [wkey] + rk, [psk(bank)])
        return bank

    def rope_fm(bank, tok0, n, dst, dkey):
        i = ctr["rope"] % 2
        ctr["rope"] += 1
        tb = tbs[i]
        P.dma("sp", tb[:, 0, 0:n], tblC.ap()[:, tok0:tok0 + n], f"tbs{i}", ["tblC"], [f"tbs{i}"])
        P.dma("sp", tb[:, 1, 0:n], tblS.ap()[:, tok0:tok0 + n], f"tbs{i}", ["tblS"], [f"tbs{i}"])
        if RSTOP == "R1":
            return
        P.cp("act", x16[i][:, 0:n], ps(bank, n), [psk(bank)], [f"x16{i}"])
        if RSTOP == "R2":
            return
        P.mm(ps(2, n), permT, x16[i][:, 0:n], True, True, ["c16", f"x16{i}"], [psk(2)])
        if RSTOP == "R3":
            return
        if RSTOP == "E1":
            P.tt("dve", t1[i][:, 0:n], t2[i][:, 0:n], tb[:, 0, 0:n], ALU.mult, [psk(bank), f"tbs{i}"], [f"t1{i}"])
            return
        if RSTOP == "E4":
            P.cp("dve", t1[i][:, 0:n], ps(bank, n), [psk(bank), f"tbs{i}"], [f"t1{i}"])
            return
        if RSTOP == "E5":
            P.cp("dve", t1[i][:, 0:n], ps(bank, n), [psk(bank)], [f"t1{i}"])
            return
        P.tt("dve", t1[i][:, 0:n], ps(bank, n), tb[:, 0, 0:n], ALU.mult, [psk(bank), f"tbs{i}"], [f"t1{i}"])
        if RSTOP == "R4":
            return
        P.tt("dve", t2[i][:, 0:n], ps(2, n), tb[:, 1, 0:n], ALU.mult, [psk(2), f"tbs{i}"], [f"t2{i}"])
        if RSTOP == "R5":
            return
        if dst is None:
            P.tt(ROPE_ENG, kTp[0:64, 0, tok0:tok0 + n], t1[i][0:64, 0:n], t2[i][0:64, 0:n], ALU.add, [f"t1{i}", f"t2{i}"], [dkey])
            P.tt(ROPE_ENG, kTp[64:128, 1, tok0:tok0 + n], t1[i][64:128, 0:n], t2[i][64:128, 0:n], ALU.add, [f"t1{i}", f"t2{i}"], [dkey])
        else:
            P.tt(ROPE_ENG, dst, t1[i][:, 0:n], t2[i][:, 0:n], ALU.add, [f"t1{i}", f"t2{i}"], [dkey])

    for sp_ in range(4):
        for g in range(3):
            Dg = (1, 4, 16)[g]
            nbl = 16 // Dg
            halo = 128 * Dg
            start = 2048 - halo
            slot = ctr["ws"] % 2
            ctr["ws"] += 1
            wd = wsl[slot]
            wkey = f"wsl{slot}"
            for part, base in enumerate((6144, 7680, 9216)):
                col0 = base + (g * 8 + 2 * sp_) * 64
                P.dma("pool", wd[:, :, part * 128:(part + 1) * 128], w_in_v[:, :, col0:col0 + 128], wkey, [], [wkey])
            if stop == "B0":
                return finalize()
            for b in range(4):
                bank = proj_fm(wd[:, :, 0:128], wkey, 2048 + b * 512, 512)
                if stop == "B0b":
                    return finalize()
                rope_fm(bank, 2048 + b * 512, 512, qT[:, b * 512:(b + 1) * 512], "qT")
            if stop == "B1":
                return finalize()
            blocks = [(start, min(512, halo))] if halo <= 512 else [(i * 512, 512) for i in range(4)]
            blocks += [(2048 + i * 512, 512) for i in range(4)]
            for (tok0, n) in blocks:
                bank = proj_fm(wd[:, :, 128:256], wkey, tok0, n)
                rope_fm(bank, tok0, n, None, "kT")
            if stop == "B2":
                return finalize()
            for half in range(2):
                hb = [bk for bk in blocks if (bk[0] < 2048) == (half == 0)]
                for (tok0, n) in hb:
                    bank = proj_fm(wd[:, :, 256:384], wkey, tok0, n)
                    o = tok0 - 2048 * half
                    P.cp("act", vT[:, o:o + n], ps(bank, n), [psk(bank)], ["vT"])
                if half == 0:
                    tiles = [(c, 2048 - 128 * Dg + c) for c in range(Dg)]
                    idx0 = 0
                else:
                    tiles = [(j * Dg + c, (j - 1) * 128 * Dg + c) for j in range(1, nbl + 1) for c in range(Dg)]
                    idx0 = Dg
                for q0 in range(0, len(tiles), 8):
                    grp = tiles[q0:q0 + 8]
                    for q, (ti, s0) in enumerate(grp):
                        P.tr(psb(7)[:, q * 128:(q + 1) * 128], vT[:, s0:s0 + 127 * Dg + 1:Dg], ident,
                             ["vT", "c16"], [psk(7)])
                    i0 = grp[0][0]
                    ng = len(grp)
                    src = psb(7).rearrange("p (a b) -> p a b", b=128)
                    P.cp("act", vpad[:, i0:i0 + ng, 0, 0:64], src[:, 0:ng, 0:64], [psk(7)], ["vpad"])
                    P.cp("dve", vpad[:, i0:i0 + ng, 1, 64:128], src[:, 0:ng, 64:128], [psk(7)], ["vpad"])
            if stop == "B3":
                return finalize()
            for j in range(1, nbl + 1):
                for c in range(Dg):
                    q0 = (j - 1) * 128 * Dg + c
                    qsl = slice(q0, q0 + 127 * Dg + 1, Dg)
                    kprev0 = (2048 - 128 * Dg + c) if j == 1 else (2048 + (j - 2) * 128 * Dg + c)
                    kcur0 = 2048 + q0
                    ksl = [slice(kprev0, kprev0 + 127 * Dg + 1, Dg), slice(kcur0, kcur0 + 127 * Dg + 1, Dg)]
                    tidx = [(j - 1) * Dg + c, j * Dg + c]
                    sb = 3 + ctr["sc"] % 2
                    si = ctr["sc"] % 2
                    ctr["sc"] += 1
                    for kb in range(2):
                        for s in range(2):
                            P.mm(ps(sb, 128, (kb * 2 + s) * 128), kTp[:, s, ksl[kb]],
                                 qT[:, qsl], True, True, ["kT", "qT"], [psk(sb)])
                    if RSTOP == "B4a":
                        return finalize()
                    P.act(pexp[si][:], ps(sb), AF.Exp, [psk(sb)], [f"pexp{si}"], scale=0.125)
                    if RSTOP == "B4b":
                        return finalize()
                    msk = mk0 if j == 1 else mk
                    m4 = bcast(msk.rearrange("p (a b) -> p a b", a=2), 2, 2)
                    P.tt("dve", pmk[si][:].rearrange("p (a s b) -> p a s b", a=2, s=2),
                         pexp[si][:].rearrange("p (a s b) -> p a s b", a=2, s=2), m4, ALU.mult,
                         [f"pexp{si}", "c16"], [f"pmk{si}"])
                    if RSTOP == "B4c":
                        return finalize()
                    ub = 5 + ctr["pu"] % 2
                    ctr["pu"] += 1
                    n_ = 0
                    for kb in range(2):
                        for s in range(2):
                            P.mm(ps(ub, 128, 0), vpad[:, tidx[kb], s, :], pmk[si][:, (kb * 2 + s) * 128:(kb * 2 + s + 1) * 128],
                                 n_ == 0, n_ == 3, ["vpad", f"pmk{si}"], [psk(ub)])
                            n_ += 1
                    n_ = 0
                    for kb in range(2):
                        for s in range(2):
                            P.mm(ps(ub, 128, 128), c16[:, O_ONES + s * 128:O_ONES + (s + 1) * 128],
                                 pmk[si][:, (kb * 2 + s) * 128:(kb * 2 + s + 1) * 128],
                                 n_ == 0, n_ == 3, ["c16", f"pmk{si}"], [psk(ub)])
                            n_ += 1
                    if RSTOP == "B4d":
                        return finalize()
                    dsta = acc[:, :, qsl]
                    srca = ps(ub, 256).rearrange("p (a b) -> p a b", a=2)
                    if g == 0:
                        P.cp("act", dsta, srca, [psk(ub)], ["acc"])
                    else:
                        P.tt("dve", dsta, srca, dsta, ALU.add, [psk(ub), "acc"], ["acc"])
            if stop == "B4":
                return finalize()
        P.add("dve", lambda e: e.reciprocal(acc[:, 1, :], acc[:, 1, :]), ["acc"], ["acc"])
        P.tt("dve", yaT[:, sp_, :], acc[:, 0, :], acc[:, 1, :], ALU.mult, ["acc"], [f"yaT{sp_}"])
    if "yaT" in dbg_t:
        for sp_ in range(4):
            P.cp("dve", acc[:, 0, :], yaT[:, sp_, :], [f"yaT{sp_}", "acc"], ["acc"])
            k = dbgs()
            P.dma("sp", dbg_t["yaT"].ap()[sp_ * 128:(sp_ + 1) * 128, :], acc[:, 0, :], k, ["acc"], [k])
    P.barrier()
    A.pop()

    if stop == "B":
        return finalize()

    A.push()
    qTr = A.alloc("qTr", [2, 2048], BF16)
    kTr = A.alloc("kTr", [2, 2048], BF16)
    kd = A.alloc("kd", [16, 256], BF16)
    v16 = A.alloc("v16", [16, 512], BF16)
    Gb = A.alloc("Gb", [4, 1024], F32)
    Rf = A.alloc("Rf", [2, 512], F32)
    R16 = A.alloc("R16", [2, 512], BF16)
    ra = [A.alloc(f"ra{i}", [4, 256], F32) for i in range(2)]
    qk16 = [A.alloc(f"qk16{i}", [512], BF16) for i in range(2)]
    pmr = [A.alloc(f"pmr{i}", [128], BF16) for i in range(2)]
    yo = [A.alloc(f"yo{i}", [512], F32) for i in range(2)]
    yy = [A.alloc(f"yy{i}", [512], F32) for i in range(2)]
    sg = [A.alloc(f"sg{i}", [512], F32) for i in range(2)]
    yr16 = [A.alloc(f"yr16{i}", [512], BF16) for i in range(2)]
    yst = [A.alloc(f"yst{i}", [4, 128], BF16) for i in range(2)]
    junkc = A.alloc("junkc", [1024], BF16)
    Rflat = Rf[:].rearrange("p a b -> p (a b)")
    PS67 = PS[:, 6 * 512:8 * 512]
    lgam = [math.log1p(-(2.0 ** (-5.0 - h))) for h in range(4)]
    for h in range(4):
        cdh = float(np.float32(math.exp(128.0 * lgam[h])))
        wq, wv, wg = wsl[0], wsl[1], wsl[2]
        P.dma("pool", wq[:, :, 0:256], w_in_v[:, :, 256 * h:256 * h + 256], "wsl0", [], ["wsl0"])
        P.dma("pool", wq[:, :, 256:512], w_in_v[:, :, 1024 + 256 * h:1024 + 256 * h + 256], "wsl0", [], ["wsl0"])
        P.dma("pool", wv[:], w_in_v[:, :, 2048 + 512 * h:2048 + 512 * h + 512], "wsl1", [], ["wsl1"])
        P.dma("pool", wg[:], w_in_v[:, :, 4096 + 512 * h:4096 + 512 * h + 512], "wsl2", [], ["wsl2"])
        ksc = c32[:, O_KSC + h:O_KSC + h + 1]
        qdc = c32[:, O_QDEC + h:O_QDEC + h + 1]
        DTh = c32[:, O_DT + h * 128:O_DT + (h + 1) * 128]
        for t in range(16):
            bank = t % 2
            i = t % 2
            tsl = slice(t * 128, (t + 1) * 128)
            for k in range(8):
                P.mm(ps(bank), uTo[:, k, tsl], wq[:, k, :], k == 0, k == 7, [f"uT{16 + t}", "wsl0"], [psk(bank)])
            ps4 = ps(bank).rearrange("p (a b c) -> p a b c", a=2, b=2)
            x1, x2 = ps4[:, :, 0, :], ps4[:, :, 1, :]
            cb = bcast(cos_r[:, t, :], 1, 2)
            sb_ = bcast(sin_r[:, t, :], 1, 2)
            r4 = ra[i][:].rearrange("p a (b c) -> p a b c", b=2)
            P.tt("dve", r4[:, 0], x1, cb, ALU.mult, [psk(bank), "csr"], [f"ra{i}"])
            P.tt("dve", r4[:, 1], x2, sb_, ALU.mult, [psk(bank), "csr"], [f"ra{i}"])
            P.tt("dve", r4[:, 2], x2, cb, ALU.mult, [psk(bank), "csr"], [f"ra{i}"])
            P.tt("dve", r4[:, 3], x1, sb_, ALU.mult, [psk(bank), "csr"], [f"ra{i}"])
            q4 = qk16[i][:].rearrange("p (a b c) -> p a b c", a=2, b=2)
            P.tt("pool", q4[:, :, 0, :], r4[:, 0], r4[:, 1], ALU.subtract, [f"ra{i}"], [f"qk16{i}"])
            P.tt("pool", q4[:, :, 1, :], r4[:, 2], r4[:, 3], ALU.add, [f"ra{i}"], [f"qk16{i}"])
            P.act(kd[:, t, :], qk16[i][:, 256:512], AF.Copy, [f"qk16{i}", "c32"], ["kd"], scale=ksc)
            for j in range(4):
                P.tr(psb(2)[:, j * 128:(j + 1) * 128], qk16[i][:, j * 128:(j + 1) * 128], ident,
                     [f"qk16{i}", "c16"], [psk(2)])
            P.cp("act", qTr[:, :, tsl], psb(2)[:, 0:256].rearrange("p (a b) -> p a b", a=2), [psk(2)], ["qTr"])
            P.cp("dve", kTr[:, :, tsl], psb(2)[:, 256:512].rearrange("p (a b) -> p a b", a=2), [psk(2)], ["kTr"])
            vb = 3 + t % 2
            for k in range(8):
                P.mm(ps(vb), uTo[:, k, tsl], wv[:, k, :], k == 0, k == 7, [f"uT{16 + t}", "wsl1"], [psk(vb)])
            P.cp("act", v16[:, t, :], ps(vb), [psk(vb)], ["v16"])
        P.add("dve", lambda e: e.memset(Rf[:], 0.0), [], ["Rf"])
        for n in range(16):
            P.mm(ps(6), kd[:, n, 0:128], v16[:, n, :], True, True, ["kd", "v16"], [psk(6)])
            P.mm(ps(7), kd[:, n, 128:256], v16[:, n, :], True, True, ["kd", "v16"], [psk(7)])
            P.stt("dve", Rflat, Rflat, cdh, PS67, ALU.mult, ALU.add, ["Rf", psk(6), psk(7)], ["Rf"])
        P.dma("pool", loc_d[h].ap(), Rflat, f"agw{h}", ["Rf"], [f"loc{h}"])
        if fake_ag:
            for r_ in range(8):
                P.dma("pool", gath_d[h].ap()[r_ * 128:(r_ + 1) * 128, :], loc_d[h].ap(), f"cc{h}", [f"loc{h}"], [f"gath{h}"])
        else:
            P.add("pool", lambda e, h=h: e.collective_compute(
                "AllGather", ALU.bypass, replica_groups=[list(range(NCORES))],
                ins=[loc_d[h].ap()], outs=[gath_d[h].ap()]), [f"loc{h}"], [f"gath{h}"], stream=f"cc{h}", inc=None)
        gv = gath_d[h].ap().rearrange("(r p) c -> p r c", p=128)
        for half in range(2):
            P.dma("sp", Gb[:], gv[:, half * 4:(half + 1) * 4, :], "Gb", [f"gath{h}"], ["Gb"])
            for rr in range(4):
                r_ = half * 4 + rr
                cf = c32[:, O_COEF + r_ * 4 + h:O_COEF + r_ * 4 + h + 1]
                if r_ == 0:
                    P.ts("dve", Rflat, Gb[:, rr, :], cf, None, ALU.mult, None, ["Gb", "c32", "Rf"], ["Rf"])
                else:
                    P.stt("dve", Rflat, Gb[:, rr, :], cf, Rflat, ALU.mult, ALU.add, ["Gb", "c32", "Rf"], ["Rf"])
        P.cp("act", R16[:], Rf[:], ["Rf"], ["R16"])
        for n in range(16):
            i = n % 2
            nt = slice(n * 128, (n + 1) * 128)
            gbk = n % 2
            for k in range(8):
                P.mm(ps(gbk), uTo[:, k, nt], wg[:, k, :], k == 0, k == 7, [f"uT{16 + n}", "wsl2"], [psk(gbk)])
            P.act(sg[i][:], ps(gbk), AF.Silu, [psk(gbk)], [f"sg{i}"])
            P.mm(ps(3, 128), kTr[:, 0, nt], qTr[:, 0, nt], True, False, ["kTr", "qTr"], [psk(3)])
            P.mm(ps(3, 128), kTr[:, 1, nt], qTr[:, 1, nt], False, True, ["kTr", "qTr"], [psk(3)])
            P.tt("dve", pmr[i][:], ps(3, 128), DTh, ALU.mult, [psk(3), "c32"], [f"pmr{i}"])
            P.mm(ps(4), pmr[i][:], v16[:, n, :], True, True, [f"pmr{i}", "v16"], [psk(4)])
            P.mm(ps(5), qTr[:, 0, nt], R16[:, 0, :], True, False, ["qTr", "R16"], [psk(5)])
            P.mm(ps(5), qTr[:, 1, nt], R16[:, 1, :], False, True, ["qTr", "R16"], [psk(5)])
            P.cp("act", yo[i][:], ps(4), [psk(4)], [f"yo{i}"])
            P.stt("dve", yy[i][:], ps(5), qdc, yo[i][:], ALU.mult, ALU.add, [psk(5), f"yo{i}", "c32"], [f"yy{i}"])
            if n < 15:
                P.mm(ps(6), kd[:, n, 0:128], v16[:, n, :], True, True, ["kd", "v16"], [psk(6)])
                P.mm(ps(7), kd[:, n, 128:256], v16[:, n, :], True, True, ["kd", "v16"], [psk(7)])
                P.stt("dve", Rflat, Rflat, cdh, PS67, ALU.mult, ALU.add, ["Rf", psk(6), psk(7)], ["Rf"])
                P.cp("act", R16[:], Rf[:], ["Rf"], ["R16"])
            if "yraw" in dbg_t:
                kq = dbgs()
                P.dma("sp", dbg_t["yraw"].ap()[n * 128:(n + 1) * 128, h * 512:(h + 1) * 512], yy[i][:], kq, [f"yy{i}"], [kq])
            rs, k_ = rms_rstd(yy[i][:], 512, [f"yy{i}"], junkc)
            P.stt("dve", yr16[i][:], yy[i][:], rs, sg[i][:], ALU.mult, ALU.mult, [f"yy{i}", k_, f"sg{i}"], [f"yr16{i}"])
            for j in range(4):
                P.tr(psb(2)[:, j * 128:(j + 1) * 128], yr16[i][:, j * 128:(j + 1) * 128], ident,
                     [f"yr16{i}", "c16"], [psk(2)])
            P.cp("act", yst[i][:], psb(2)[:, 0:512].rearrange("p (a b) -> p a b", a=4), [psk(2)], [f"yst{i}"])
            P.dma("sp", yrT_d.ap()[:, h * 4:(h + 1) * 4, nt], yst[i][:], f"yst{i}", [f"yst{i}"], [f"yrT_d{h}_{n}"])
    P.barrier()
    A.pop()
    if "yrT" in dbg_t:
        A.push()
        bnc = A.alloc("bnc", [16, 2048], BF16)
        P.dma("sp", bnc[:], yrT_d.ap(), "bnc", [], ["bnc"])
        kq = dbgs()
        P.dma("pool", dbg_t["yrT"].ap().rearrange("(k p) t -> p k t", p=128), bnc[:], kq, ["bnc"], [kq])
        P.barrier()
        A.pop()
    if stop == "C":
        return finalize()

    A.push()
    yrT = A.alloc("yrT", [16, 2048], BF16)
    wm = [{"wr": A.alloc(f"wr{i}", [16, 128], BF16), "wdo": A.alloc(f"wdo{i}", [4, 128], BF16),
           "wgr": A.alloc(f"wgr{i}", [8, 128], BF16), "wga": A.alloc(f"wga{i}", [8, 128], BF16)} for i in range(2)]
    sr = [A.alloc(f"sr{i}", [512], F32) for i in range(2)]
    sa = [A.alloc(f"sa{i}", [512], F32) for i in range(2)]
    m1 = [A.alloc(f"m1{i}", [512], F32) for i in range(2)]
    m2 = [A.alloc(f"m2{i}", [512], F32) for i in range(2)]
    mx = [A.alloc(f"mx{i}", [512], BF16) for i in range(2)]
    for q in range(4):
        P.dma("sp", yrT[:, q * 4:(q + 1) * 4, :], yrT_d.ap()[:, q * 4:(q + 1) * 4, :], "yrTl", [], ["yrT"])
    wro_v = w_ret_out.ap().rearrange("(k p) c -> p k c", p=128)
    wdo_v = w_dil_out.ap().rearrange("(k p) c -> p k c", p=128)

    def d1_load(m):
        sl = m % 2
        w = wm[sl]
        st_ = f"wm{sl}"
        cs_ = slice(m * 128, (m + 1) * 128)
        P.dma("pool", w["wr"][:], wro_v[:, :, cs_], st_, [], [st_])
        P.dma("pool", w["wdo"][:], wdo_v[:, :, cs_], st_, [], [st_])
        P.dma("pool", w["wgr"][:], w_in_v[:, :, 10752 + m * 128:10752 + (m + 1) * 128], st_, [], [st_])
        P.dma("pool", w["wga"][:], w_in_v[:, :, 11776 + m * 128:11776 + (m + 1) * 128], st_, [], [st_])

    d1_load(0)
    cnt_d1 = 0
    for m in range(8):
        if m + 1 < 8:
            d1_load(m + 1)
        sl = m % 2
        w = wm[sl]
        wk = f"wm{sl}"
        for b in range(4):
            base = 4 * (cnt_d1 % 2)
            i = cnt_d1 % 2
            cnt_d1 += 1
            bs = slice(b * 512, (b + 1) * 512)
            ukeys = [f"uT{16 + b * 4 + q}" for q in range(4)]
            for k in range(16):
                P.mm(ps(base), w["wr"][:, k, :], yrT[:, k, bs], k == 0, k == 15, [wk, "yrT"], [psk(base)])
            for k in range(4):
                P.mm(ps(base + 1), w["wdo"][:, k, :], yaT[:, k, bs], k == 0, k == 3, [wk] + [f"yaT{q}" for q in range(4)], [psk(base + 1)])
            for k in range(8):
                P.mm(ps(base + 2), w["wgr"][:, k, :], uTo[:, k, bs], k == 0, k == 7, [wk] + ukeys, [psk(base + 2)])
            for k in range(8):
                P.mm(ps(base + 3), w["wga"][:, k, :], uTo[:, k, bs], k == 0, k == 7, [wk] + ukeys, [psk(base + 3)])
            P.act(sr[i][:], ps(base + 2), AF.Sigmoid, [psk(base + 2), "c32"], [f"sr{i}"], bias=c32[:, O_BGT + m:O_BGT + m + 1])
            P.act(sa[i][:], ps(base + 3), AF.Sigmoid, [psk(base + 3), "c32"], [f"sa{i}"], bias=c32[:, O_BGT + 8 + m:O_BGT + 8 + m + 1])
            P.tt("dve", m1[i][:], ps(base), sr[i][:], ALU.mult, [psk(base), f"sr{i}"], [f"m1{i}"])
            P.tt("dve", m2[i][:], ps(base + 1), sa[i][:], ALU.mult, [psk(base + 1), f"sa{i}"], [f"m2{i}"])
            P.tt("pool", mx[i][:], m1[i][:], m2[i][:], ALU.add, [f"m1{i}", f"m2{i}"], [f"mx{i}"])
            P.dma("sp", mixT_d.ap()[:, m, bs], mx[i][:], f"mx{i}", [f"mx{i}"], [f"mixT_d{m}_{b}"])
    P.barrier()
    A.pop()
    A.pop()
    if "mixT" in dbg_t:
        A.push()
        bnc = A.alloc("bnc2", [8, 2048], BF16)
        P.dma("sp", bnc[:], mixT_d.ap(), "bnc2", [], ["bnc2"])
        kq = dbgs()
        P.dma("pool", dbg_t["mixT"].ap().rearrange("(k p) t -> p k t", p=128), bnc[:], kq, ["bnc2"], [kq])
        P.barrier()
        A.pop()
    if stop == "D1":
        return finalize()

    v2T = A.alloc("v2T", [8, 2048], BF16)
    facc = A.alloc("facc", [16, 1024], F32)
    A.push()
    wo = A.alloc("wo", [8, 1024], BF16)
    gpo = A.alloc("gpo", [1024], F32)
    gpm = A.alloc("gpm", [1024], F32)
    mixb = [A.alloc(f"mixb{i}", [8, 512], BF16) for i in range(2)]
    xt = [A.alloc(f"xt{i}", [1024], F32) for i in range(2)]
    tmpd = [A.alloc(f"tmpd{i}", [1024], F32) for i in range(2)]
    h1t = [A.alloc(f"h1t{i}", [1024], F32) for i in range(2)]
    v2t = [A.alloc(f"v2t{i}", [1024], BF16) for i in range(2)]
    junkd = A.alloc("junkd", [1024], BF16)
    P.dma("pool", wo[:], w_o.ap().rearrange("(k p) c -> p k c", p=128), "wo", [], ["wo"])
    P.dma("sp", gpo[:], gbc.ap()[1], "gpo", [], ["gpo"])
    P.dma("sp", gpm[:], gbc.ap()[2], "gpm", [], ["gpm"])
    for b in range(4):
        bi = b % 2
        P.dma("sp", mixb[bi][:], mixT_d.ap()[:, :, b * 512:(b + 1) * 512], f"mixb{bi}", [], [f"mixb{bi}"])
        for tt in range(4):
            t = b * 4 + tt
            i = t % 2
            bp = (t % 2) * 2
            PSO = PS[:, bp * 512:bp * 512 + 1024]
            for c2 in range(2):
                for k in range(8):
                    P.mm(ps(bp + c2), mixb[bi][:, k, tt * 128:(tt + 1) * 128], wo[:, k, c2 * 512:(c2 + 1) * 512],
                         k == 0, k == 7, [f"mixb{bi}", "wo"], [psk(bp + c2)])
            rs, k_ = rms_rstd(PSO, 1024, [psk(bp), psk(bp + 1)], junkd)
            P.dma("sp", xt[i][:], x_all.ap()[2048 + t * 128:2048 + (t + 1) * 128, :], f"xt{i}", [], [f"xt{i}"])
            P.stt("dve", tmpd[i][:], PSO, rs, gpo[:], ALU.mult, ALU.mult, [psk(bp), psk(bp + 1), k_, "gpo"], [f"tmpd{i}"])
            P.tt("pool", h1t[i][:], tmpd[i][:], xt[i][:], ALU.add, [f"tmpd{i}", f"xt{i}"], [f"h1t{i}"])
            P.dma("sp", h1_d.ap()[t * 128:(t + 1) * 128, :], h1t[i][:], f"h1w{i}", [f"h1t{i}"], [f"h1_d{t}"])
            rs2, k2 = rms_rstd(h1t[i][:], 1024, [f"h1t{i}"], junkd)
            P.stt("dve", v2t[i][:], h1t[i][:], rs2, gpm[:], ALU.mult, ALU.mult, [f"h1t{i}", k2, "gpm"], [f"v2t{i}"])
            for kk in range(8):
                P.tr(psb(4)[:, kk * 128:(kk + 1) * 128], v2t[i][:, kk * 128:(kk + 1) * 128], ident, [f"v2t{i}", "c16"], [psk(4)])
            P.cp("act", v2T[:, :, t * 128:(t + 1) * 128], psb(4).rearrange("p (k c) -> p k c", k=8), [psk(4)], [f"v2T{t}"])
    P.barrier()
    A.pop()
    if "h1" in dbg_t:
        A.push()
        bnc = A.alloc("bnc3", [1024], F32)
        for t in range(16):
            P.dma("sp", bnc[:], h1_d.ap()[t * 128:(t + 1) * 128, :], "bnc3", [], ["bnc3"])
            kq = dbgs()
            P.dma("sp", dbg_t["h1"].ap()[t * 128:(t + 1) * 128, :], bnc[:], kq, ["bnc3"], [kq])
        P.barrier()
        A.pop()
    if stop == "D2":
        return finalize()

    A.push()
    wu = [A.alloc(f"wu{i}", [8, 512], BF16) for i in range(2)]
    wdn = [A.alloc(f"wdn{i}", [4, 1024], BF16) for i in range(2)]
    hid = [A.alloc(f"hid{i}", [4, 512], BF16) for i in range(2)]
    rl = [A.alloc(f"rl{i}", [512], F32) for i in range(2)]
    gq = A.alloc("gq", [1024], F32)
    gpp = A.alloc("gpp", [1024], F32)
    h1r = [A.alloc(f"h1r{i}", [1024], F32) for i in range(2)]
    tmpe = [A.alloc(f"tmpe{i}", [1024], F32) for i in range(2)]
    v3t = [A.alloc(f"v3t{i}", [1024], BF16) for i in range(2)]
    junke = A.alloc("junke", [1024], BF16)
    wup_v = w_up.ap().rearrange("(k p) c -> p k c", p=128)
    P.dma("sp", gq[:], gbc.ap()[3], "gq", [], ["gq"])
    P.dma("sp", gpp[:], gbc.ap()[4], "gpp", [], ["gpp"])

    def e_load(j):
        s_ = j % 2
        P.dma("pool", wu[s_][:], wup_v[:, :, j * 512:(j + 1) * 512], f"wu{s_}", [], [f"wu{s_}"])
        P.dma("pool", wdn[s_][:], w_down.ap()[j * 512:(j + 1) * 512, :].rearrange("(c p) n -> p c n", p=128),
              f"wdn{s_}", [], [f"wdn{s_}"])

    e_load(0)
    cnt_e = 0
    for j in range(8):
        if j + 1 < 8:
            e_load(j + 1)
        s_ = j % 2
        for b in range(4):
            hi = cnt_e % 2
            cnt_e += 1
            bs = slice(b * 512, (b + 1) * 512)
            vkeys = [f"v2T{b * 4 + q}" for q in range(4)]
            for c in range(4):
                for k in range(8):
                    P.mm(ps(c), wu[s_][:, k, c * 128:(c + 1) * 128], v2T[:, k, bs], k == 0, k == 7, [f"wu{s_}"] + vkeys, [psk(c)])
                ri = c % 2
                P.act(rl[ri][:], ps(c), AF.Relu, [psk(c)], [f"rl{ri}"])
                P.tt("pool", hid[hi][:, c, :], rl[ri][:], rl[ri][:], ALU.mult, [f"rl{ri}"], [f"hid{hi}"])
            for tt in range(4):
                t = b * 4 + tt
                bp = 4 + (tt % 2) * 2
                PSO = PS[:, bp * 512:bp * 512 + 1024]
                for c2 in range(2):
                    for c in range(4):
                        P.mm(ps(bp + c2), hid[hi][:, c, tt * 128:(tt + 1) * 128], wdn[s_][:, c, c2 * 512:(c2 + 1) * 512],
                             c == 0, c == 3, [f"hid{hi}", f"wdn{s_}"], [psk(bp + c2)])
                if j == 0:
                    P.cp("act", facc[:, t, :], PSO, [psk(bp), psk(bp + 1)], [f"facc{t}"])
                else:
                    P.tt("dve", facc[:, t, :], PSO, facc[:, t, :], ALU.add, [psk(bp), psk(bp + 1), f"facc{t}"], [f"facc{t}"])
    for t in range(16):
        i = t % 2
        rs, k_ = rms_rstd(facc[:, t, :], 1024, [f"facc{t}"], junke)
        P.dma("sp", h1r[i][:], h1_d.ap()[t * 128:(t + 1) * 128, :], f"h1r{i}", [], [f"h1r{i}"])
        P.stt("dve", tmpe[i][:], facc[:, t, :], rs, gq[:], ALU.mult, ALU.mult, [f"facc{t}", k_, "gq"], [f"tmpe{i}"])
        P.tt("pool", facc[:, t, :], tmpe[i][:], h1r[i][:], ALU.add, [f"tmpe{i}", f"h1r{i}"], [f"facc{t}"])
        rs2, k2 = rms_rstd(facc[:, t, :], 1024, [f"facc{t}"], junke)
        P.stt("dve", v3t[i][:], facc[:, t, :], rs2, gpp[:], ALU.mult, ALU.mult, [f"facc{t}", k2, "gpp"], [f"v3t{i}"])
        for kk in range(8):
            P.tr(psb(0)[:, kk * 128:(kk + 1) * 128], v3t[i][:, kk * 128:(kk + 1) * 128], ident, [f"v3t{i}", "c16"], [psk(0)])
        P.cp("act", v2T[:, :, t * 128:(t + 1) * 128], psb(0).rearrange("p (k c) -> p k c", k=8), [psk(0)], [f"v2T{t}"])
        if "h2" in dbg_t:
            kq = dbgs()
            P.dma("sp", dbg_t["h2"].ap()[t * 128:(t + 1) * 128, :], facc[:, t, :], kq, [f"facc{t}"], [kq])
    P.barrier()
    A.pop()
    if stop == "E":
        return finalize()

    A.push()
    wpg = A.alloc("wpg", [8, 1024], BF16)
    wpi = A.alloc("wpi", [2, 1024], BF16)
    bple = A.alloc("bple", [1024], F32)
    gl = A.alloc("gl", [1024], F32)
    ptl = [A.alloc(f"ptl{i}", [256], F32) for i in range(2)]
    p16 = [A.alloc(f"p16{i}", [256], BF16) for i in range(2)]
    pT = [A.alloc(f"pT{i}", [2, 128], BF16) for i in range(2)]
    tg = [A.alloc(f"tg{i}", [1024], F32) for i in range(2)]
    sgp = [A.alloc(f"sgp{i}", [1024], F32) for i in range(2)]
    ge = [A.alloc(f"ge{i}", [1024], F32) for i in range(2)]
    ot = [A.alloc(f"ot{i}", [1024], F32) for i in range(2)]
    junkf = A.alloc("junkf", [1024], BF16)
    P.dma("pool", wpg[:], w_ple_gate.ap().rearrange("(k p) c -> p k c", p=128), "wpg", [], ["wpg"])
    P.dma("pool", wpi[:], w_ple_in.ap().rearrange("(k p) c -> p k c", p=128), "wpi", [], ["wpi"])
    P.dma("sp", bple[:], bple_bc.ap(), "bple", [], ["bple"])
    P.dma("sp", gl[:], gbc.ap()[5], "gl", [], ["gl"])
    outkeys = []
    for t in range(16):
        i = t % 2
        tsl = slice(t * 128, (t + 1) * 128)
        P.dma("sp", ptl[i][:], p_own.ap()[tsl, :], f"ptl{i}", [], [f"ptl{i}"])
        P.cp("dve", p16[i][:], ptl[i][:], [f"ptl{i}"], [f"p16{i}"])
        for kk in range(2):
            P.tr(psb(6)[:, kk * 128:(kk + 1) * 128], p16[i][:, kk * 128:(kk + 1) * 128], ident, [f"p16{i}", "c16"], [psk(6)])
        P.cp("act", pT[i][:], psb(6)[:, 0:256].rearrange("p (a b) -> p a b", a=2), [psk(6)], [f"pT{i}"])
        bp = 0 if t % 2 == 0 else 4
        PSG = PS[:, bp * 512:bp * 512 + 1024]
        PSE = PS[:, 2 * 512:2 * 512 + 1024]
        for c2 in range(2):
            for k in range(8):
                P.mm(ps(bp + c2), v2T[:, k, tsl], wpg[:, k, c2 * 512:(c2 + 1) * 512], k == 0, k == 7, [f"v2T{t}", "wpg"], [psk(bp + c2)])
        for c2 in range(2):
            for k in range(2):
                P.mm(ps(2 + c2), pT[i][:, k, :], wpi[:, k, c2 * 512:(c2 + 1) * 512], k == 0, k == 1, [f"pT{i}", "wpi"], [psk(2 + c2)])
        P.tt("dve", tg[i][:], PSG, bple[:], ALU.add, [psk(bp), psk(bp + 1), "bple"], [f"tg{i}"])
        P.act(sgp[i][:], tg[i][:], AF.Sigmoid, [f"tg{i}"], [f"sgp{i}"])
        P.tt("dve", ge[i][:], PSE, sgp[i][:], ALU.mult, [psk(2), psk(3), f"sgp{i}"], [f"ge{i}"])
        rs, k_ = rms_rstd(ge[i][:], 1024, [f"ge{i}"], junkf)
        P.stt("dve", tg[i][:], ge[i][:], rs, gl[:], ALU.mult, ALU.mult, [f"ge{i}", k_, "gl", f"tg{i}"], [f"tg{i}"])
        P.tt("pool", ot[i][:], tg[i][:], facc[:, t, :], ALU.add, [f"tg{i}", f"facc{t}"], [f"ot{i}"])
        ok_ = f"outw{t}"
        outkeys.append(ok_)
        P.dma("sp", out.ap()[tsl, :], ot[i][:], f"ot{i}", [f"ot{i}"], [ok_])
    dbgkeys.extend(outkeys)
    A.pop()
    return finalize()


def _consts(core):
    c = np.zeros((128, NCST), np.float32)
    half = 128
    inv_freq = (1.0 / (np.float32(10000.0) ** np.linspace(0.0, 1.0, half, dtype=np.float32))).astype(np.float32)
    c[:, O_INVF:O_INVF + 128] = inv_freq[None, :]
    H = 4
    lg = np.log1p(-(2.0 ** (-5.0 - np.arange(H, dtype=np.float64))))
    idx = np.arange(128, dtype=np.float64)
    for h in range(H):
        diff = idx[None, :] - idx[:, None]
        DT = np.where(diff >= 0, np.exp(np.maximum(diff, 0.0) * lg[h]), 0.0) * (256.0 ** -0.5)
        c[:, O_DT + h * 128:O_DT + (h + 1) * 128] = DT.astype(np.float32)
        c[:, O_KSC + h] = (np.exp((127.0 - idx) * lg[h]) * (256.0 ** -0.5)).astype(np.float32)
        c[:, O_QDEC + h] = np.exp((idx + 1.0) * lg[h]).astype(np.float32)
        for r in range(8):
            if r < core:
                c[:, O_COEF + r * 4 + h] = np.float32(np.exp(128.0 * 16.0 * (core - 1 - r) * lg[h]))
    freqs = (np.float32(500000.0) ** (-np.arange(0, 16, 2, dtype=np.float32) / np.float32(16))).astype(np.float32)
    for p in range(128):
        d = p % 64
        if d < 16:
            c[p, O_FREQD] = freqs[d % 8]
            c[p, O_SSIGN] = -1.0 if d < 8 else 1.0
    return c


def _consts16(core):
    c = np.zeros((128, NC16), np.float32)
    c[:, O_ID:O_ID + 128] = np.eye(128, dtype=np.float32)
    perm = np.zeros((128, 128), np.float32)
    for m in range(128):
        d = m % 64
        if d < 8:
            perm[m + 8, m] = 1.0
        elif d < 16:
            perm[m - 8, m] = 1.0
    c[:, O_PERM:O_PERM + 128] = perm
    j = np.arange(128)[:, None]
    i = np.arange(128)[None, :]
    mprev = (j >= i).astype(np.float32)
    mcur = (j <= i).astype(np.float32)
    c[:, O_MK:O_MK + 128] = mprev
    c[:, O_MK + 128:O_MK + 256] = mcur
    c[:, O_MK0:O_MK0 + 128] = mprev if core > 0 else 0.0
    c[:, O_MK0 + 128:O_MK0 + 256] = mcur
    c[:, O_ONES:O_ONES + 64] = 1.0
    c[:, O_ONES + 128 + 64:O_ONES + 256] = 1.0
    return c


def make_in_maps(inputs):
    x = np.asarray(inputs["x"], np.float32)[0]
    p = np.asarray(inputs["p"], np.float32)[0, 0]
    pos = np.asarray(inputs["positions"], np.int32)[0]
    sq = lambda n: np.ascontiguousarray(np.asarray(inputs[n], np.float32)[0])
    shared = {
        "w_in": sq("w_in"), "w_ret_out": sq("w_ret_out"), "w_dil_out": sq("w_dil_out"), "w_o": sq("w_o"),
        "w_up": sq("w_up"), "w_down": sq("w_down"), "w_ple_gate": sq("w_ple_gate"), "w_ple_in": sq("w_ple_in"),
    }
    gnames = ["g_pre_mix", "g_post_mix", "g_pre_mlp", "g_post_mlp", "g_pre_ple", "g_post_ple"]
    gb = np.stack([np.broadcast_to(np.asarray(inputs[n], np.float32)[0][None, :], (128, 1024)) for n in gnames], 0)
    shared["gbc"] = np.ascontiguousarray(gb)
    shared["bple_bc"] = np.ascontiguousarray(np.broadcast_to(np.asarray(inputs["b_ple_gate"], np.float32)[0][None, :], (128, 1024)))
    bg = np.asarray(inputs["b_gate"], np.float32)[0]
    bgT = bg.reshape(2, 8, 128).transpose(2, 0, 1).reshape(128, 16)
    maps = []
    for c in range(NCORES):
        lo = c * TOK
        xa = np.zeros((4096, 1024), np.float32)
        pa = np.zeros((4096,), np.int32)
        if c > 0:
            xa[:2048] = x[lo - 2048:lo]
            pa[:2048] = pos[lo - 2048:lo]
        xa[2048:] = x[lo:lo + TOK]
        pa[2048:] = pos[lo:lo + TOK]
        cs = _consts(c)
        cs[:, O_BGT:O_BGT + 16] = bgT
        m = dict(shared)
        m.update({
            "x_all": xa,
            "p_own": np.ascontiguousarray(p[lo:lo + TOK]),
            "pos_bc": np.ascontiguousarray(np.broadcast_to(pa[None, :], (128, 4096))),
            "pos_tm": np.ascontiguousarray(pa[2048:].reshape(16, 128).T),
            "cst": cs,
            "cst16": _consts16(c),
        })
        maps.append(m)
    return maps


_NC_CACHE = {}


def kernel(**inputs):
    if "nc" not in _NC_CACHE:
        _NC_CACHE["nc"] = build()
    nc = _NC_CACHE["nc"]
    maps = make_in_maps(inputs)
    res = run_bass_kernel_spmd(nc, maps, core_ids=list(range(NCORES)))
    outs = [np.asarray(r["out"], np.float32) for r in res.results]
    return np.concatenate(outs, 0)[None]
```

```python
import math
import numpy as np
import concourse.bass as bass
import concourse.mybir as mb
from concourse.bass_utils import run_bass_kernel_spmd

F32 = mb.dt.float32
BF16 = mb.dt.bfloat16
I32 = mb.dt.int32
AF = mb.ActivationFunctionType
ALU = mb.AluOpType

NCORES = 8
TOK = 2048
EPS = 1e-6
SAME_ENG_SYNC = True
import os
ROPE_ENG = os.environ.get('ROPE_ENG', 'pool')
RSTOP = os.environ.get('RSTOP', '')
ARENA_BASE = 16640
ARENA_END = 229376

INV2PI = float(np.float32(1.0 / (2.0 * math.pi)))
C1 = 6.28125
C2 = 2.0 * math.pi - 6.28125
PI = math.pi
PIC = 3.14159
HALFPI = math.pi / 2.0
TWOPI = 2.0 * math.pi

O_INVF = 0
O_DT = 128
O_KSC = 640
O_QDEC = 644
O_COEF = 648
O_FREQD = 680
O_SSIGN = 681
O_BGT = 682
NCST = 704
O_ID = 0
O_PERM = 128
O_MK = 256
O_MK0 = 512
O_ONES = 768
NC16 = 1024


class Op:
    __slots__ = ("eng", "fn", "r", "w", "stream", "inc", "epoch", "sig", "val", "deps")


class Prog:
    def __init__(self):
        self.ops = []
        self.epoch = 0

    def add(self, eng, fn, r=(), w=(), stream=None, inc=16):
        op = Op()
        op.eng = eng
        op.fn = fn
        r = tuple(r)
        w = tuple(w)
        if eng != "pe":
            extra = tuple(k for k in r if k.startswith("ps") and k[2:].isdigit() and k not in w)
            w = w + extra
        op.r = r
        op.w = w
        op.stream = stream
        op.inc = inc
        op.epoch = self.epoch
        op.sig = stream is not None
        op.val = 0
        op.deps = ()
        self.ops.append(op)
        return op

    def barrier(self):
        self.epoch += 1

    def mm(self, out, lhsT, rhs, start, stop, r, w):
        self.add("pe", lambda e: e.matmul(out, lhsT, rhs, start=start, stop=stop), r, w)

    def tr(self, out, in_, ident, r, w):
        self.add("pe", lambda e: e.transpose(out, in_, ident), r, w)

    def act(self, out, in_, func, r, w, bias=None, scale=None, accum=None):
        kw = {}
        if bias is not None:
            kw["bias"] = bias
        if scale is not None:
            kw["scale"] = scale
        if accum is not None:
            kw["accum_out"] = accum
        self.add("act", lambda e: e.activation(out, in_, func, **kw), r, w)

    def tt(self, eng, out, in0, in1, op, r, w):
        self.add(eng, lambda e: e.tensor_tensor(out, in0, in1, op), r, w)

    def stt(self, eng, out, in0, scalar, in1, op0, op1, r, w):
        self.add(eng, lambda e: e.scalar_tensor_tensor(out, in0, scalar, in1, op0, op1), r, w)

    def ts(self, eng, out, in0, s1, s2, op0, op1, r, w):
        if op1 is None:
            self.add(eng, lambda e: e.tensor_scalar(out, in0, s1, None, op0), r, w)
        else:
            self.add(eng, lambda e: e.tensor_scalar(out, in0, s1, s2, op0, op1), r, w)

    def cp(self, eng, out, in_, r, w):
        if eng == "act":
            self.add(eng, lambda e: e.copy(out, in_), r, w)
        else:
            self.add(eng, lambda e: e.tensor_copy(out, in_), r, w)

    def dma(self, eng, out, in_, stream, r, w):
        self.add(eng, lambda e: e.dma_start(out=out, in_=in_), r, w, stream=stream)

    def analyse(self):
        ops = self.ops
        last_w, readers, last_by_prod = {}, {}, {}
        bar_deps = frozenset()
        cur_epoch = 0
        for i, op in enumerate(ops):
            if op.epoch != cur_epoch:
                cur_epoch = op.epoch
                bar_deps = frozenset(last_by_prod.values())
                last_w, readers = {}, {}
            deps = set(bar_deps)
            for k in op.r:
                j = last_w.get(k)
                if j is not None:
                    deps.add(j)
            for k in op.w:
                j = last_w.get(k)
                rd = readers.get(k)
                if j is not None:
                    if not (op.stream is not None and ops[j].stream == op.stream and not rd):
                        deps.add(j)
                if rd:
                    deps.update(rd.values())
            pid = (op.stream, i) if op.stream else op.eng
            for k in op.r:
                readers.setdefault(k, {})[pid] = i
            for k in op.w:
                last_w[k] = i
                readers[k] = {}
            last_by_prod[op.stream or op.eng] = i
            fdeps = []
            for j in deps:
                pj = ops[j]
                if pj.stream is None and op.stream is None and pj.eng == op.eng:
                    if op.eng == "pe" or not SAME_ENG_SYNC:
                        continue
                fdeps.append(j)
                pj.sig = True
            op.deps = fdeps
        cnt = {}
        for op in ops:
            if op.stream:
                cnt[op.stream] = cnt.get(op.stream, 0) + (op.inc if op.inc else 1)
                op.val = cnt[op.stream]
            elif op.sig:
                cnt[op.eng] = cnt.get(op.eng, 0) + 1
                op.val = cnt[op.eng]
        return cnt

    def emit(self, nc):
        import contextlib
        cnt = self.analyse()
        ops = self.ops
        names = sorted(cnt.keys())
        with contextlib.ExitStack() as st:
            sems = {n: st.enter_context(nc.semaphore("s_" + n)) for n in names}
            block = st.enter_context(nc.Block())

            def run(engname):
                def body(e):
                    waited = {}
                    for op in ops:
                        if op.eng != engname:
                            continue
                        need = {}
                        for j in op.deps:
                            pj = ops[j]
                            key = pj.stream or pj.eng
                            if pj.val > need.get(key, 0):
                                need[key] = pj.val
                        for key, val in need.items():
                            if waited.get(key, 0) < val:
                                e.wait_ge(sems[key], val)
                                waited[key] = val
                        if op.fn is None:
                            continue
                        ins = op.fn(e)
                        if op.stream:
                            if op.inc:
                                ins.then_inc(sems[op.stream], op.inc)
                            else:
                                ins.then_inc(sems[op.stream])
                        elif op.sig:
                            ins.then_inc(sems[op.eng], 1)
                return body

            block.tensor(run("pe"))
            block.scalar(run("act"))
            block.vector(run("dve"))
            block.gpsimd(run("pool"))
            block.sync(run("sp"))


def bcast(ap, axis, n):
    dims = [list(d) for d in ap.ap]
    dims.insert(axis, [0, n])
    return bass.AP(ap.tensor, ap.offset, dims)


def build(dbg=(), stop=None, fake_ag=False):
    nc = bass.Bass("TRN2", target_bir_lowering=False)
    P = Prog()

    def din(name, shape, dt=F32):
        return nc.dram_tensor(name, list(shape), dt, kind="ExternalInput")

    dbgkeys = []

    def dbgs():
        k = f"dbg{len(dbgkeys)}"
        dbgkeys.append(k)
        return k

    def finalize():
        P.add("sp", None, list(dbgkeys), [])
        P.emit(nc)
        return nc

    x_all = din("x_all", [4096, 1024])
    p_own = din("p_own", [2048, 256])
    pos_bc = din("pos_bc", [128, 4096], I32)
    pos_tm = din("pos_tm", [128, 16], I32)
    w_in = din("w_in", [1024, 12800])
    w_ret_out = din("w_ret_out", [2048, 1024])
    w_dil_out = din("w_dil_out", [512, 1024])
    w_o = din("w_o", [1024, 1024])
    w_up = din("w_up", [1024, 4096])
    w_down = din("w_down", [4096, 1024])
    w_ple_gate = din("w_ple_gate", [1024, 1024])
    w_ple_in = din("w_ple_in", [256, 1024])
    gbc = din("gbc", [6, 128, 1024])
    bple_bc = din("bple_bc", [128, 1024])
    cst = din("cst", [128, NCST])
    cst16 = din("cst16", [128, NC16])
    out = nc.dram_tensor("out", [2048, 1024], F32, kind="ExternalOutput")
    dbg_t = {}
    for name, shape in dbg:
        dbg_t[name] = nc.dram_tensor("dbg_" + name, list(shape), F32, kind="ExternalOutput")

    tblC = nc.dram_tensor("tblC", [128, 4096], F32)
    tblS = nc.dram_tensor("tblS", [128, 4096], F32)
    yrT_d = nc.dram_tensor("yrT_d", [128, 16, 2048], BF16)
    yaT_d = nc.dram_tensor("yaT_d", [128, 4, 2048], BF16)
    mixT_d = nc.dram_tensor("mixT_d", [128, 8, 2048], BF16)
    h1_d = nc.dram_tensor("h1_d", [2048, 1024], F32)
    loc_d = [nc.dram_tensor(f"loc{h}", [128, 1024], F32) for h in range(4)]
    gath_d = [nc.dram_tensor(f"gath{h}", [1024, 1024], F32) for h in range(4)]

    w_in_v = w_in.ap().rearrange("(k p) c -> p k c", p=128)

    class Arena:
        def __init__(s):
            s.cur = ARENA_BASE
            s.stack = []
            s.n = 0

        def alloc(s, name, fshape, dt):
            nbytes = int(np.prod(fshape)) * mb.dt.size(dt)
            off = s.cur
            s.cur += (nbytes + 31) // 32 * 32
            assert s.cur <= ARENA_END, (name, s.cur)
            s.n += 1
            return nc.alloc_sbuf_tensor_at(f"{name}_{s.n}", [128] + list(fshape), dt, offset=off)

        def push(s):
            s.stack.append(s.cur)

        def pop(s):
            s.cur = s.stack.pop()

    A = Arena()
    PS = nc.alloc_psum_tensor("ps", [128, 4096], F32)
    PSB = PS.bitcast(BF16)

    def ps(b, n=512, off=0):
        return PS[:, b * 512 + off: b * 512 + off + n]

    def psb(b, n=1024, off=0):
        return PSB[:, b * 1024 + off: b * 1024 + off + n]

    def psk(b):
        return f"ps{b}"

    c32 = A.alloc("c32", [NCST], F32)
    c16 = A.alloc("c16", [NC16], BF16)
    epst = A.alloc("eps", [8], F32)
    stat = A.alloc("stat", [3, 16], F32)
    wsl = [A.alloc(f"wsl{i}", [8, 512], BF16) for i in range(3)]
    A.push()
    uTo = A.alloc("uTo", [8, 2048], BF16)
    cos_r = A.alloc("cos_r", [16, 128], F32)
    sin_r = A.alloc("sin_r", [16, 128], F32)

    P.dma("sp", c32[:], cst.ap(), "c32", [], ["c32"])
    P.dma("pool", c16[:], cst16.ap(), "c16", [], ["c16"])
    P.add("dve", lambda e: e.memset(epst[:], EPS), [], ["eps"])

    invf = c32[:, O_INVF:O_INVF + 128]
    ident = c16[:, O_ID:O_ID + 128]
    permT = c16[:, O_PERM:O_PERM + 128]
    freq_d = c32[:, O_FREQD:O_FREQD + 1]
    ssign = c32[:, O_SSIGN:O_SSIGN + 1]

    stat_ctr = [0]

    def rms_rstd(src, n, rkeys, junk):
        i = stat_ctr[0] % 16
        stat_ctr[0] += 1
        k = f"st{i}"
        ss, sq, rs = stat[:, 0, i:i + 1], stat[:, 1, i:i + 1], stat[:, 2, i:i + 1]
        P.add("dve", lambda e: e.memset(ss, 0.0), [], [k])
        P.act(junk[:, 0:n], src, AF.Square, list(rkeys) + [k], ["junk", k], accum=ss)
        P.act(sq, ss, AF.Sqrt, [k, "eps"], [k], bias=epst[:, 0:1], scale=1.0 / n)
        P.add("dve", lambda e: e.reciprocal(rs, sq), [k], [k])
        return rs, k

    def sincos(ang, n, T, sin_out, cos_out, kin, kout, sin_scale=None):
        e = "dve"
        kt = "sctmp"
        P.ts(e, T["ki"][:, 0:n], ang, INV2PI, None, ALU.mult, None, [kin], [kt])
        P.cp(e, T["kf"][:, 0:n], T["ki"][:, 0:n], [kt], [kt])
        r_, rc, tf, kf = T["r"][:, 0:n], T["rc"][:, 0:n], T["tf"][:, 0:n], T["kf"][:, 0:n]
        P.stt(e, r_, kf, -C1, ang, ALU.mult, ALU.add, [kt, kin], [kt])
        P.stt(e, r_, kf, -C2, r_, ALU.mult, ALU.add, [kt], [kt])
        P.ts(e, tf, r_, PI, None, ALU.is_gt, None, [kt], [kt])
        P.stt(e, r_, tf, -TWOPI, r_, ALU.mult, ALU.add, [kt], [kt])
        P.ts(e, rc, r_, HALFPI, None, ALU.add, None, [kt], [kt])
        P.ts(e, tf, rc, PI, None, ALU.is_gt, None, [kt], [kt])
        P.stt(e, rc, tf, -TWOPI, rc, ALU.mult, ALU.add, [kt], [kt])
        P.ts(e, r_, r_, -PIC, PIC, ALU.max, ALU.min, [kt], [kt])
        P.ts(e, rc, rc, -PIC, PIC, ALU.max, ALU.min, [kt], [kt])
        if sin_scale is not None:
            P.act(sin_out, r_, AF.Sin, [kt, "c32"], [kout], scale=sin_scale)
        else:
            P.act(sin_out, r_, AF.Sin, [kt], [kout])
        P.act(cos_out, rc, AF.Sin, [kt], [kout])

    def dbg_dump(name, src_ap, rkeys, dst_ap=None):
        if name in dbg_t:
            d = dbg_t[name].ap() if dst_ap is None else dst_ap
            k = dbgs()
            P.dma("sp", d, src_ap, k, rkeys, [k])

    A.push()
    uTh = A.alloc("uTh", [8, 2048], BF16)
    yaT = A.alloc("yaT", [4, 2048], BF16)
    tbs = [A.alloc(f"tbs{i}", [2, 512], F32) for i in range(2)]
    A.push()
    xsl = [A.alloc(f"xs{i}", [1024], F32) for i in range(3)]
    u16 = [A.alloc(f"u16{i}", [1024], BF16) for i in range(2)]
    junk = A.alloc("junk", [1024], BF16)
    g0 = A.alloc("g0", [1024], F32)
    T = {"ki": A.alloc("ki", [1024], I32), "kf": A.alloc("kf", [1024], F32), "r": A.alloc("r", [1024], F32),
         "rc": A.alloc("rc", [1024], F32), "tf": A.alloc("tf", [1024], F32)}
    angb = A.alloc("angb", [1024], F32)
    posi = A.alloc("posi", [1024], I32)
    csl = A.alloc("csl", [2, 1024], F32)
    ptm_i = A.alloc("ptm_i", [16], I32)
    ptm_f = A.alloc("ptm_f", [16], F32)

    P.dma("sp", g0[:], gbc.ap()[0], "g0", [], ["g0"])
    for t in range(32):
        s3, s2 = t % 3, t % 2
        xs, u = xsl[s3], u16[s2]
        P.dma("sp", xs[:], x_all.ap()[t * 128:(t + 1) * 128, :], f"xs{s3}", [], [f"xs{s3}"])
        rs, k = rms_rstd(xs[:], 1024, [f"xs{s3}"], junk)
        P.stt("dve", u[:], xs[:], rs, g0[:], ALU.mult, ALU.mult, [f"xs{s3}", k, "g0"], [f"u16{s2}"])
        for kk in range(8):
            P.tr(psb(s2)[:, kk * 128:(kk + 1) * 128], u[:, kk * 128:(kk + 1) * 128], ident,
                 [f"u16{s2}", "c16"], [psk(s2)])
        dst = (uTh if t < 16 else uTo)[:, :, (t % 16) * 128:(t % 16 + 1) * 128]
        P.cp("act" if t % 2 else "dve", dst, psb(s2).rearrange("p (k c) -> p k c", k=8), [psk(s2)], [f"uT{t}"])

    for c in range(4):
        P.dma("sp", posi[:], pos_bc.ap()[:, c * 1024:(c + 1) * 1024], "posi", [], ["posi"])
        P.cp("dve", angb[:], posi[:], ["posi"], ["angb"])
        P.ts("dve", angb[:], angb[:], freq_d, None, ALU.mult, None, ["angb", "c32"], ["angb"])
        sincos(angb[:], 1024, T, csl[:, 1, :], csl[:, 0, :], "angb", "csl", sin_scale=ssign)
        P.dma("sp", tblC.ap()[:, c * 1024:(c + 1) * 1024], csl[:, 0, :], "tblCw", ["csl"], ["tblC"])
        P.dma("sp", tblS.ap()[:, c * 1024:(c + 1) * 1024], csl[:, 1, :], "tblSw", ["csl"], ["tblS"])
    P.dma("sp", ptm_i[:], pos_tm.ap(), "ptm", [], ["ptm_i"])
    P.cp("dve", ptm_f[:], ptm_i[:], ["ptm_i"], ["ptm_f"])
    for c in range(2):
        for t in range(8):
            P.ts("dve", angb[:, t * 128:(t + 1) * 128], invf, ptm_f[:, c * 8 + t:c * 8 + t + 1], None, ALU.mult, None,
                 ["c32", "ptm_f"], ["angb"])
        sincos(angb[:], 1024, T, sin_r[:, c * 8:(c + 1) * 8, :].rearrange("p a b -> p (a b)"),
               cos_r[:, c * 8:(c + 1) * 8, :].rearrange("p a b -> p (a b)"), "angb", "csr")
    if "uT" in dbg_t:
        dd = dbg_t["uT"].ap().rearrange("(k p) t -> p k t", p=128)
        k = dbgs()
        P.dma("pool", dd[:, :, 0:2048], uTh[:], k, [f"uT{t}" for t in range(16)], [k])
        k = dbgs()
        P.dma("pool", dd[:, :, 2048:4096], uTo[:], k, [f"uT{t}" for t in range(16, 32)], [k])
    dbg_dump("cosr", cos_r[:].rearrange("p a b -> p (a b)"), ["csr"])
    dbg_dump("sinr", sin_r[:].rearrange("p a b -> p (a b)"), ["csr"])
    if stop == "A":
        return finalize()
    P.barrier()
    A.pop()

    qTp = A.alloc("qTp", [2, 2048], BF16)
    kT = A.alloc("kT", [4096], BF16)
    vT = A.alloc("vT", [2048], BF16)
    vtm = A.alloc("vtm", [32, 128], BF16)
    ones16 = A.alloc("ones16", [128], BF16)
    acc = A.alloc("acc", [2, 2048], F32)
    x16 = [A.alloc(f"x16{i}", [512], BF16) for i in range(2)]
    t1 = [A.alloc(f"t1{i}", [512], F32) for i in range(2)]
    t2 = [A.alloc(f"t2{i}", [512], F32) for i in range(2)]
    pexp = [A.alloc(f"pexp{i}", [512], BF16) for i in range(2)]
    pmk = [A.alloc(f"pmk{i}", [512], BF16) for i in range(2)]
    P.add("pool", lambda e: e.memset(qTp[:], 0.0), [], ["qT"])
    P.add("pool", lambda e: e.memset(ones16[:], 1.0), [], ["ones16"])
    mk = c16[:, O_MK:O_MK + 256]
    mk0 = c16[:, O_MK0:O_MK0 + 256]

    ctr = {"pb": 0, "rope": 0, "sc": 0, "pu": 0, "ws": 0}

    def utok(k, tok0, n):
        if tok0 < 2048:
            assert tok0 + n <= 2048
            return uTh[:, k, tok0:tok0 + n], [f"uT{t}" for t in range(tok0 // 128, (tok0 + n + 127) // 128)]
        o = tok0 - 2048
        return uTo[:, k, o:o + n], [f"uT{16 + t}" for t in range(o // 128, (o + n + 127) // 128)]

    def proj_fm(wcols, wkey, tok0, n):
        bank = ctr["pb"] % 2
        ctr["pb"] += 1
        for k in range(8):
            rhs, rk = utok(k, tok0, n)
            P.mm(ps(bank, n), wcols[:, k, :], rhs, k == 0, k == 7, [wkey] + rk, [psk(bank)])
        return bank

    def rope_fm(bank, tok0, n, dst, dkey):
        i = ctr["rope"] % 2
        ctr["rope"] += 1
        tb = tbs[i]
        P.dma("sp", tb[:, 0, 0:n], tblC.ap()[:, tok0:tok0 + n], f"tbs{i}", ["tblC"], [f"tbs{i}"])
        P.dma("sp", tb[:, 1, 0:n], tblS.ap()[:, tok0:tok0 + n], f"tbs{i}", ["tblS"], [f"tbs{i}"])
        if RSTOP == "R1":
            return
        P.cp("act", x16[i][:, 0:n], ps(bank, n), [psk(bank)], [f"x16{i}"])
        if RSTOP == "R2":
            return
        P.mm(ps(2, n), permT, x16[i][:, 0:n], True, True, ["c16", f"x16{i}"], [psk(2)])
        if RSTOP == "R3":
            return
        if RSTOP == "E1":
            P.tt("dve", t1[i][:, 0:n], t2[i][:, 0:n], tb[:, 0, 0:n], ALU.mult, [psk(bank), f"tbs{i}"], [f"t1{i}"])
            return
        if RSTOP == "E4":
            P.cp("dve", t1[i][:, 0:n], ps(bank, n), [psk(bank), f"tbs{i}"], [f"t1{i}"])
            return
        if RSTOP == "E5":
            P.cp("dve", t1[i][:, 0:n], ps(bank, n), [psk(bank)], [f"t1{i}"])
            return
        P.tt("dve", t1[i][:, 0:n], ps(bank, n), tb[:, 0, 0:n], ALU.mult, [psk(bank), f"tbs{i}"], [f"t1{i}"])
        if RSTOP == "R4":
            return
        P.tt("dve", t2[i][:, 0:n], ps(2, n), tb[:, 1, 0:n], ALU.mult, [psk(2), f"tbs{i}"], [f"t2{i}"])
        if RSTOP == "R5":
            return
        if dst is None:
            o = tok0 - 2048
            P.tt(ROPE_ENG, qTp[0:64, 0, o:o + n], t1[i][0:64, 0:n], t2[i][0:64, 0:n], ALU.add, [f"t1{i}", f"t2{i}"], [dkey])
            P.tt(ROPE_ENG, qTp[64:128, 1, o:o + n], t1[i][64:128, 0:n], t2[i][64:128, 0:n], ALU.add, [f"t1{i}", f"t2{i}"], [dkey])
        else:
            P.tt(ROPE_ENG, dst, t1[i][:, 0:n], t2[i][:, 0:n], ALU.add, [f"t1{i}", f"t2{i}"], [dkey])

    for sp_ in range(4):
        for g in range(3):
            Dg = (1, 4, 16)[g]
            nbl = 16 // Dg
            halo = 128 * Dg
            start = 2048 - halo
            slot = ctr["ws"] % 2
            ctr["ws"] += 1
            wd = wsl[slot]
            wkey = f"wsl{slot}"
            for part, base in enumerate((6144, 7680, 9216)):
                col0 = base + (g * 8 + 2 * sp_) * 64
                P.dma("pool", wd[:, :, part * 128:(part + 1) * 128], w_in_v[:, :, col0:col0 + 128], wkey, [], [wkey])
            if stop == "B0":
                return finalize()
            for b in range(4):
                bank = proj_fm(wd[:, :, 0:128], wkey, 2048 + b * 512, 512)
                if stop == "B0b":
                    return finalize()
                rope_fm(bank, 2048 + b * 512, 512, None, "qT")
            if stop == "B1":
                return finalize()
            blocks = [(start, min(512, halo))] if halo <= 512 else [(i * 512, 512) for i in range(4)]
            blocks += [(2048 + i * 512, 512) for i in range(4)]
            for (tok0, n) in blocks:
                bank = proj_fm(wd[:, :, 128:256], wkey, tok0, n)
                rope_fm(bank, tok0, n, kT[:, tok0:tok0 + n], "kT")
            if stop == "B2":
                return finalize()
            for half in range(2):
                hb = [bk for bk in blocks if (bk[0] < 2048) == (half == 0)]
                for (tok0, n) in hb:
                    bank = proj_fm(wd[:, :, 256:384], wkey, tok0, n)
                    o = tok0 - 2048 * half
                    P.cp("act", vT[:, o:o + n], ps(bank, n), [psk(bank)], ["vT"])
                if half == 0:
                    tiles = [(c, 2048 - 128 * Dg + c) for c in range(Dg)]
                    idx0 = 0
                else:
                    tiles = [(j * Dg + c, (j - 1) * 128 * Dg + c) for j in range(1, nbl + 1) for c in range(Dg)]
                    idx0 = Dg
                for q0 in range(0, len(tiles), 8):
                    grp = tiles[q0:q0 + 8]
                    for q, (ti, s0) in enumerate(grp):
                        P.tr(psb(7)[:, q * 128:(q + 1) * 128], vT[:, s0:s0 + 127 * Dg + 1:Dg], ident,
                             ["vT", "c16"], [psk(7)])
                    i0 = grp[0][0]
                    ng = len(grp)
                    src = psb(7).rearrange("p (a b) -> p a b", b=128)
                    P.cp("act" if (q0 // 8) % 2 == 0 else "dve", vtm[:, i0:i0 + ng, :], src[:, 0:ng, :], [psk(7)], ["vtm"])
            if stop == "B3":
                return finalize()
            for j in range(1, nbl + 1):
                for c in range(Dg):
                    q0 = (j - 1) * 128 * Dg + c
                    qsl = slice(q0, q0 + 127 * Dg + 1, Dg)
                    kprev0 = (2048 - 128 * Dg + c) if j == 1 else (2048 + (j - 2) * 128 * Dg + c)
                    kcur0 = 2048 + q0
                    ksl = [slice(kprev0, kprev0 + 127 * Dg + 1, Dg), slice(kcur0, kcur0 + 127 * Dg + 1, Dg)]
                    tidx = [(j - 1) * Dg + c, j * Dg + c]
                    sb = 3 + ctr["sc"] % 2
                    si = ctr["sc"] % 2
                    ctr["sc"] += 1
                    for kb in range(2):
                        P.mm(ps(sb, 256, kb * 256), kT[:, ksl[kb]], qTp[:, :, qsl], True, True, ["kT", "qT"], [psk(sb)])
                    P.act(pexp[si][:], ps(sb), AF.Exp, [psk(sb)], [f"pexp{si}"], scale=0.125)
                    msk = mk0 if j == 1 else mk
                    m4 = bcast(msk.rearrange("p (a b) -> p a b", a=2), 2, 2)
                    P.tt("dve", pmk[si][:].rearrange("p (a s b) -> p a s b", a=2, s=2),
                         pexp[si][:].rearrange("p (a s b) -> p a s b", a=2, s=2), m4, ALU.mult,
                         [f"pexp{si}", "c16"], [f"pmk{si}"])
                    ub = 5 + ctr["pu"] % 2
                    ctr["pu"] += 1
                    for kb in range(2):
                        P.mm(ps(ub, 256, 0), vtm[:, tidx[kb], :], pmk[si][:, kb * 256:(kb + 1) * 256],
                             kb == 0, kb == 1, ["vtm", f"pmk{si}"], [psk(ub)])
                    for kb in range(2):
                        P.mm(ps(ub, 256, 256), ones16[:], pmk[si][:, kb * 256:(kb + 1) * 256],
                             kb == 0, kb == 1, ["ones16", f"pmk{si}"], [psk(ub)])
                    for hf in range(2):
                        pr = slice(64 * hf, 64 * hf + 64)
                        dsta = acc[pr, :, qsl]
                        srca = PS[pr, ub * 512 + hf * 128:ub * 512 + hf * 128 + 512].rearrange("p (a b) -> p a b", a=2)[:, :, 0:128]
                        if g == 0:
                            P.cp("act" if hf == 0 else "dve", dsta, srca, [psk(ub)], ["acc"])
                        else:
                            P.tt("dve", dsta, srca, dsta, ALU.add, [psk(ub), "acc"], ["acc"])
            if stop == "B4":
                return finalize()
        P.add("dve", lambda e: e.reciprocal(acc[:, 1, :], acc[:, 1, :]), ["acc"], ["acc"])
        P.tt("dve", yaT[:, sp_, :], acc[:, 0, :], acc[:, 1, :], ALU.mult, ["acc"], [f"yaT{sp_}"])
    if "yaT" in dbg_t:
        for sp_ in range(4):
            P.cp("dve", acc[:, 0, :], yaT[:, sp_, :], [f"yaT{sp_}", "acc"], ["acc"])
            k = dbgs()
            P.dma("sp", dbg_t["yaT"].ap()[sp_ * 128:(sp_ + 1) * 128, :], acc[:, 0, :], k, ["acc"], [k])
    P.dma("sp", yaT_d.ap(), yaT[:], "yaTw", [f"yaT{q}" for q in range(4)], ["yaT_d"])
    P.barrier()
    A.pop()

    if stop == "B":
        return finalize()

    A.push()
    qTr = [A.alloc(f"qTr{i}", [2, 2048], BF16) for i in range(2)]
    kTr = [A.alloc(f"kTr{i}", [2, 2048], BF16) for i in range(2)]
    kd = [A.alloc(f"kd{i}", [16, 256], BF16) for i in range(2)]
    v16 = [A.alloc(f"v16{i}", [16, 512], BF16) for i in range(2)]
    Gb = [A.alloc(f"Gb{i}", [1024], F32) for i in range(2)]
    Rf = [A.alloc(f"Rf{i}", [2, 512], F32) for i in range(2)]
    R16 = [A.alloc(f"R16{i}", [2, 512], BF16) for i in range(2)]
    ra = [A.alloc(f"ra{i}", [4, 256], F32) for i in range(2)]
    qk16 = [A.alloc(f"qk16{i}", [512], BF16) for i in range(2)]
    pmr = [A.alloc(f"pmr{i}", [128], BF16) for i in range(2)]
    yo = [A.alloc(f"yo{i}", [512], F32) for i in range(2)]
    yy = [A.alloc(f"yy{i}", [512], F32) for i in range(2)]
    sg = [A.alloc(f"sg{i}", [512], F32) for i in range(2)]
    yr16 = [A.alloc(f"yr16{i}", [512], BF16) for i in range(2)]
    yst = [A.alloc(f"yst{i}", [4, 128], BF16) for i in range(2)]
    junkc = A.alloc("junkc", [512], BF16)
    agflag = A.alloc("agflag", [8], F32)
    PS67 = PS[:, 6 * 512:8 * 512]
    lgam = [math.log1p(-(2.0 ** (-5.0 - h))) for h in range(4)]
    cdhs = [float(np.float32(math.exp(128.0 * lgam[h]))) for h in range(4)]
    wq, wv, wg = wsl[0], wsl[1], wsl[2]

    def c_load(h):
        P.dma("pool", wq[:, :, 0:256], w_in_v[:, :, 256 * h:256 * h + 256], "wsl0", [], ["wsl0"])
        P.dma("pool", wq[:, :, 256:512], w_in_v[:, :, 1024 + 256 * h:1024 + 256 * h + 256], "wsl0", [], ["wsl0"])
        P.dma("pool", wv[:], w_in_v[:, :, 2048 + 512 * h:2048 + 512 * h + 512], "wsl1", [], ["wsl1"])

    def c_load_g(h):
        P.dma("pool", wg[:], w_in_v[:, :, 4096 + 512 * h:4096 + 512 * h + 512], "wsl2", [], ["wsl2"])

    def c_proj(h):
        hb = h % 2
        ksc = c32[:, O_KSC + h:O_KSC + h + 1]
        for t in range(16):
            bank = t % 2
            i = t % 2
            tsl = slice(t * 128, (t + 1) * 128)
            for k in range(8):
                P.mm(ps(bank), uTo[:, k, tsl], wq[:, k, :], k == 0, k == 7, [f"uT{16 + t}", "wsl0"], [psk(bank)])
            ps4 = ps(bank).rearrange("p (a b c) -> p a b c", a=2, b=2)
            x1, x2 = ps4[:, :, 0, :], ps4[:, :, 1, :]
            cb = bcast(cos_r[:, t, :], 1, 2)
            sb_ = bcast(sin_r[:, t, :], 1, 2)
            r4 = ra[i][:].rearrange("p a (b c) -> p a b c", b=2)
            P.tt("dve", r4[:, 0], x1, cb, ALU.mult, [psk(bank), "csr"], [f"ra{i}"])
            P.tt("dve", r4[:, 1], x2, sb_, ALU.mult, [psk(bank), "csr"], [f"ra{i}"])
            P.tt("dve", r4[:, 2], x2, cb, ALU.mult, [psk(bank), "csr"], [f"ra{i}"])
            P.tt("dve", r4[:, 3], x1, sb_, ALU.mult, [psk(bank), "csr"], [f"ra{i}"])
            q4 = qk16[i][:].rearrange("p (a b c) -> p a b c", a=2, b=2)
            P.tt("pool", q4[:, :, 0, :], r4[:, 0], r4[:, 1], ALU.subtract, [f"ra{i}"], [f"qk16{i}"])
            P.tt("pool", q4[:, :, 1, :], r4[:, 2], r4[:, 3], ALU.add, [f"ra{i}"], [f"qk16{i}"])
            P.act(kd[hb][:, t, :], qk16[i][:, 256:512], AF.Copy, [f"qk16{i}", "c32"], [f"kd{hb}"], scale=ksc)
            for j in range(4):
                P.tr(psb(2)[:, j * 128:(j + 1) * 128], qk16[i][:, j * 128:(j + 1) * 128], ident,
                     [f"qk16{i}", "c16"], [psk(2)])
            P.cp("act", qTr[hb][:, :, tsl], psb(2)[:, 0:256].rearrange("p (a b) -> p a b", a=2), [psk(2)], [f"qTr{hb}"])
            P.cp("dve", kTr[hb][:, :, tsl], psb(2)[:, 256:512].rearrange("p (a b) -> p a b", a=2), [psk(2)], [f"kTr{hb}"])
            vb = 3 + t % 2
            for k in range(8):
                P.mm(ps(vb), uTo[:, k, tsl], wv[:, k, :], k == 0, k == 7, [f"uT{16 + t}", "wsl1"], [psk(vb)])
            P.cp("act", v16[hb][:, t, :], ps(vb), [psk(vb)], [f"v16{hb}"])

    def c_passA(h):
        hb = h % 2
        Rflat = Rf[hb][:].rearrange("p a b -> p (a b)")
        P.add("dve", lambda e: e.memset(Rf[hb][:], 0.0), [], [f"Rf{hb}"])
        for n in range(16):
            P.mm(ps(6), kd[hb][:, n, 0:128], v16[hb][:, n, :], True, True, [f"kd{hb}", f"v16{hb}"], [psk(6)])
            P.mm(ps(7), kd[hb][:, n, 128:256], v16[hb][:, n, :], True, True, [f"kd{hb}", f"v16{hb}"], [psk(7)])
            P.stt("dve", Rflat, Rflat, cdhs[h], PS67, ALU.mult, ALU.add, [f"Rf{hb}", psk(6), psk(7)], [f"Rf{hb}"])
        P.barrier()
        P.dma("pool", loc_d[h].ap(), Rflat, f"agw{h}", [f"Rf{hb}"], [f"loc{h}"])
        if fake_ag:
            for r_ in range(8):
                P.dma("pool", gath_d[h].ap()[r_ * 128:(r_ + 1) * 128, :], loc_d[h].ap(), f"cc{h}", [f"loc{h}"], [f"gath{h}"])
        else:
            P.add("pool", lambda e, h=h: e.collective_compute(
                "AllGather", ALU.bypass, replica_groups=[list(range(NCORES))],
                ins=[loc_d[h].ap()], outs=[gath_d[h].ap()]), [f"loc{h}"], [f"gath{h}"], stream=f"cc{h}", inc=None)
            P.add("pool", lambda e: e.memset(agflag[:], 0.0), [f"gath{h}"], ["agflag"])
        P.barrier()

    def c_combine(h):
        hb = h % 2
        Rflat = Rf[hb][:].rearrange("p a b -> p (a b)")
        for r_ in range(8):
            gi = r_ % 2
            P.dma("sp", Gb[gi][:], gath_d[h].ap()[r_ * 128:(r_ + 1) * 128, :], f"Gb{gi}", [f"gath{h}"], [f"Gb{gi}"])
            cf = c32[:, O_COEF + r_ * 4 + h:O_COEF + r_ * 4 + h + 1]
            if r_ == 0:
                P.ts("dve", Rflat, Gb[gi][:], cf, None, ALU.mult, None, [f"Gb{gi}", "c32", f"Rf{hb}"], [f"Rf{hb}"])
            else:
                P.stt("dve", Rflat, Gb[gi][:], cf, Rflat, ALU.mult, ALU.add, [f"Gb{gi}", "c32", f"Rf{hb}"], [f"Rf{hb}"])
        P.cp("act", R16[0][:], Rf[hb][:], [f"Rf{hb}"], ["R160"])

    def c_passB(h):
        hb = h % 2
        Rflat = Rf[hb][:].rearrange("p a b -> p (a b)")
        qdc = c32[:, O_QDEC + h:O_QDEC + h + 1]
        DTh = c32[:, O_DT + h * 128:O_DT + (h + 1) * 128]
        kdk, vk, qk_, kk_ = f"kd{hb}", f"v16{hb}", f"qTr{hb}", f"kTr{hb}"
        for n in range(16):
            i = n % 2
            rb, rn = n % 2, (n + 1) % 2
            nt = slice(n * 128, (n + 1) * 128)
            if n < 15:
                P.mm(ps(6), kd[hb][:, n, 0:128], v16[hb][:, n, :], True, True, [kdk, vk], [psk(6)])
                P.mm(ps(7), kd[hb][:, n, 128:256], v16[hb][:, n, :], True, True, [kdk, vk], [psk(7)])
                P.stt("dve", Rflat, Rflat, cdhs[h], PS67, ALU.mult, ALU.add, [f"Rf{hb}", psk(6), psk(7)], [f"Rf{hb}"])
                P.cp("act", R16[rn][:], Rf[hb][:], [f"Rf{hb}"], [f"R16{rn}"])
            gbk = n % 2
            for k in range(8):
                P.mm(ps(gbk), uTo[:, k, nt], wg[:, k, :], k == 0, k == 7, [f"uT{16 + n}", "wsl2"], [psk(gbk)])
            P.act(sg[i][:], ps(gbk), AF.Silu, [psk(gbk)], [f"sg{i}"])
            P.mm(ps(3, 128), kTr[hb][:, 0, nt], qTr[hb][:, 0, nt], True, False, [kk_, qk_], [psk(3)])
            P.mm(ps(3, 128), kTr[hb][:, 1, nt], qTr[hb][:, 1, nt], False, True, [kk_, qk_], [psk(3)])
            P.tt("dve", pmr[i][:], ps(3, 128), DTh, ALU.mult, [psk(3), "c32"], [f"pmr{i}"])
            P.mm(ps(5), qTr[hb][:, 0, nt], R16[rb][:, 0, :], True, False, [qk_, f"R16{rb}"], [psk(5)])
            P.mm(ps(5), qTr[hb][:, 1, nt], R16[rb][:, 1, :], False, True, [qk_, f"R16{rb}"], [psk(5)])
            P.mm(ps(4), pmr[i][:], v16[hb][:, n, :], True, True, [f"pmr{i}", vk], [psk(4)])
            P.cp("act", yo[i][:], ps(4), [psk(4)], [f"yo{i}"])
            P.stt("dve", yy[i][:], ps(5), qdc, yo[i][:], ALU.mult, ALU.add, [psk(5), f"yo{i}", "c32"], [f"yy{i}"])
            if "yraw" in dbg_t:
                kq = dbgs()
                P.dma("sp", dbg_t["yraw"].ap()[n * 128:(n + 1) * 128, h * 512:(h + 1) * 512], yy[i][:], kq, [f"yy{i}"], [kq])
            rs, k_ = rms_rstd(yy[i][:], 512, [f"yy{i}"], junkc)
            P.stt("dve", yr16[i][:], yy[i][:], rs, sg[i][:], ALU.mult, ALU.mult, [f"yy{i}", k_, f"sg{i}"], [f"yr16{i}"])
            for j in range(4):
                P.tr(psb(2)[:, j * 128:(j + 1) * 128], yr16[i][:, j * 128:(j + 1) * 128], ident,
                     [f"yr16{i}", "c16"], [psk(2)])
            P.cp("act", yst[i][:], psb(2)[:, 0:512].rearrange("p (a b) -> p a b", a=4), [psk(2)], [f"yst{i}"])
            P.dma("sp", yrT_d.ap()[:, h * 4:(h + 1) * 4, nt], yst[i][:], f"yst{i}", [f"yst{i}"], [f"yrT_d{h}_{n}"])

    c_load(0)
    c_load_g(0)
    for h in range(4):
        c_proj(h)
        if h + 1 < 4:
            c_load(h + 1)
        c_passA(h)
        c_combine(h)
        c_passB(h)
        if h + 1 < 4:
            c_load_g(h + 1)
    P.barrier()
    A.pop()
    if "yrT" in dbg_t:
        A.push()
        bnc = A.alloc("bnc", [16, 2048], BF16)
        P.dma("sp", bnc[:], yrT_d.ap(), "bnc", [], ["bnc"])
        kq = dbgs()
        P.dma("pool", dbg_t["yrT"].ap().rearrange("(k p) t -> p k t", p=128), bnc[:], kq, ["bnc"], [kq])
        P.barrier()
        A.pop()
    if stop == "C":
        return finalize()

    A.push()
    yrT = A.alloc("yrT", [16, 2048], BF16)
    yaT = A.alloc("yaTl", [4, 2048], BF16)
    P.dma("sp", yaT[:], yaT_d.ap(), "yaTl", [], ["yaTl"])
    wm = [{"wr": A.alloc(f"wr{i}", [16, 128], BF16), "wdo": A.alloc(f"wdo{i}", [4, 128], BF16),
           "wgr": A.alloc(f"wgr{i}", [8, 128], BF16), "wga": A.alloc(f"wga{i}", [8, 128], BF16)} for i in range(2)]
    sr = [A.alloc(f"sr{i}", [512], F32) for i in range(2)]
    sa = [A.alloc(f"sa{i}", [512], F32) for i in range(2)]
    m1 = [A.alloc(f"m1{i}", [512], F32) for i in range(2)]
    m2 = [A.alloc(f"m2{i}", [512], F32) for i in range(2)]
    mx = [A.alloc(f"mx{i}", [512], BF16) for i in range(2)]
    for q in range(4):
        P.dma("sp", yrT[:, q * 4:(q + 1) * 4, :], yrT_d.ap()[:, q * 4:(q + 1) * 4, :], "yrTl", [], ["yrT"])
    wro_v = w_ret_out.ap().rearrange("(k p) c -> p k c", p=128)
    wdo_v = w_dil_out.ap().rearrange("(k p) c -> p k c", p=128)

    def d1_load(m):
        sl = m % 2
        w = wm[sl]
        st_ = f"wm{sl}"
        cs_ = slice(m * 128, (m + 1) * 128)
        P.dma("pool", w["wr"][:], wro_v[:, :, cs_], st_, [], [st_])
        P.dma("pool", w["wdo"][:], wdo_v[:, :, cs_], st_, [], [st_])
        P.dma("pool", w["wgr"][:], w_in_v[:, :, 10752 + m * 128:10752 + (m + 1) * 128], st_, [], [st_])
        P.dma("pool", w["wga"][:], w_in_v[:, :, 11776 + m * 128:11776 + (m + 1) * 128], st_, [], [st_])

    d1_load(0)
    cnt_d1 = 0
    for m in range(8):
        if m + 1 < 8:
            d1_load(m + 1)
        sl = m % 2
        w = wm[sl]
        wk = f"wm{sl}"
        for b in range(4):
            base = 4 * (cnt_d1 % 2)
            i = cnt_d1 % 2
            cnt_d1 += 1
            bs = slice(b * 512, (b + 1) * 512)
            ukeys = [f"uT{16 + b * 4 + q}" for q in range(4)]
            for k in range(16):
                P.mm(ps(base), w["wr"][:, k, :], yrT[:, k, bs], k == 0, k == 15, [wk, "yrT"], [psk(base)])
            for k in range(4):
                P.mm(ps(base + 1), w["wdo"][:, k, :], yaT[:, k, bs], k == 0, k == 3, [wk, "yaTl"], [psk(base + 1)])
            for k in range(8):
                P.mm(ps(base + 2), w["wgr"][:, k, :], uTo[:, k, bs], k == 0, k == 7, [wk] + ukeys, [psk(base + 2)])
            for k in range(8):
                P.mm(ps(base + 3), w["wga"][:, k, :], uTo[:, k, bs], k == 0, k == 7, [wk] + ukeys, [psk(base + 3)])
            P.act(sr[i][:], ps(base + 2), AF.Sigmoid, [psk(base + 2), "c32"], [f"sr{i}"], bias=c32[:, O_BGT + m:O_BGT + m + 1])
            P.act(sa[i][:], ps(base + 3), AF.Sigmoid, [psk(base + 3), "c32"], [f"sa{i}"], bias=c32[:, O_BGT + 8 + m:O_BGT + 8 + m + 1])
            P.tt("dve", m1[i][:], ps(base), sr[i][:], ALU.mult, [psk(base), f"sr{i}"], [f"m1{i}"])
            P.tt("dve", m2[i][:], ps(base + 1), sa[i][:], ALU.mult, [psk(base + 1), f"sa{i}"], [f"m2{i}"])
            P.tt("pool", mx[i][:], m1[i][:], m2[i][:], ALU.add, [f"m1{i}", f"m2{i}"], [f"mx{i}"])
            P.dma("sp", mixT_d.ap()[:, m, bs], mx[i][:], f"mx{i}", [f"mx{i}"], [f"mixT_d{m}_{b}"])
    P.barrier()
    A.pop()
    A.pop()
    if "mixT" in dbg_t:
        A.push()
        bnc = A.alloc("bnc2", [8, 2048], BF16)
        P.dma("sp", bnc[:], mixT_d.ap(), "bnc2", [], ["bnc2"])
        kq = dbgs()
        P.dma("pool", dbg_t["mixT"].ap().rearrange("(k p) t -> p k t", p=128), bnc[:], kq, ["bnc2"], [kq])
        P.barrier()
        A.pop()
    if stop == "D1":
        return finalize()

    v2T = A.alloc("v2T", [8, 2048], BF16)
    facc = A.alloc("facc", [16, 1024], F32)
    A.push()
    wo = A.alloc("wo", [8, 1024], BF16)
    gpo = A.alloc("gpo", [1024], F32)
    gpm = A.alloc("gpm", [1024], F32)
    mixb = [A.alloc(f"mixb{i}", [8, 512], BF16) for i in range(2)]
    xt = [A.alloc(f"xt{i}", [1024], F32) for i in range(2)]
    tmpd = [A.alloc(f"tmpd{i}", [1024], F32) for i in range(2)]
    h1t = [A.alloc(f"h1t{i}", [1024], F32) for i in range(2)]
    v2t = [A.alloc(f"v2t{i}", [1024], BF16) for i in range(2)]
    junkd = A.alloc("junkd", [1024], BF16)
    P.dma("pool", wo[:], w_o.ap().rearrange("(k p) c -> p k c", p=128), "wo", [], ["wo"])
    P.dma("sp", gpo[:], gbc.ap()[1], "gpo", [], ["gpo"])
    P.dma("sp", gpm[:], gbc.ap()[2], "gpm", [], ["gpm"])
    def d2_loadb(b):
        bi = b % 2
        P.dma("sp", mixb[bi][:], mixT_d.ap()[:, :, b * 512:(b + 1) * 512], f"mixb{bi}", [], [f"mixb{bi}"])

    def d2_front(t):
        b, tt = t // 4, t % 4
        bi = b % 2
        bp = (t % 2) * 2
        if tt == 0 and b + 1 < 4:
            d2_loadb(b + 1)
        i = t % 2
        P.dma("sp", xt[i][:], x_all.ap()[2048 + t * 128:2048 + (t + 1) * 128, :], f"xt{i}", [], [f"xt{i}"])
        for c2 in range(2):
            for k in range(8):
                P.mm(ps(bp + c2), mixb[bi][:, k, tt * 128:(tt + 1) * 128], wo[:, k, c2 * 512:(c2 + 1) * 512],
                     k == 0, k == 7, [f"mixb{bi}", "wo"], [psk(bp + c2)])

    def d2_back(t):
        i = t % 2
        bp = (t % 2) * 2
        PSO = PS[:, bp * 512:bp * 512 + 1024]
        rs, k_ = rms_rstd(PSO, 1024, [psk(bp), psk(bp + 1)], junkd)
        P.stt("dve", tmpd[i][:], PSO, rs, gpo[:], ALU.mult, ALU.mult, [psk(bp), psk(bp + 1), k_, "gpo"], [f"tmpd{i}"])
        P.tt("pool", h1t[i][:], tmpd[i][:], xt[i][:], ALU.add, [f"tmpd{i}", f"xt{i}"], [f"h1t{i}"])
        P.dma("sp", h1_d.ap()[t * 128:(t + 1) * 128, :], h1t[i][:], f"h1w{i}", [f"h1t{i}"], [f"h1_d{t}"])
        rs2, k2 = rms_rstd(h1t[i][:], 1024, [f"h1t{i}"], junkd)
        P.stt("dve", v2t[i][:], h1t[i][:], rs2, gpm[:], ALU.mult, ALU.mult, [f"h1t{i}", k2, "gpm"], [f"v2t{i}"])
        pb4 = 4 + (t % 2)
        for kk in range(8):
            P.tr(psb(pb4)[:, kk * 128:(kk + 1) * 128], v2t[i][:, kk * 128:(kk + 1) * 128], ident, [f"v2t{i}", "c16"], [psk(pb4)])
        P.cp("act", v2T[:, :, t * 128:(t + 1) * 128], psb(pb4).rearrange("p (k c) -> p k c", k=8), [psk(pb4)], [f"v2T{t}"])

    d2_loadb(0)
    d2_front(0)
    for t in range(16):
        if t + 1 < 16:
            d2_front(t + 1)
        d2_back(t)
    P.barrier()
    A.pop()
    if "h1" in dbg_t:
        A.push()
        bnc = A.alloc("bnc3", [1024], F32)
        for t in range(16):
            P.dma("sp", bnc[:], h1_d.ap()[t * 128:(t + 1) * 128, :], "bnc3", [], ["bnc3"])
            kq = dbgs()
            P.dma("sp", dbg_t["h1"].ap()[t * 128:(t + 1) * 128, :], bnc[:], kq, ["bnc3"], [kq])
        P.barrier()
        A.pop()
    if stop == "D2":
        return finalize()

    A.push()
    wpg = A.alloc("wpg", [8, 1024], BF16)
    wpi = A.alloc("wpi", [2, 1024], BF16)
    bple = A.alloc("bple", [1024], F32)
    gl = A.alloc("gl", [1024], F32)
    gq = A.alloc("gq", [1024], F32)
    gpp = A.alloc("gpp", [1024], F32)
    A.push()
    wu = [A.alloc(f"wu{i}", [8, 512], BF16) for i in range(2)]
    wdn = [A.alloc(f"wdn{i}", [4, 1024], BF16) for i in range(2)]
    hid = [A.alloc(f"hid{i}", [4, 512], BF16) for i in range(2)]
    rl = [A.alloc(f"rl{i}", [512], F32) for i in range(2)]
    wup_v = w_up.ap().rearrange("(k p) c -> p k c", p=128)
    P.dma("sp", gq[:], gbc.ap()[3], "gq", [], ["gq"])
    P.dma("sp", gpp[:], gbc.ap()[4], "gpp", [], ["gpp"])
    P.dma("sp", bple[:], bple_bc.ap(), "bple", [], ["bple"])
    P.dma("sp", gl[:], gbc.ap()[5], "gl", [], ["gl"])

    def e_load(j):
        s_ = j % 2
        P.dma("pool", wu[s_][:], wup_v[:, :, j * 512:(j + 1) * 512], f"wu{s_}", [], [f"wu{s_}"])
        P.dma("pool", wdn[s_][:], w_down.ap()[j * 512:(j + 1) * 512, :].rearrange("(c p) n -> p c n", p=128),
              f"wdn{s_}", [], [f"wdn{s_}"])

    e_load(0)
    cnt_e = 0
    for j in range(8):
        if j + 1 < 8:
            e_load(j + 1)
        else:
            P.dma("pool", wpg[:], w_ple_gate.ap().rearrange("(k p) c -> p k c", p=128), "wpg", [], ["wpg"])
            P.dma("pool", wpi[:], w_ple_in.ap().rearrange("(k p) c -> p k c", p=128), "wpi", [], ["wpi"])
        s_ = j % 2
        for b in range(4):
            hi = cnt_e % 2
            cnt_e += 1
            bs = slice(b * 512, (b + 1) * 512)
            vkeys = [f"v2T{b * 4 + q}" for q in range(4)]
            for c in range(4):
                for k in range(8):
                    P.mm(ps(c), wu[s_][:, k, c * 128:(c + 1) * 128], v2T[:, k, bs], k == 0, k == 7, [f"wu{s_}"] + vkeys, [psk(c)])
                ri = c % 2
                P.act(rl[ri][:], ps(c), AF.Relu, [psk(c)], [f"rl{ri}"])
                P.tt("pool", hid[hi][:, c, :], rl[ri][:], rl[ri][:], ALU.mult, [f"rl{ri}"], [f"hid{hi}"])
            for tt in range(4):
                t = b * 4 + tt
                bp = 4 + (tt % 2) * 2
                PSO = PS[:, bp * 512:bp * 512 + 1024]
                for c2 in range(2):
                    for c in range(4):
                        P.mm(ps(bp + c2), hid[hi][:, c, tt * 128:(tt + 1) * 128], wdn[s_][:, c, c2 * 512:(c2 + 1) * 512],
                             c == 0, c == 3, [f"hid{hi}", f"wdn{s_}"], [psk(bp + c2)])
                if j == 0:
                    P.cp("act", facc[:, t, :], PSO, [psk(bp), psk(bp + 1)], [f"facc{t}"])
                else:
                    P.tt("dve", facc[:, t, :], PSO, facc[:, t, :], ALU.add, [psk(bp), psk(bp + 1), f"facc{t}"], [f"facc{t}"])
    P.barrier()
    A.pop()
    if stop == "E":
        return finalize()

    h1r = A.alloc("h1r", [1024], F32)
    tmpe = A.alloc("tmpe", [1024], F32)
    v3t = [A.alloc(f"v3t{i}", [1024], BF16) for i in range(2)]
    ptl = [A.alloc(f"ptl{i}", [256], F32) for i in range(2)]
    p16 = [A.alloc(f"p16{i}", [256], BF16) for i in range(2)]
    pT = [A.alloc(f"pT{i}", [2, 128], BF16) for i in range(2)]
    tg = [A.alloc(f"tg{i}", [1024], F32) for i in range(2)]
    ge = A.alloc("ge", [1024], F32)
    ot = [A.alloc(f"ot{i}", [1024], F32) for i in range(2)]
    junkf = A.alloc("junkf", [1024], BF16)
    outkeys = []

    def e_tail(t):
        i = t % 2
        rs, k_ = rms_rstd(facc[:, t, :], 1024, [f"facc{t}"], junkf)
        P.dma("sp", h1r[:], h1_d.ap()[t * 128:(t + 1) * 128, :], "h1r", [], ["h1r"])
        P.stt("dve", tmpe[:], facc[:, t, :], rs, gq[:], ALU.mult, ALU.mult, [f"facc{t}", k_, "gq"], ["tmpe"])
        P.tt("pool", facc[:, t, :], tmpe[:], h1r[:], ALU.add, ["tmpe", "h1r"], [f"facc{t}"])
        if "h2" in dbg_t:
            kq = dbgs()
            P.dma("sp", dbg_t["h2"].ap()[t * 128:(t + 1) * 128, :], facc[:, t, :], kq, [f"facc{t}"], [kq])
        rs2, k2 = rms_rstd(facc[:, t, :], 1024, [f"facc{t}"], junkf)
        P.stt("dve", v3t[i][:], facc[:, t, :], rs2, gpp[:], ALU.mult, ALU.mult, [f"facc{t}", k2, "gpp"], [f"v3t{i}"])
        for kk in range(8):
            P.tr(psb(7)[:, kk * 128:(kk + 1) * 128], v3t[i][:, kk * 128:(kk + 1) * 128], ident, [f"v3t{i}", "c16"], [psk(7)])
        P.cp("act", v2T[:, :, t * 128:(t + 1) * 128], psb(7).rearrange("p (k c) -> p k c", k=8), [psk(7)], [f"v2T{t}"])
        tsl = slice(t * 128, (t + 1) * 128)
        P.dma("sp", ptl[i][:], p_own.ap()[tsl, :], f"ptl{i}", [], [f"ptl{i}"])
        P.cp("dve", p16[i][:], ptl[i][:], [f"ptl{i}"], [f"p16{i}"])
        for kk in range(2):
            P.tr(psb(6)[:, kk * 128:(kk + 1) * 128], p16[i][:, kk * 128:(kk + 1) * 128], ident, [f"p16{i}", "c16"], [psk(6)])
        P.cp("act", pT[i][:], psb(6)[:, 0:256].rearrange("p (a b) -> p a b", a=2), [psk(6)], [f"pT{i}"])

    def f_main(t):
        i = t % 2
        tsl = slice(t * 128, (t + 1) * 128)
        bp = 0 if t % 2 == 0 else 4
        PSG = PS[:, bp * 512:bp * 512 + 1024]
        PSE = PS[:, 2 * 512:2 * 512 + 1024]
        for c2 in range(2):
            for k in range(8):
                P.mm(ps(bp + c2), v2T[:, k, tsl], wpg[:, k, c2 * 512:(c2 + 1) * 512], k == 0, k == 7, [f"v2T{t}", "wpg"], [psk(bp + c2)])
        for c2 in range(2):
            for k in range(2):
                P.mm(ps(2 + c2), pT[i][:, k, :], wpi[:, k, c2 * 512:(c2 + 1) * 512], k == 0, k == 1, [f"pT{i}", "wpi"], [psk(2 + c2)])
        P.tt("dve", tg[i][:], PSG, bple[:], ALU.add, [psk(bp), psk(bp + 1), "bple"], [f"tg{i}"])
        P.act(tg[i][:], tg[i][:], AF.Sigmoid, [f"tg{i}"], [f"tg{i}"])
        P.tt("dve", ge[:], PSE, tg[i][:], ALU.mult, [psk(2), psk(3), f"tg{i}"], ["ge"])
        rs, k_ = rms_rstd(ge[:], 1024, ["ge"], junkf)
        P.stt("dve", tg[i][:], ge[:], rs, gl[:], ALU.mult, ALU.mult, ["ge", k_, "gl", f"tg{i}"], [f"tg{i}"])
        P.tt("pool", ot[i][:], tg[i][:], facc[:, t, :], ALU.add, [f"tg{i}", f"facc{t}"], [f"ot{i}"])
        ok_ = f"outw{t}"
        outkeys.append(ok_)
        P.dma("sp", out.ap()[tsl, :], ot[i][:], f"ot{i}", [f"ot{i}"], [ok_])

    e_tail(0)
    for t in range(16):
        if t + 1 < 16:
            e_tail(t + 1)
        f_main(t)
    dbgkeys.extend(outkeys)
    A.pop()
    return finalize()


def _consts(core):
    c = np.zeros((128, NCST), np.float32)
    half = 128
    inv_freq = (1.0 / (np.float32(10000.0) ** np.linspace(0.0, 1.0, half, dtype=np.float32))).astype(np.float32)
    c[:, O_INVF:O_INVF + 128] = inv_freq[None, :]
    H = 4
    lg = np.log1p(-(2.0 ** (-5.0 - np.arange(H, dtype=np.float64))))
    idx = np.arange(128, dtype=np.float64)
    for h in range(H):
        diff = idx[None, :] - idx[:, None]
        DT = np.where(diff >= 0, np.exp(np.maximum(diff, 0.0) * lg[h]), 0.0) * (256.0 ** -0.5)
        c[:, O_DT + h * 128:O_DT + (h + 1) * 128] = DT.astype(np.float32)
        c[:, O_KSC + h] = (np.exp((127.0 - idx) * lg[h]) * (256.0 ** -0.5)).astype(np.float32)
        c[:, O_QDEC + h] = np.exp((idx + 1.0) * lg[h]).astype(np.float32)
        for r in range(8):
            if r < core:
                c[:, O_COEF + r * 4 + h] = np.float32(np.exp(128.0 * 16.0 * (core - 1 - r) * lg[h]))
    freqs = (np.float32(500000.0) ** (-np.arange(0, 16, 2, dtype=np.float32) / np.float32(16))).astype(np.float32)
    for p in range(128):
        d = p % 64
        if d < 16:
            c[p, O_FREQD] = freqs[d % 8]
            c[p, O_SSIGN] = -1.0 if d < 8 else 1.0
    return c


def _consts16(core):
    c = np.zeros((128, NC16), np.float32)
    c[:, O_ID:O_ID + 128] = np.eye(128, dtype=np.float32)
    perm = np.zeros((128, 128), np.float32)
    for m in range(128):
        d = m % 64
        if d < 8:
            perm[m + 8, m] = 1.0
        elif d < 16:
            perm[m - 8, m] = 1.0
    c[:, O_PERM:O_PERM + 128] = perm
    j = np.arange(128)[:, None]
    i = np.arange(128)[None, :]
    mprev = (j >= i).astype(np.float32)
    mcur = (j <= i).astype(np.float32)
    c[:, O_MK:O_MK + 128] = mprev
    c[:, O_MK + 128:O_MK + 256] = mcur
    c[:, O_MK0:O_MK0 + 128] = mprev if core > 0 else 0.0
    c[:, O_MK0 + 128:O_MK0 + 256] = mcur
    c[:, O_ONES:O_ONES + 64] = 1.0
    c[:, O_ONES + 128 + 64:O_ONES + 256] = 1.0
    return c


def make_in_maps(inputs):
    x = np.asarray(inputs["x"], np.float32)[0]
    p = np.asarray(inputs["p"], np.float32)[0, 0]
    pos = np.asarray(inputs["positions"], np.int32)[0]
    sq = lambda n: np.ascontiguousarray(np.asarray(inputs[n], np.float32)[0])
    shared = {
        "w_in": sq("w_in"), "w_ret_out": sq("w_ret_out"), "w_dil_out": sq("w_dil_out"), "w_o": sq("w_o"),
        "w_up": sq("w_up"), "w_down": sq("w_down"), "w_ple_gate": sq("w_ple_gate"), "w_ple_in": sq("w_ple_in"),
    }
    gnames = ["g_pre_mix", "g_post_mix", "g_pre_mlp", "g_post_mlp", "g_pre_ple", "g_post_ple"]
    gb = np.stack([np.broadcast_to(np.asarray(inputs[n], np.float32)[0][None, :], (128, 1024)) for n in gnames], 0)
    shared["gbc"] = np.ascontiguousarray(gb)
    shared["bple_bc"] = np.ascontiguousarray(np.broadcast_to(np.asarray(inputs["b_ple_gate"], np.float32)[0][None, :], (128, 1024)))
    bg = np.asarray(inputs["b_gate"], np.float32)[0]
    bgT = bg.reshape(2, 8, 128).transpose(2, 0, 1).reshape(128, 16)
    maps = []
    for c in range(NCORES):
        lo = c * TOK
        xa = np.zeros((4096, 1024), np.float32)
        pa = np.zeros((4096,), np.int32)
        if c > 0:
            xa[:2048] = x[lo - 2048:lo]
            pa[:2048] = pos[lo - 2048:lo]
        xa[2048:] = x[lo:lo + TOK]
        pa[2048:] = pos[lo:lo + TOK]
        cs = _consts(c)
        cs[:, O_BGT:O_BGT + 16] = bgT
        m = dict(shared)
        m.update({
            "x_all": xa,
            "p_own": np.ascontiguousarray(p[lo:lo + TOK]),
            "pos_bc": np.ascontiguousarray(np.broadcast_to(pa[None, :], (128, 4096))),
            "pos_tm": np.ascontiguousarray(pa[2048:].reshape(16, 128).T),
            "cst": cs,
            "cst16": _consts16(c),
        })
        maps.append(m)
    return maps


_NC_CACHE = {}


def kernel(**inputs):
    if "nc" not in _NC_CACHE:
        _NC_CACHE["nc"] = build()
    nc = _NC_CACHE["nc"]
    maps = make_in_maps(inputs)
    res = run_bass_kernel_spmd(nc, maps, core_ids=list(range(NCORES)))
    outs = [np.asarray(r["out"], np.float32) for r in res.results]
    return np.concatenate(outs, 0)[None]
```

```python
import math
import numpy as np
import concourse.bass as bass
import concourse.mybir as mb
from concourse.bass_utils import run_bass_kernel_spmd

F32 = mb.dt.float32
BF16 = mb.dt.bfloat16
I32 = mb.dt.int32
AF = mb.ActivationFunctionType
ALU = mb.AluOpType

NCORES = 8
TOK = 2048
EPS = 1e-6
SAME_ENG_SYNC = True
import os
ROPE_ENG = os.environ.get('ROPE_ENG', 'pool')
RSTOP = os.environ.get('RSTOP', '')
ARENA_BASE = 16640
ARENA_END = 229376

INV2PI = float(np.float32(1.0 / (2.0 * math.pi)))
C1 = 6.28125
C2 = 2.0 * math.pi - 6.28125
PI = math.pi
PIC = 3.14159
HALFPI = math.pi / 2.0
TWOPI = 2.0 * math.pi

O_INVF = 0
O_DT = 128
O_KSC = 640
O_QDEC = 644
O_COEF = 648
O_FREQD = 680
O_SSIGN = 681
O_BGT = 682
NCST = 704
O_ID = 0
O_PERM = 128
O_MK = 256
O_MK0 = 512
O_ONES = 768
NC16 = 1024


class Op:
    __slots__ = ("eng", "fn", "r", "w", "stream", "inc", "epoch", "sig", "val", "deps")


class Prog:
    def __init__(self):
        self.ops = []
        self.epoch = 0

    def add(self, eng, fn, r=(), w=(), stream=None, inc=16):
        op = Op()
        op.eng = eng
        op.fn = fn
        r = tuple(r)
        w = tuple(w)
        if eng != "pe":
            extra = tuple(k for k in r if k.startswith("ps") and k[2:].isdigit() and k not in w)
            w = w + extra
        op.r = r
        op.w = w
        op.stream = stream
        op.inc = inc
        op.epoch = self.epoch
        op.sig = stream is not None
        op.val = 0
        op.deps = ()
        self.ops.append(op)
        return op

    def barrier(self):
        self.epoch += 1

    def mm(self, out, lhsT, rhs, start, stop, r, w):
        self.add("pe", lambda e: e.matmul(out, lhsT, rhs, start=start, stop=stop), r, w)

    def tr(self, out, in_, ident, r, w):
        self.add("pe", lambda e: e.transpose(out, in_, ident), r, w)

    def act(self, out, in_, func, r, w, bias=None, scale=None, accum=None):
        kw = {}
        if bias is not None:
            kw["bias"] = bias
        if scale is not None:
            kw["scale"] = scale
        if accum is not None:
            kw["accum_out"] = accum
        self.add("act", lambda e: e.activation(out, in_, func, **kw), r, w)

    def tt(self, eng, out, in0, in1, op, r, w):
        self.add(eng, lambda e: e.tensor_tensor(out, in0, in1, op), r, w)

    def stt(self, eng, out, in0, scalar, in1, op0, op1, r, w):
        self.add(eng, lambda e: e.scalar_tensor_tensor(out, in0, scalar, in1, op0, op1), r, w)

    def ts(self, eng, out, in0, s1, s2, op0, op1, r, w):
        if op1 is None:
            self.add(eng, lambda e: e.tensor_scalar(out, in0, s1, None, op0), r, w)
        else:
            self.add(eng, lambda e: e.tensor_scalar(out, in0, s1, s2, op0, op1), r, w)

    def cp(self, eng, out, in_, r, w):
        if eng == "act":
            self.add(eng, lambda e: e.copy(out, in_), r, w)
        else:
            self.add(eng, lambda e: e.tensor_copy(out, in_), r, w)

    def dma(self, eng, out, in_, stream, r, w):
        self.add(eng, lambda e: e.dma_start(out=out, in_=in_), r, w, stream=stream)

    def analyse(self):
        ops = self.ops
        last_w, readers, last_by_prod = {}, {}, {}
        bar_deps = frozenset()
        cur_epoch = 0
        for i, op in enumerate(ops):
            if op.epoch != cur_epoch:
                cur_epoch = op.epoch
                bar_deps = frozenset(last_by_prod.values())
                last_w, readers = {}, {}
            deps = set(bar_deps)
            for k in op.r:
                j = last_w.get(k)
                if j is not None:
                    deps.add(j)
            for k in op.w:
                j = last_w.get(k)
                rd = readers.get(k)
                if j is not None:
                    if not (op.stream is not None and ops[j].stream == op.stream and not rd):
                        deps.add(j)
                if rd:
                    deps.update(rd.values())
            pid = (op.stream, i) if op.stream else op.eng
            for k in op.r:
                readers.setdefault(k, {})[pid] = i
            for k in op.w:
                last_w[k] = i
                readers[k] = {}
            last_by_prod[op.stream or op.eng] = i
            fdeps = []
            for j in deps:
                pj = ops[j]
                if pj.stream is None and op.stream is None and pj.eng == op.eng:
                    if op.eng == "pe" or not SAME_ENG_SYNC:
                        continue
                fdeps.append(j)
                pj.sig = True
            op.deps = fdeps
        cnt = {}
        for op in ops:
            if op.stream:
                cnt[op.stream] = cnt.get(op.stream, 0) + (op.inc if op.inc else 1)
                op.val = cnt[op.stream]
            elif op.sig:
                cnt[op.eng] = cnt.get(op.eng, 0) + 1
                op.val = cnt[op.eng]
        return cnt

    def emit(self, nc):
        import contextlib
        cnt = self.analyse()
        ops = self.ops
        names = sorted(cnt.keys())
        with contextlib.ExitStack() as st:
            sems = {n: st.enter_context(nc.semaphore("s_" + n)) for n in names}
            block = st.enter_context(nc.Block())

            def run(engname):
                def body(e):
                    waited = {}
                    for op in ops:
                        if op.eng != engname:
                            continue
                        need = {}
                        for j in op.deps:
                            pj = ops[j]
                            key = pj.stream or pj.eng
                            if pj.val > need.get(key, 0):
                                need[key] = pj.val
                        for key, val in need.items():
                            if waited.get(key, 0) < val:
                                e.wait_ge(sems[key], val)
                                waited[key] = val
                        if op.fn is None:
                            continue
                        ins = op.fn(e)
                        if op.stream:
                            if op.inc:
                                ins.then_inc(sems[op.stream], op.inc)
                            else:
                                ins.then_inc(sems[op.stream])
                        elif op.sig:
                            ins.then_inc(sems[op.eng], 1)
                return body

            block.tensor(run("pe"))
            block.scalar(run("act"))
            block.vector(run("dve"))
            block.gpsimd(run("pool"))
            block.sync(run("sp"))


def bcast(ap, axis, n):
    dims = [list(d) for d in ap.ap]
    dims.insert(axis, [0, n])
    return bass.AP(ap.tensor, ap.offset, dims)


def build(dbg=(), stop=None, fake_ag=False):
    nc = bass.Bass("TRN2", target_bir_lowering=False)
    P = Prog()

    def din(name, shape, dt=F32):
        return nc.dram_tensor(name, list(shape), dt, kind="ExternalInput")

    dbgkeys = []

    def dbgs():
        k = f"dbg{len(dbgkeys)}"
        dbgkeys.append(k)
        return k

    def finalize():
        P.add("sp", None, list(dbgkeys), [])
        P.emit(nc)
        return nc

    x_all = din("x_all", [4096, 1024])
    p_own = din("p_own", [2048, 256])
    pos_bc = din("pos_bc", [128, 4096], I32)
    pos_tm = din("pos_tm", [128, 16], I32)
    w_in = din("w_in", [1024, 12800])
    w_ret_out = din("w_ret_out", [2048, 1024])
    w_dil_out = din("w_dil_out", [512, 1024])
    w_o = din("w_o", [1024, 1024])
    w_up = din("w_up", [1024, 4096])
    w_down = din("w_down", [4096, 1024])
    w_ple_gate = din("w_ple_gate", [1024, 1024])
    w_ple_in = din("w_ple_in", [256, 1024])
    gbc = din("gbc", [6, 128, 1024])
    bple_bc = din("bple_bc", [128, 1024])
    cst = din("cst", [128, NCST])
    cst16 = din("cst16", [128, NC16])
    out = nc.dram_tensor("out", [2048, 1024], F32, kind="ExternalOutput")
    dbg_t = {}
    for name, shape in dbg:
        dbg_t[name] = nc.dram_tensor("dbg_" + name, list(shape), F32, kind="ExternalOutput")

    tblC = nc.dram_tensor("tblC", [128, 4096], F32)
    tblS = nc.dram_tensor("tblS", [128, 4096], F32)
    yrT_d = nc.dram_tensor("yrT_d", [128, 16, 2048], BF16)
    yaT_d = nc.dram_tensor("yaT_d", [128, 4, 2048], BF16)
    mixT_d = nc.dram_tensor("mixT_d", [128, 8, 2048], BF16)
    h1_d = nc.dram_tensor("h1_d", [2048, 1024], F32)
    loc_d = [nc.dram_tensor(f"loc{h}", [128, 1024], F32) for h in range(4)]
    gath_d = [nc.dram_tensor(f"gath{h}", [1024, 1024], F32) for h in range(4)]

    w_in_v = w_in.ap().rearrange("(k p) c -> p k c", p=128)

    class Arena:
        def __init__(s):
            s.cur = ARENA_BASE
            s.stack = []
            s.n = 0

        def alloc(s, name, fshape, dt):
            nbytes = int(np.prod(fshape)) * mb.dt.size(dt)
            off = s.cur
            s.cur += (nbytes + 31) // 32 * 32
            assert s.cur <= ARENA_END, (name, s.cur)
            s.n += 1
            return nc.alloc_sbuf_tensor_at(f"{name}_{s.n}", [128] + list(fshape), dt, offset=off)

        def push(s):
            s.stack.append(s.cur)

        def pop(s):
            s.cur = s.stack.pop()

    A = Arena()
    PS = nc.alloc_psum_tensor("ps", [128, 4096], F32)
    PSB = PS.bitcast(BF16)

    def ps(b, n=512, off=0):
        return PS[:, b * 512 + off: b * 512 + off + n]

    def psb(b, n=1024, off=0):
        return PSB[:, b * 1024 + off: b * 1024 + off + n]

    def psk(b):
        return f"ps{b}"

    c32 = A.alloc("c32", [NCST], F32)
    c16 = A.alloc("c16", [NC16], BF16)
    epst = A.alloc("eps", [8], F32)
    stat = A.alloc("stat", [3, 16], F32)
    wsl = [A.alloc(f"wsl{i}", [8, 512], BF16) for i in range(3)]
    A.push()
    uTo = A.alloc("uTo", [8, 2048], BF16)
    cos_r = A.alloc("cos_r", [16, 128], F32)
    sin_r = A.alloc("sin_r", [16, 128], F32)

    P.dma("sp", c32[:], cst.ap(), "c32", [], ["c32"])
    P.dma("pool", c16[:], cst16.ap(), "c16", [], ["c16"])
    P.add("dve", lambda e: e.memset(epst[:], EPS), [], ["eps"])

    invf = c32[:, O_INVF:O_INVF + 128]
    ident = c16[:, O_ID:O_ID + 128]
    permT = c16[:, O_PERM:O_PERM + 128]
    freq_d = c32[:, O_FREQD:O_FREQD + 1]
    ssign = c32[:, O_SSIGN:O_SSIGN + 1]

    stat_ctr = [0]

    def rms_rstd(src, n, rkeys, junk):
        i = stat_ctr[0] % 16
        stat_ctr[0] += 1
        k = f"st{i}"
        ss, sq, rs = stat[:, 0, i:i + 1], stat[:, 1, i:i + 1], stat[:, 2, i:i + 1]
        P.add("dve", lambda e: e.memset(ss, 0.0), [], [k])
        P.act(junk[:, 0:n], src, AF.Square, list(rkeys) + [k], ["junk", k], accum=ss)
        P.act(sq, ss, AF.Sqrt, [k, "eps"], [k], bias=epst[:, 0:1], scale=1.0 / n)
        P.add("dve", lambda e: e.reciprocal(rs, sq), [k], [k])
        return rs, k

    def sincos(ang, n, T, sin_out, cos_out, kin, kout, sin_scale=None):
        e = "dve"
        kt = "sctmp"
        P.ts(e, T["ki"][:, 0:n], ang, INV2PI, None, ALU.mult, None, [kin], [kt])
        P.cp(e, T["kf"][:, 0:n], T["ki"][:, 0:n], [kt], [kt])
        r_, rc, tf, kf = T["r"][:, 0:n], T["rc"][:, 0:n], T["tf"][:, 0:n], T["kf"][:, 0:n]
        P.stt(e, r_, kf, -C1, ang, ALU.mult, ALU.add, [kt, kin], [kt])
        P.stt(e, r_, kf, -C2, r_, ALU.mult, ALU.add, [kt], [kt])
        P.ts(e, tf, r_, PI, None, ALU.is_gt, None, [kt], [kt])
        P.stt(e, r_, tf, -TWOPI, r_, ALU.mult, ALU.add, [kt], [kt])
        P.ts(e, rc, r_, HALFPI, None, ALU.add, None, [kt], [kt])
        P.ts(e, tf, rc, PI, None, ALU.is_gt, None, [kt], [kt])
        P.stt(e, rc, tf, -TWOPI, rc, ALU.mult, ALU.add, [kt], [kt])
        P.ts(e, r_, r_, -PIC, PIC, ALU.max, ALU.min, [kt], [kt])
        P.ts(e, rc, rc, -PIC, PIC, ALU.max, ALU.min, [kt], [kt])
        if sin_scale is not None:
            P.act(sin_out, r_, AF.Sin, [kt, "c32"], [kout], scale=sin_scale)
        else:
            P.act(sin_out, r_, AF.Sin, [kt], [kout])
        P.act(cos_out, rc, AF.Sin, [kt], [kout])

    def dbg_dump(name, src_ap, rkeys, dst_ap=None):
        if name in dbg_t:
            d = dbg_t[name].ap() if dst_ap is None else dst_ap
            k = dbgs()
            P.dma("sp", d, src_ap, k, rkeys, [k])

    A.push()
    uTh = A.alloc("uTh", [8, 2048], BF16)
    yaT = A.alloc("yaT", [4, 2048], BF16)
    tbs = [A.alloc(f"tbs{i}", [2, 512], F32) for i in range(2)]
    A.push()
    xsl = [A.alloc(f"xs{i}", [1024], F32) for i in range(3)]
    u16 = [A.alloc(f"u16{i}", [1024], BF16) for i in range(2)]
    junk = A.alloc("junk", [1024], BF16)
    g0 = A.alloc("g0", [1024], F32)
    T = {"ki": A.alloc("ki", [1024], I32), "kf": A.alloc("kf", [1024], F32), "r": A.alloc("r", [1024], F32),
         "rc": A.alloc("rc", [1024], F32), "tf": A.alloc("tf", [1024], F32)}
    angb = A.alloc("angb", [1024], F32)
    posi = A.alloc("posi", [1024], I32)
    csl = A.alloc("csl", [2, 1024], F32)
    ptm_i = A.alloc("ptm_i", [16], I32)
    ptm_f = A.alloc("ptm_f", [16], F32)

    P.dma("sp", g0[:], gbc.ap()[0], "g0", [], ["g0"])
    for t in range(32):
        s3, s2 = t % 3, t % 2
        xs, u = xsl[s3], u16[s2]
        P.dma("sp", xs[:], x_all.ap()[t * 128:(t + 1) * 128, :], f"xs{s3}", [], [f"xs{s3}"])
        rs, k = rms_rstd(xs[:], 1024, [f"xs{s3}"], junk)
        P.stt("dve", u[:], xs[:], rs, g0[:], ALU.mult, ALU.mult, [f"xs{s3}", k, "g0"], [f"u16{s2}"])
        for kk in range(8):
            P.tr(psb(s2)[:, kk * 128:(kk + 1) * 128], u[:, kk * 128:(kk + 1) * 128], ident,
                 [f"u16{s2}", "c16"], [psk(s2)])
        dst = (uTh if t < 16 else uTo)[:, :, (t % 16) * 128:(t % 16 + 1) * 128]
        P.cp("act" if t % 2 else "dve", dst, psb(s2).rearrange("p (k c) -> p k c", k=8), [psk(s2)], [f"uT{t}"])

    for c in range(4):
        P.dma("sp", posi[:], pos_bc.ap()[:, c * 1024:(c + 1) * 1024], "posi", [], ["posi"])
        P.cp("dve", angb[:], posi[:], ["posi"], ["angb"])
        P.ts("dve", angb[:], angb[:], freq_d, None, ALU.mult, None, ["angb", "c32"], ["angb"])
        sincos(angb[:], 1024, T, csl[:, 1, :], csl[:, 0, :], "angb", "csl", sin_scale=ssign)
        P.dma("sp", tblC.ap()[:, c * 1024:(c + 1) * 1024], csl[:, 0, :], "tblCw", ["csl"], ["tblC"])
        P.dma("sp", tblS.ap()[:, c * 1024:(c + 1) * 1024], csl[:, 1, :], "tblSw", ["csl"], ["tblS"])
    P.dma("sp", ptm_i[:], pos_tm.ap(), "ptm", [], ["ptm_i"])
    P.cp("dve", ptm_f[:], ptm_i[:], ["ptm_i"], ["ptm_f"])
    for c in range(2):
        for t in range(8):
            P.ts("dve", angb[:, t * 128:(t + 1) * 128], invf, ptm_f[:, c * 8 + t:c * 8 + t + 1], None, ALU.mult, None,
                 ["c32", "ptm_f"], ["angb"])
        sincos(angb[:], 1024, T, sin_r[:, c * 8:(c + 1) * 8, :].rearrange("p a b -> p (a b)"),
               cos_r[:, c * 8:(c + 1) * 8, :].rearrange("p a b -> p (a b)"), "angb", "csr")
    if "uT" in dbg_t:
        dd = dbg_t["uT"].ap().rearrange("(k p) t -> p k t", p=128)
        k = dbgs()
        P.dma("pool", dd[:, :, 0:2048], uTh[:], k, [f"uT{t}" for t in range(16)], [k])
        k = dbgs()
        P.dma("pool", dd[:, :, 2048:4096], uTo[:], k, [f"uT{t}" for t in range(16, 32)], [k])
    dbg_dump("cosr", cos_r[:].rearrange("p a b -> p (a b)"), ["csr"])
    dbg_dump("sinr", sin_r[:].rearrange("p a b -> p (a b)"), ["csr"])
    if stop == "A":
        return finalize()
    P.barrier()
    A.pop()

    qTp = A.alloc("qTp", [2, 2048], BF16)
    kT = A.alloc("kT", [4096], BF16)
    vT = A.alloc("vT", [2048], BF16)
    vtm = A.alloc("vtm", [32, 128], BF16)
    ones16 = A.alloc("ones16", [128], BF16)
    acc = A.alloc("acc", [2, 2048], F32)
    x16 = [A.alloc(f"x16{i}", [512], BF16) for i in range(2)]
    t1 = [A.alloc(f"t1{i}", [512], F32) for i in range(2)]
    t2 = [A.alloc(f"t2{i}", [512], F32) for i in range(2)]
    pexp = [A.alloc(f"pexp{i}", [512], BF16) for i in range(2)]
    pmk = [A.alloc(f"pmk{i}", [512], BF16) for i in range(2)]
    P.add("pool", lambda e: e.memset(qTp[:], 0.0), [], ["qT"])
    P.add("pool", lambda e: e.memset(ones16[:], 1.0), [], ["ones16"])
    mk = c16[:, O_MK:O_MK + 256]
    mk0 = c16[:, O_MK0:O_MK0 + 256]

    ctr = {"pb": 0, "rope": 0, "sc": 0, "pu": 0, "ws": 0}

    def utok(k, tok0, n):
        if tok0 < 2048:
            assert tok0 + n <= 2048
            return uTh[:, k, tok0:tok0 + n], [f"uT{t}" for t in range(tok0 // 128, (tok0 + n + 127) // 128)]
        o = tok0 - 2048
        return uTo[:, k, o:o + n], [f"uT{16 + t}" for t in range(o // 128, (o + n + 127) // 128)]

    def proj_fm(wcols, wkey, tok0, n):
        bank = ctr["pb"] % 2
        ctr["pb"] += 1
        for k in range(8):
            rhs, rk = utok(k, tok0, n)
            P.mm(ps(bank, n), wcols[:, k, :], rhs, k == 0, k == 7, [wkey] + rk, [psk(bank)])
        return bank

    def rope_fm(bank, tok0, n, dst, dkey):
        i = ctr["rope"] % 2
        ctr["rope"] += 1
        tb = tbs[i]
        P.dma("sp", tb[:, 0, 0:n], tblC.ap()[:, tok0:tok0 + n], f"tbs{i}", ["tblC"], [f"tbs{i}"])
        P.dma("sp", tb[:, 1, 0:n], tblS.ap()[:, tok0:tok0 + n], f"tbs{i}", ["tblS"], [f"tbs{i}"])
        if RSTOP == "R1":
            return
        P.cp("act", x16[i][:, 0:n], ps(bank, n), [psk(bank)], [f"x16{i}"])
        if RSTOP == "R2":
            return
        P.mm(ps(2, n), permT, x16[i][:, 0:n], True, True, ["c16", f"x16{i}"], [psk(2)])
        if RSTOP == "R3":
            return
        if RSTOP == "E1":
            P.tt("dve", t1[i][:, 0:n], t2[i][:, 0:n], tb[:, 0, 0:n], ALU.mult, [psk(bank), f"tbs{i}"], [f"t1{i}"])
            return
        if RSTOP == "E4":
            P.cp("dve", t1[i][:, 0:n], ps(bank, n), [psk(bank), f"tbs{i}"], [f"t1{i}"])
            return
        if RSTOP == "E5":
            P.cp("dve", t1[i][:, 0:n], ps(bank, n), [psk(bank)], [f"t1{i}"])
            return
        P.tt("dve", t1[i][:, 0:n], ps(bank, n), tb[:, 0, 0:n], ALU.mult, [psk(bank), f"tbs{i}"], [f"t1{i}"])
        if RSTOP == "R4":
            return
        P.tt("dve", t2[i][:, 0:n], ps(2, n), tb[:, 1, 0:n], ALU.mult, [psk(2), f"tbs{i}"], [f"t2{i}"])
        if RSTOP == "R5":
            return
        if dst is None:
            o = tok0 - 2048
            P.tt(ROPE_ENG, qTp[0:64, 0, o:o + n], t1[i][0:64, 0:n], t2[i][0:64, 0:n], ALU.add, [f"t1{i}", f"t2{i}"], [dkey])
            P.tt(ROPE_ENG, qTp[64:128, 1, o:o + n], t1[i][64:128, 0:n], t2[i][64:128, 0:n], ALU.add, [f"t1{i}", f"t2{i}"], [dkey])
        else:
            P.tt(ROPE_ENG, dst, t1[i][:, 0:n], t2[i][:, 0:n], ALU.add, [f"t1{i}", f"t2{i}"], [dkey])

    for sp_ in range(4):
        for g in range(3):
            Dg = (1, 4, 16)[g]
            nbl = 16 // Dg
            halo = 128 * Dg
            start = 2048 - halo
            slot = ctr["ws"] % 2
            ctr["ws"] += 1
            wd = wsl[slot]
            wkey = f"wsl{slot}"
            for part, base in enumerate((6144, 7680, 9216)):
                col0 = base + (g * 8 + 2 * sp_) * 64
                P.dma("pool", wd[:, :, part * 128:(part + 1) * 128], w_in_v[:, :, col0:col0 + 128], wkey, [], [wkey])
            if stop == "B0":
                return finalize()
            for b in range(4):
                bank = proj_fm(wd[:, :, 0:128], wkey, 2048 + b * 512, 512)
                if stop == "B0b":
                    return finalize()
                rope_fm(bank, 2048 + b * 512, 512, None, "qT")
            if stop == "B1":
                return finalize()
            blocks = [(start, min(512, halo))] if halo <= 512 else [(i * 512, 512) for i in range(4)]
            blocks += [(2048 + i * 512, 512) for i in range(4)]
            for (tok0, n) in blocks:
                bank = proj_fm(wd[:, :, 128:256], wkey, tok0, n)
                rope_fm(bank, tok0, n, kT[:, tok0:tok0 + n], "kT")
            if stop == "B2":
                return finalize()
            for half in range(2):
                hb = [bk for bk in blocks if (bk[0] < 2048) == (half == 0)]
                for (tok0, n) in hb:
                    bank = proj_fm(wd[:, :, 256:384], wkey, tok0, n)
                    o = tok0 - 2048 * half
                    P.cp("act", vT[:, o:o + n], ps(bank, n), [psk(bank)], ["vT"])
                if half == 0:
                    tiles = [(c, 2048 - 128 * Dg + c) for c in range(Dg)]
                    idx0 = 0
                else:
                    tiles = [(j * Dg + c, (j - 1) * 128 * Dg + c) for j in range(1, nbl + 1) for c in range(Dg)]
                    idx0 = Dg
                for q0 in range(0, len(tiles), 8):
                    grp = tiles[q0:q0 + 8]
                    for q, (ti, s0) in enumerate(grp):
                        P.tr(psb(7)[:, q * 128:(q + 1) * 128], vT[:, s0:s0 + 127 * Dg + 1:Dg], ident,
                             ["vT", "c16"], [psk(7)])
                    i0 = grp[0][0]
                    ng = len(grp)
                    src = psb(7).rearrange("p (a b) -> p a b", b=128)
                    P.cp("act" if (q0 // 8) % 2 == 0 else "dve", vtm[:, i0:i0 + ng, :], src[:, 0:ng, :], [psk(7)], ["vtm"])
            if stop == "B3":
                return finalize()
            blks = [(j, c) for j in range(1, nbl + 1) for c in range(Dg)]

            def att_front(j, c):
                q0 = (j - 1) * 128 * Dg + c
                qsl = slice(q0, q0 + 127 * Dg + 1, Dg)
                kprev0 = (2048 - 128 * Dg + c) if j == 1 else (2048 + (j - 2) * 128 * Dg + c)
                kcur0 = 2048 + q0
                ksl = [slice(kprev0, kprev0 + 127 * Dg + 1, Dg), slice(kcur0, kcur0 + 127 * Dg + 1, Dg)]
                sb = 3 + ctr["sc"] % 2
                si = ctr["sc"] % 2
                ctr["sc"] += 1
                for kb in range(2):
                    P.mm(ps(sb, 256, kb * 256), kT[:, ksl[kb]], qTp[:, :, qsl], True, True, ["kT", "qT"], [psk(sb)])
                P.act(pexp[si][:], ps(sb), AF.Exp, [psk(sb)], [f"pexp{si}"], scale=0.125)
                msk = mk0 if j == 1 else mk
                m4 = bcast(msk.rearrange("p (a b) -> p a b", a=2), 2, 2)
                P.tt("dve", pmk[si][:].rearrange("p (a s b) -> p a s b", a=2, s=2),
                     pexp[si][:].rearrange("p (a s b) -> p a s b", a=2, s=2), m4, ALU.mult,
                     [f"pexp{si}", "c16"], [f"pmk{si}"])
                return si, qsl

            def att_back(j, c, si, qsl):
                tidx = [(j - 1) * Dg + c, j * Dg + c]
                ub = 5 + ctr["pu"] % 2
                ctr["pu"] += 1
                for kb in range(2):
                    P.mm(ps(ub, 256, 0), vtm[:, tidx[kb], :], pmk[si][:, kb * 256:(kb + 1) * 256],
                         kb == 0, kb == 1, ["vtm", f"pmk{si}"], [psk(ub)])
                for kb in range(2):
                    P.mm(ps(ub, 256, 256), ones16[:], pmk[si][:, kb * 256:(kb + 1) * 256],
                         kb == 0, kb == 1, ["ones16", f"pmk{si}"], [psk(ub)])
                for hf in range(2):
                    pr = slice(64 * hf, 64 * hf + 64)
                    dsta = acc[pr, :, qsl]
                    srca = PS[pr, ub * 512 + hf * 128:ub * 512 + hf * 128 + 512].rearrange("p (a b) -> p a b", a=2)[:, :, 0:128]
                    if g == 0:
                        P.cp("act" if hf == 0 else "dve", dsta, srca, [psk(ub)], ["acc"])
                    else:
                        P.tt("dve", dsta, srca, dsta, ALU.add, [psk(ub), "acc"], ["acc"])

            pend = att_front(*blks[0])
            for bi_, (j, c) in enumerate(blks):
                nxt = att_front(*blks[bi_ + 1]) if bi_ + 1 < len(blks) else None
                att_back(j, c, *pend)
                pend = nxt
            if stop == "B4":
                return finalize()
        P.add("dve", lambda e: e.reciprocal(acc[:, 1, :], acc[:, 1, :]), ["acc"], ["acc"])
        P.tt("dve", yaT[:, sp_, :], acc[:, 0, :], acc[:, 1, :], ALU.mult, ["acc"], [f"yaT{sp_}"])
    if "yaT" in dbg_t:
        for sp_ in range(4):
            P.cp("dve", acc[:, 0, :], yaT[:, sp_, :], [f"yaT{sp_}", "acc"], ["acc"])
            k = dbgs()
            P.dma("sp", dbg_t["yaT"].ap()[sp_ * 128:(sp_ + 1) * 128, :], acc[:, 0, :], k, ["acc"], [k])
    P.dma("sp", yaT_d.ap(), yaT[:], "yaTw", [f"yaT{q}" for q in range(4)], ["yaT_d"])
    P.barrier()
    A.pop()

    if stop == "B":
        return finalize()

    A.push()
    qTr = [A.alloc(f"qTr{i}", [2, 2048], BF16) for i in range(2)]
    kTr = [A.alloc(f"kTr{i}", [2, 2048], BF16) for i in range(2)]
    kd = [A.alloc(f"kd{i}", [16, 256], BF16) for i in range(2)]
    v16 = [A.alloc(f"v16{i}", [16, 512], BF16) for i in range(2)]
    Gb = [A.alloc(f"Gb{i}", [1024], F32) for i in range(2)]
    Rf = [A.alloc(f"Rf{i}", [2, 512], F32) for i in range(2)]
    R16 = [A.alloc(f"R16{i}", [2, 512], BF16) for i in range(2)]
    ra = [A.alloc(f"ra{i}", [4, 256], F32) for i in range(2)]
    qk16 = [A.alloc(f"qk16{i}", [512], BF16) for i in range(2)]
    pmr = [A.alloc(f"pmr{i}", [128], BF16) for i in range(2)]
    yo = [A.alloc(f"yo{i}", [512], F32) for i in range(2)]
    yy = [A.alloc(f"yy{i}", [512], F32) for i in range(2)]
    sg = [A.alloc(f"sg{i}", [512], F32) for i in range(2)]
    yr16 = [A.alloc(f"yr16{i}", [512], BF16) for i in range(2)]
    yst = [A.alloc(f"yst{i}", [4, 128], BF16) for i in range(2)]
    junkc = A.alloc("junkc", [512], BF16)
    agflag = A.alloc("agflag", [8], F32)
    PS67 = PS[:, 6 * 512:8 * 512]
    lgam = [math.log1p(-(2.0 ** (-5.0 - h))) for h in range(4)]
    cdhs = [float(np.float32(math.exp(128.0 * lgam[h]))) for h in range(4)]
    wq, wv, wg = wsl[0], wsl[1], wsl[2]

    def c_load(h):
        P.dma("pool", wq[:, :, 0:256], w_in_v[:, :, 256 * h:256 * h + 256], "wsl0", [], ["wsl0"])
        P.dma("pool", wq[:, :, 256:512], w_in_v[:, :, 1024 + 256 * h:1024 + 256 * h + 256], "wsl0", [], ["wsl0"])
        P.dma("pool", wv[:], w_in_v[:, :, 2048 + 512 * h:2048 + 512 * h + 512], "wsl1", [], ["wsl1"])

    def c_load_g(h):
        P.dma("pool", wg[:], w_in_v[:, :, 4096 + 512 * h:4096 + 512 * h + 512], "wsl2", [], ["wsl2"])

    def c_proj(h):
        hb = h % 2
        ksc = c32[:, O_KSC + h:O_KSC + h + 1]
        for t in range(16):
            bank = t % 2
            i = t % 2
            tsl = slice(t * 128, (t + 1) * 128)
            for k in range(8):
                P.mm(ps(bank), uTo[:, k, tsl], wq[:, k, :], k == 0, k == 7, [f"uT{16 + t}", "wsl0"], [psk(bank)])
            ps4 = ps(bank).rearrange("p (a b c) -> p a b c", a=2, b=2)
            x1, x2 = ps4[:, :, 0, :], ps4[:, :, 1, :]
            cb = bcast(cos_r[:, t, :], 1, 2)
            sb_ = bcast(sin_r[:, t, :], 1, 2)
            r4 = ra[i][:].rearrange("p a (b c) -> p a b c", b=2)
            P.tt("dve", r4[:, 0], x1, cb, ALU.mult, [psk(bank), "csr"], [f"ra{i}"])
            P.tt("dve", r4[:, 1], x2, sb_, ALU.mult, [psk(bank), "csr"], [f"ra{i}"])
            P.tt("dve", r4[:, 2], x2, cb, ALU.mult, [psk(bank), "csr"], [f"ra{i}"])
            P.tt("dve", r4[:, 3], x1, sb_, ALU.mult, [psk(bank), "csr"], [f"ra{i}"])
            q4 = qk16[i][:].rearrange("p (a b c) -> p a b c", a=2, b=2)
            P.tt("pool", q4[:, :, 0, :], r4[:, 0], r4[:, 1], ALU.subtract, [f"ra{i}"], [f"qk16{i}"])
            P.tt("pool", q4[:, :, 1, :], r4[:, 2], r4[:, 3], ALU.add, [f"ra{i}"], [f"qk16{i}"])
            P.act(kd[hb][:, t, :], qk16[i][:, 256:512], AF.Copy, [f"qk16{i}", "c32"], [f"kd{hb}"], scale=ksc)
            for j in range(4):
                P.tr(psb(2)[:, j * 128:(j + 1) * 128], qk16[i][:, j * 128:(j + 1) * 128], ident,
                     [f"qk16{i}", "c16"], [psk(2)])
            P.cp("act", qTr[hb][:, :, tsl], psb(2)[:, 0:256].rearrange("p (a b) -> p a b", a=2), [psk(2)], [f"qTr{hb}"])
            P.cp("dve", kTr[hb][:, :, tsl], psb(2)[:, 256:512].rearrange("p (a b) -> p a b", a=2), [psk(2)], [f"kTr{hb}"])
            vb = 3 + t % 2
            for k in range(8):
                P.mm(ps(vb), uTo[:, k, tsl], wv[:, k, :], k == 0, k == 7, [f"uT{16 + t}", "wsl1"], [psk(vb)])
            P.cp("act", v16[hb][:, t, :], ps(vb), [psk(vb)], [f"v16{hb}"])

    def c_passA(h):
        hb = h % 2
        Rflat = Rf[hb][:].rearrange("p a b -> p (a b)")
        P.add("dve", lambda e: e.memset(Rf[hb][:], 0.0), [], [f"Rf{hb}"])
        for n in range(16):
            P.mm(ps(6), kd[hb][:, n, 0:128], v16[hb][:, n, :], True, True, [f"kd{hb}", f"v16{hb}"], [psk(6)])
            P.mm(ps(7), kd[hb][:, n, 128:256], v16[hb][:, n, :], True, True, [f"kd{hb}", f"v16{hb}"], [psk(7)])
            P.stt("dve", Rflat, Rflat, cdhs[h], PS67, ALU.mult, ALU.add, [f"Rf{hb}", psk(6), psk(7)], [f"Rf{hb}"])
        P.barrier()
        P.dma("pool", loc_d[h].ap(), Rflat, f"agw{h}", [f"Rf{hb}"], [f"loc{h}"])
        if fake_ag:
            for r_ in range(8):
                P.dma("pool", gath_d[h].ap()[r_ * 128:(r_ + 1) * 128, :], loc_d[h].ap(), f"cc{h}", [f"loc{h}"], [f"gath{h}"])
        else:
            P.add("pool", lambda e, h=h: e.collective_compute(
                "AllGather", ALU.bypass, replica_groups=[list(range(NCORES))],
                ins=[loc_d[h].ap()], outs=[gath_d[h].ap()]), [f"loc{h}"], [f"gath{h}"], stream=f"cc{h}", inc=None)
            P.add("pool", lambda e: e.memset(agflag[:], 0.0), [f"gath{h}"], ["agflag"])
        P.barrier()

    def c_combine(h):
        hb = h % 2
        Rflat = Rf[hb][:].rearrange("p a b -> p (a b)")
        for r_ in range(8):
            gi = r_ % 2
            P.dma("sp", Gb[gi][:], gath_d[h].ap()[r_ * 128:(r_ + 1) * 128, :], f"Gb{gi}", [f"gath{h}"], [f"Gb{gi}"])
            cf = c32[:, O_COEF + r_ * 4 + h:O_COEF + r_ * 4 + h + 1]
            if r_ == 0:
                P.ts("dve", Rflat, Gb[gi][:], cf, None, ALU.mult, None, [f"Gb{gi}", "c32", f"Rf{hb}"], [f"Rf{hb}"])
            else:
                P.stt("dve", Rflat, Gb[gi][:], cf, Rflat, ALU.mult, ALU.add, [f"Gb{gi}", "c32", f"Rf{hb}"], [f"Rf{hb}"])
        P.cp("act", R16[0][:], Rf[hb][:], [f"Rf{hb}"], ["R160"])

    def c_passB(h):
        hb = h % 2
        Rflat = Rf[hb][:].rearrange("p a b -> p (a b)")
        qdc = c32[:, O_QDEC + h:O_QDEC + h + 1]
        DTh = c32[:, O_DT + h * 128:O_DT + (h + 1) * 128]
        kdk, vk, qk_, kk_ = f"kd{hb}", f"v16{hb}", f"qTr{hb}", f"kTr{hb}"
        def pb_front(n):
            i = n % 2
            rb, rn = n % 2, (n + 1) % 2
            nt = slice(n * 128, (n + 1) * 128)
            if n < 15:
                P.mm(ps(6), kd[hb][:, n, 0:128], v16[hb][:, n, :], True, True, [kdk, vk], [psk(6)])
                P.mm(ps(7), kd[hb][:, n, 128:256], v16[hb][:, n, :], True, True, [kdk, vk], [psk(7)])
                P.stt("dve", Rflat, Rflat, cdhs[h], PS67, ALU.mult, ALU.add, [f"Rf{hb}", psk(6), psk(7)], [f"Rf{hb}"])
                P.cp("act", R16[rn][:], Rf[hb][:], [f"Rf{hb}"], [f"R16{rn}"])
            gbk = n % 2
            for k in range(8):
                P.mm(ps(gbk), uTo[:, k, nt], wg[:, k, :], k == 0, k == 7, [f"uT{16 + n}", "wsl2"], [psk(gbk)])
            P.act(sg[i][:], ps(gbk), AF.Silu, [psk(gbk)], [f"sg{i}"])
            P.mm(ps(3, 128), kTr[hb][:, 0, nt], qTr[hb][:, 0, nt], True, False, [kk_, qk_], [psk(3)])
            P.mm(ps(3, 128), kTr[hb][:, 1, nt], qTr[hb][:, 1, nt], False, True, [kk_, qk_], [psk(3)])
            P.tt("dve", pmr[i][:], ps(3, 128), DTh, ALU.mult, [psk(3), "c32"], [f"pmr{i}"])
            P.mm(ps(5), qTr[hb][:, 0, nt], R16[rb][:, 0, :], True, False, [qk_, f"R16{rb}"], [psk(5)])
            P.mm(ps(5), qTr[hb][:, 1, nt], R16[rb][:, 1, :], False, True, [qk_, f"R16{rb}"], [psk(5)])
            P.mm(ps(4), pmr[i][:], v16[hb][:, n, :], True, True, [f"pmr{i}", vk], [psk(4)])
            P.cp("act", yo[i][:], ps(4), [psk(4)], [f"yo{i}"])
            P.stt("dve", yy[i][:], ps(5), qdc, yo[i][:], ALU.mult, ALU.add, [psk(5), f"yo{i}", "c32"], [f"yy{i}"])

        def pb_back(n):
            i = n % 2
            nt = slice(n * 128, (n + 1) * 128)
            if "yraw" in dbg_t:
                kq = dbgs()
                P.dma("sp", dbg_t["yraw"].ap()[n * 128:(n + 1) * 128, h * 512:(h + 1) * 512], yy[i][:], kq, [f"yy{i}"], [kq])
            rs, k_ = rms_rstd(yy[i][:], 512, [f"yy{i}"], junkc)
            P.stt("dve", yr16[i][:], yy[i][:], rs, sg[i][:], ALU.mult, ALU.mult, [f"yy{i}", k_, f"sg{i}"], [f"yr16{i}"])
            for j in range(4):
                P.tr(psb(2)[:, j * 128:(j + 1) * 128], yr16[i][:, j * 128:(j + 1) * 128], ident,
                     [f"yr16{i}", "c16"], [psk(2)])
            P.cp("act", yst[i][:], psb(2)[:, 0:512].rearrange("p (a b) -> p a b", a=4), [psk(2)], [f"yst{i}"])
            P.dma("sp", yrT_d.ap()[:, h * 4:(h + 1) * 4, nt], yst[i][:], f"yst{i}", [f"yst{i}"], [f"yrT_d{h}_{n}"])

        pb_front(0)
        for n in range(16):
            if n + 1 < 16:
                pb_front(n + 1)
            pb_back(n)

    c_load(0)
    c_load_g(0)
    for h in range(4):
        c_proj(h)
        if h + 1 < 4:
            c_load(h + 1)
        c_passA(h)
        c_combine(h)
        c_passB(h)
        if h + 1 < 4:
            c_load_g(h + 1)
    P.barrier()
    A.pop()
    if "yrT" in dbg_t:
        A.push()
        bnc = A.alloc("bnc", [16, 2048], BF16)
        P.dma("sp", bnc[:], yrT_d.ap(), "bnc", [], ["bnc"])
        kq = dbgs()
        P.dma("pool", dbg_t["yrT"].ap().rearrange("(k p) t -> p k t", p=128), bnc[:], kq, ["bnc"], [kq])
        P.barrier()
        A.pop()
    if stop == "C":
        return finalize()

    A.push()
    yrT = A.alloc("yrT", [16, 2048], BF16)
    yaT = A.alloc("yaTl", [4, 2048], BF16)
    P.dma("sp", yaT[:], yaT_d.ap(), "yaTl", [], ["yaTl"])
    wm = [{"wr": A.alloc(f"wr{i}", [16, 128], BF16), "wdo": A.alloc(f"wdo{i}", [4, 128], BF16),
           "wgr": A.alloc(f"wgr{i}", [8, 128], BF16), "wga": A.alloc(f"wga{i}", [8, 128], BF16)} for i in range(2)]
    sr = [A.alloc(f"sr{i}", [512], F32) for i in range(2)]
    sa = [A.alloc(f"sa{i}", [512], F32) for i in range(2)]
    m1 = [A.alloc(f"m1{i}", [512], F32) for i in range(2)]
    m2 = [A.alloc(f"m2{i}", [512], F32) for i in range(2)]
    mx = [A.alloc(f"mx{i}", [512], BF16) for i in range(2)]
    for q in range(4):
        P.dma("sp", yrT[:, q * 4:(q + 1) * 4, :], yrT_d.ap()[:, q * 4:(q + 1) * 4, :], "yrTl", [], ["yrT"])
    wro_v = w_ret_out.ap().rearrange("(k p) c -> p k c", p=128)
    wdo_v = w_dil_out.ap().rearrange("(k p) c -> p k c", p=128)

    def d1_load(m):
        sl = m % 2
        w = wm[sl]
        st_ = f"wm{sl}"
        cs_ = slice(m * 128, (m + 1) * 128)
        P.dma("pool", w["wr"][:], wro_v[:, :, cs_], st_, [], [st_])
        P.dma("pool", w["wdo"][:], wdo_v[:, :, cs_], st_, [], [st_])
        P.dma("pool", w["wgr"][:], w_in_v[:, :, 10752 + m * 128:10752 + (m + 1) * 128], st_, [], [st_])
        P.dma("pool", w["wga"][:], w_in_v[:, :, 11776 + m * 128:11776 + (m + 1) * 128], st_, [], [st_])

    d1_load(0)
    cnt_d1 = 0
    for m in range(8):
        if m + 1 < 8:
            d1_load(m + 1)
        sl = m % 2
        w = wm[sl]
        wk = f"wm{sl}"
        for b in range(4):
            base = 4 * (cnt_d1 % 2)
            i = cnt_d1 % 2
            cnt_d1 += 1
            bs = slice(b * 512, (b + 1) * 512)
            ukeys = [f"uT{16 + b * 4 + q}" for q in range(4)]
            for k in range(16):
                P.mm(ps(base), w["wr"][:, k, :], yrT[:, k, bs], k == 0, k == 15, [wk, "yrT"], [psk(base)])
            for k in range(4):
                P.mm(ps(base + 1), w["wdo"][:, k, :], yaT[:, k, bs], k == 0, k == 3, [wk, "yaTl"], [psk(base + 1)])
            for k in range(8):
                P.mm(ps(base + 2), w["wgr"][:, k, :], uTo[:, k, bs], k == 0, k == 7, [wk] + ukeys, [psk(base + 2)])
            for k in range(8):
                P.mm(ps(base + 3), w["wga"][:, k, :], uTo[:, k, bs], k == 0, k == 7, [wk] + ukeys, [psk(base + 3)])
            P.act(sr[i][:], ps(base + 2), AF.Sigmoid, [psk(base + 2), "c32"], [f"sr{i}"], bias=c32[:, O_BGT + m:O_BGT + m + 1])
            P.act(sa[i][:], ps(base + 3), AF.Sigmoid, [psk(base + 3), "c32"], [f"sa{i}"], bias=c32[:, O_BGT + 8 + m:O_BGT + 8 + m + 1])
            P.tt("dve", m1[i][:], ps(base), sr[i][:], ALU.mult, [psk(base), f"sr{i}"], [f"m1{i}"])
            P.tt("dve", m2[i][:], ps(base + 1), sa[i][:], ALU.mult, [psk(base + 1), f"sa{i}"], [f"m2{i}"])
            P.tt("pool", mx[i][:], m1[i][:], m2[i][:], ALU.add, [f"m1{i}", f"m2{i}"], [f"mx{i}"])
            P.dma("sp", mixT_d.ap()[:, m, bs], mx[i][:], f"mx{i}", [f"mx{i}"], [f"mixT_d{m}_{b}"])
    P.barrier()
    A.pop()
    A.pop()
    if "mixT" in dbg_t:
        A.push()
        bnc = A.alloc("bnc2", [8, 2048], BF16)
        P.dma("sp", bnc[:], mixT_d.ap(), "bnc2", [], ["bnc2"])
        kq = dbgs()
        P.dma("pool", dbg_t["mixT"].ap().rearrange("(k p) t -> p k t", p=128), bnc[:], kq, ["bnc2"], [kq])
        P.barrier()
        A.pop()
    if stop == "D1":
        return finalize()

    v2T = A.alloc("v2T", [8, 2048], BF16)
    facc = A.alloc("facc", [16, 1024], F32)
    A.push()
    wo = A.alloc("wo", [8, 1024], BF16)
    gpo = A.alloc("gpo", [1024], F32)
    gpm = A.alloc("gpm", [1024], F32)
    mixb = [A.alloc(f"mixb{i}", [8, 512], BF16) for i in range(2)]
    xt = [A.alloc(f"xt{i}", [1024], F32) for i in range(2)]
    tmpd = [A.alloc(f"tmpd{i}", [1024], F32) for i in range(2)]
    h1t = [A.alloc(f"h1t{i}", [1024], F32) for i in range(2)]
    v2t = [A.alloc(f"v2t{i}", [1024], BF16) for i in range(2)]
    junkd = A.alloc("junkd", [1024], BF16)
    P.dma("pool", wo[:], w_o.ap().rearrange("(k p) c -> p k c", p=128), "wo", [], ["wo"])
    P.dma("sp", gpo[:], gbc.ap()[1], "gpo", [], ["gpo"])
    P.dma("sp", gpm[:], gbc.ap()[2], "gpm", [], ["gpm"])
    def d2_loadb(b):
        bi = b % 2
        P.dma("sp", mixb[bi][:], mixT_d.ap()[:, :, b * 512:(b + 1) * 512], f"mixb{bi}", [], [f"mixb{bi}"])

    def d2_front(t):
        b, tt = t // 4, t % 4
        bi = b % 2
        bp = (t % 2) * 2
        if tt == 0 and b + 1 < 4:
            d2_loadb(b + 1)
        i = t % 2
        P.dma("sp", xt[i][:], x_all.ap()[2048 + t * 128:2048 + (t + 1) * 128, :], f"xt{i}", [], [f"xt{i}"])
        for c2 in range(2):
            for k in range(8):
                P.mm(ps(bp + c2), mixb[bi][:, k, tt * 128:(tt + 1) * 128], wo[:, k, c2 * 512:(c2 + 1) * 512],
                     k == 0, k == 7, [f"mixb{bi}", "wo"], [psk(bp + c2)])

    def d2_back(t):
        i = t % 2
        bp = (t % 2) * 2
        PSO = PS[:, bp * 512:bp * 512 + 1024]
        rs, k_ = rms_rstd(PSO, 1024, [psk(bp), psk(bp + 1)], junkd)
        P.stt("dve", tmpd[i][:], PSO, rs, gpo[:], ALU.mult, ALU.mult, [psk(bp), psk(bp + 1), k_, "gpo"], [f"tmpd{i}"])
        P.tt("pool", h1t[i][:], tmpd[i][:], xt[i][:], ALU.add, [f"tmpd{i}", f"xt{i}"], [f"h1t{i}"])
        P.dma("sp", h1_d.ap()[t * 128:(t + 1) * 128, :], h1t[i][:], f"h1w{i}", [f"h1t{i}"], [f"h1_d{t}"])
        rs2, k2 = rms_rstd(h1t[i][:], 1024, [f"h1t{i}"], junkd)
        P.stt("dve", v2t[i][:], h1t[i][:], rs2, gpm[:], ALU.mult, ALU.mult, [f"h1t{i}", k2, "gpm"], [f"v2t{i}"])
        pb4 = 4 + (t % 2)
        for kk in range(8):
            P.tr(psb(pb4)[:, kk * 128:(kk + 1) * 128], v2t[i][:, kk * 128:(kk + 1) * 128], ident, [f"v2t{i}", "c16"], [psk(pb4)])
        P.cp("act", v2T[:, :, t * 128:(t + 1) * 128], psb(pb4).rearrange("p (k c) -> p k c", k=8), [psk(pb4)], [f"v2T{t}"])

    d2_loadb(0)
    d2_front(0)
    for t in range(16):
        if t + 1 < 16:
            d2_front(t + 1)
        d2_back(t)
    P.barrier()
    A.pop()
    if "h1" in dbg_t:
        A.push()
        bnc = A.alloc("bnc3", [1024], F32)
        for t in range(16):
            P.dma("sp", bnc[:], h1_d.ap()[t * 128:(t + 1) * 128, :], "bnc3", [], ["bnc3"])
            kq = dbgs()
            P.dma("sp", dbg_t["h1"].ap()[t * 128:(t + 1) * 128, :], bnc[:], kq, ["bnc3"], [kq])
        P.barrier()
        A.pop()
    if stop == "D2":
        return finalize()

    A.push()
    wpg = A.alloc("wpg", [8, 1024], BF16)
    wpi = A.alloc("wpi", [2, 1024], BF16)
    bple = A.alloc("bple", [1024], F32)
    gl = A.alloc("gl", [1024], F32)
    gq = A.alloc("gq", [1024], F32)
    gpp = A.alloc("gpp", [1024], F32)
    A.push()
    wu = [A.alloc(f"wu{i}", [8, 512], BF16) for i in range(2)]
    wdn = [A.alloc(f"wdn{i}", [4, 1024], BF16) for i in range(2)]
    hid = [A.alloc(f"hid{i}", [4, 512], BF16) for i in range(2)]
    rl = [A.alloc(f"rl{i}", [512], F32) for i in range(2)]
    wup_v = w_up.ap().rearrange("(k p) c -> p k c", p=128)
    P.dma("sp", gq[:], gbc.ap()[3], "gq", [], ["gq"])
    P.dma("sp", gpp[:], gbc.ap()[4], "gpp", [], ["gpp"])
    P.dma("sp", bple[:], bple_bc.ap(), "bple", [], ["bple"])
    P.dma("sp", gl[:], gbc.ap()[5], "gl", [], ["gl"])

    def e_load(j):
        s_ = j % 2
        P.dma("pool", wu[s_][:], wup_v[:, :, j * 512:(j + 1) * 512], f"wu{s_}", [], [f"wu{s_}"])
        P.dma("pool", wdn[s_][:], w_down.ap()[j * 512:(j + 1) * 512, :].rearrange("(c p) n -> p c n", p=128),
              f"wdn{s_}", [], [f"wdn{s_}"])

    e_load(0)
    cnt_e = 0
    for j in range(8):
        if j + 1 < 8:
            e_load(j + 1)
        else:
            P.dma("pool", wpg[:], w_ple_gate.ap().rearrange("(k p) c -> p k c", p=128), "wpg", [], ["wpg"])
            P.dma("pool", wpi[:], w_ple_in.ap().rearrange("(k p) c -> p k c", p=128), "wpi", [], ["wpi"])
        s_ = j % 2
        for b in range(4):
            hi = cnt_e % 2
            cnt_e += 1
            bs = slice(b * 512, (b + 1) * 512)
            vkeys = [f"v2T{b * 4 + q}" for q in range(4)]
            for c in range(4):
                for k in range(8):
                    P.mm(ps(c), wu[s_][:, k, c * 128:(c + 1) * 128], v2T[:, k, bs], k == 0, k == 7, [f"wu{s_}"] + vkeys, [psk(c)])
                ri = c % 2
                P.act(rl[ri][:], ps(c), AF.Relu, [psk(c)], [f"rl{ri}"])
                P.tt("pool", hid[hi][:, c, :], rl[ri][:], rl[ri][:], ALU.mult, [f"rl{ri}"], [f"hid{hi}"])
            for tt in range(4):
                t = b * 4 + tt
                bp = 4 + (tt % 2) * 2
                PSO = PS[:, bp * 512:bp * 512 + 1024]
                for c2 in range(2):
                    for c in range(4):
                        P.mm(ps(bp + c2), hid[hi][:, c, tt * 128:(tt + 1) * 128], wdn[s_][:, c, c2 * 512:(c2 + 1) * 512],
                             c == 0, c == 3, [f"hid{hi}", f"wdn{s_}"], [psk(bp + c2)])
                if j == 0:
                    P.cp("act", facc[:, t, :], PSO, [psk(bp), psk(bp + 1)], [f"facc{t}"])
                else:
                    P.tt("dve", facc[:, t, :], PSO, facc[:, t, :], ALU.add, [psk(bp), psk(bp + 1), f"facc{t}"], [f"facc{t}"])
    P.barrier()
    A.pop()
    if stop == "E":
        return finalize()

    h1r = A.alloc("h1r", [1024], F32)
    tmpe = A.alloc("tmpe", [1024], F32)
    v3t = [A.alloc(f"v3t{i}", [1024], BF16) for i in range(2)]
    ptl = [A.alloc(f"ptl{i}", [256], F32) for i in range(2)]
    p16 = [A.alloc(f"p16{i}", [256], BF16) for i in range(2)]
    pT = [A.alloc(f"pT{i}", [2, 128], BF16) for i in range(2)]
    tg = [A.alloc(f"tg{i}", [1024], F32) for i in range(2)]
    ge = A.alloc("ge", [1024], F32)
    ot = [A.alloc(f"ot{i}", [1024], F32) for i in range(2)]
    junkf = A.alloc("junkf", [1024], BF16)
    outkeys = []

    def e_tail(t):
        i = t % 2
        rs, k_ = rms_rstd(facc[:, t, :], 1024, [f"facc{t}"], junkf)
        P.dma("sp", h1r[:], h1_d.ap()[t * 128:(t + 1) * 128, :], "h1r", [], ["h1r"])
        P.stt("dve", tmpe[:], facc[:, t, :], rs, gq[:], ALU.mult, ALU.mult, [f"facc{t}", k_, "gq"], ["tmpe"])
        P.tt("pool", facc[:, t, :], tmpe[:], h1r[:], ALU.add, ["tmpe", "h1r"], [f"facc{t}"])
        if "h2" in dbg_t:
            kq = dbgs()
            P.dma("sp", dbg_t["h2"].ap()[t * 128:(t + 1) * 128, :], facc[:, t, :], kq, [f"facc{t}"], [kq])
        rs2, k2 = rms_rstd(facc[:, t, :], 1024, [f"facc{t}"], junkf)
        P.stt("dve", v3t[i][:], facc[:, t, :], rs2, gpp[:], ALU.mult, ALU.mult, [f"facc{t}", k2, "gpp"], [f"v3t{i}"])
        for kk in range(8):
            P.tr(psb(7)[:, kk * 128:(kk + 1) * 128], v3t[i][:, kk * 128:(kk + 1) * 128], ident, [f"v3t{i}", "c16"], [psk(7)])
        P.cp("act", v2T[:, :, t * 128:(t + 1) * 128], psb(7).rearrange("p (k c) -> p k c", k=8), [psk(7)], [f"v2T{t}"])
        tsl = slice(t * 128, (t + 1) * 128)
        P.dma("sp", ptl[i][:], p_own.ap()[tsl, :], f"ptl{i}", [], [f"ptl{i}"])
        P.cp("dve", p16[i][:], ptl[i][:], [f"ptl{i}"], [f"p16{i}"])
        for kk in range(2):
            P.tr(psb(6)[:, kk * 128:(kk + 1) * 128], p16[i][:, kk * 128:(kk + 1) * 128], ident, [f"p16{i}", "c16"], [psk(6)])
        P.cp("act", pT[i][:], psb(6)[:, 0:256].rearrange("p (a b) -> p a b", a=2), [psk(6)], [f"pT{i}"])

    def f_main(t):
        i = t % 2
        tsl = slice(t * 128, (t + 1) * 128)
        bp = 0 if t % 2 == 0 else 4
        PSG = PS[:, bp * 512:bp * 512 + 1024]
        PSE = PS[:, 2 * 512:2 * 512 + 1024]
        for c2 in range(2):
            for k in range(8):
                P.mm(ps(bp + c2), v2T[:, k, tsl], wpg[:, k, c2 * 512:(c2 + 1) * 512], k == 0, k == 7, [f"v2T{t}", "wpg"], [psk(bp + c2)])
        for c2 in range(2):
            for k in range(2):
                P.mm(ps(2 + c2), pT[i][:, k, :], wpi[:, k, c2 * 512:(c2 + 1) * 512], k == 0, k == 1, [f"pT{i}", "wpi"], [psk(2 + c2)])
        P.tt("dve", tg[i][:], PSG, bple[:], ALU.add, [psk(bp), psk(bp + 1), "bple"], [f"tg{i}"])
        P.act(tg[i][:], tg[i][:], AF.Sigmoid, [f"tg{i}"], [f"tg{i}"])
        P.tt("dve", ge[:], PSE, tg[i][:], ALU.mult, [psk(2), psk(3), f"tg{i}"], ["ge"])
        rs, k_ = rms_rstd(ge[:], 1024, ["ge"], junkf)
        P.stt("dve", tg[i][:], ge[:], rs, gl[:], ALU.mult, ALU.mult, ["ge", k_, "gl", f"tg{i}"], [f"tg{i}"])
        P.tt("pool", ot[i][:], tg[i][:], facc[:, t, :], ALU.add, [f"tg{i}", f"facc{t}"], [f"ot{i}"])
        ok_ = f"outw{t}"
        outkeys.append(ok_)
        P.dma("sp", out.ap()[tsl, :], ot[i][:], f"ot{i}", [f"ot{i}"], [ok_])

    e_tail(0)
    for t in range(16):
        if t + 1 < 16:
            e_tail(t + 1)
        f_main(t)
    dbgkeys.extend(outkeys)
    A.pop()
    return finalize()


def _consts(core):
    c = np.zeros((128, NCST), np.float32)
    half = 128
    inv_freq = (1.0 / (np.float32(10000.0) ** np.linspace(0.0, 1.0, half, dtype=np.float32))).astype(np.float32)
    c[:, O_INVF:O_INVF + 128] = inv_freq[None, :]
    H = 4
    lg = np.log1p(-(2.0 ** (-5.0 - np.arange(H, dtype=np.float64))))
    idx = np.arange(128, dtype=np.float64)
    for h in range(H):
        diff = idx[None, :] - idx[:, None]
        DT = np.where(diff >= 0, np.exp(np.maximum(diff, 0.0) * lg[h]), 0.0) * (256.0 ** -0.5)
        c[:, O_DT + h * 128:O_DT + (h + 1) * 128] = DT.astype(np.float32)
        c[:, O_KSC + h] = (np.exp((127.0 - idx) * lg[h]) * (256.0 ** -0.5)).astype(np.float32)
        c[:, O_QDEC + h] = np.exp((idx + 1.0) * lg[h]).astype(np.float32)
        for r in range(8):
            if r < core:
                c[:, O_COEF + r * 4 + h] = np.float32(np.exp(128.0 * 16.0 * (core - 1 - r) * lg[h]))
    freqs = (np.float32(500000.0) ** (-np.arange(0, 16, 2, dtype=np.float32) / np.float32(16))).astype(np.float32)
    for p in range(128):
        d = p % 64
        if d < 16:
            c[p, O_FREQD] = freqs[d % 8]
            c[p, O_SSIGN] = -1.0 if d < 8 else 1.0
    return c


def _consts16(core):
    c = np.zeros((128, NC16), np.float32)
    c[:, O_ID:O_ID + 128] = np.eye(128, dtype=np.float32)
    perm = np.zeros((128, 128), np.float32)
    for m in range(128):
        d = m % 64
        if d < 8:
            perm[m + 8, m] = 1.0
        elif d < 16:
            perm[m - 8, m] = 1.0
    c[:, O_PERM:O_PERM + 128] = perm
    j = np.arange(128)[:, None]
    i = np.arange(128)[None, :]
    mprev = (j >= i).astype(np.float32)
    mcur = (j <= i).astype(np.float32)
    c[:, O_MK:O_MK + 128] = mprev
    c[:, O_MK + 128:O_MK + 256] = mcur
    c[:, O_MK0:O_MK0 + 128] = mprev if core > 0 else 0.0
    c[:, O_MK0 + 128:O_MK0 + 256] = mcur
    c[:, O_ONES:O_ONES + 64] = 1.0
    c[:, O_ONES + 128 + 64:O_ONES + 256] = 1.0
    return c


def make_in_maps(inputs):
    x = np.asarray(inputs["x"], np.float32)[0]
    p = np.asarray(inputs["p"], np.float32)[0, 0]
    pos = np.asarray(inputs["positions"], np.int32)[0]
    sq = lambda n: np.ascontiguousarray(np.asarray(inputs[n], np.float32)[0])
    shared = {
        "w_in": sq("w_in"), "w_ret_out": sq("w_ret_out"), "w_dil_out": sq("w_dil_out"), "w_o": sq("w_o"),
        "w_up": sq("w_up"), "w_down": sq("w_down"), "w_ple_gate": sq("w_ple_gate"), "w_ple_in": sq("w_ple_in"),
    }
    gnames = ["g_pre_mix", "g_post_mix", "g_pre_mlp", "g_post_mlp", "g_pre_ple", "g_post_ple"]
    gb = np.stack([np.broadcast_to(np.asarray(inputs[n], np.float32)[0][None, :], (128, 1024)) for n in gnames], 0)
    shared["gbc"] = np.ascontiguousarray(gb)
    shared["bple_bc"] = np.ascontiguousarray(np.broadcast_to(np.asarray(inputs["b_ple_gate"], np.float32)[0][None, :], (128, 1024)))
    bg = np.asarray(inputs["b_gate"], np.float32)[0]
    bgT = bg.reshape(2, 8, 128).transpose(2, 0, 1).reshape(128, 16)
    maps = []
    for c in range(NCORES):
        lo = c * TOK
        xa = np.zeros((4096, 1024), np.float32)
        pa = np.zeros((4096,), np.int32)
        if c > 0:
            xa[:2048] = x[lo - 2048:lo]
            pa[:2048] = pos[lo - 2048:lo]
        xa[2048:] = x[lo:lo + TOK]
        pa[2048:] = pos[lo:lo + TOK]
        cs = _consts(c)
        cs[:, O_BGT:O_BGT + 16] = bgT
        m = dict(shared)
        m.update({
            "x_all": xa,
            "p_own": np.ascontiguousarray(p[lo:lo + TOK]),
            "pos_bc": np.ascontiguousarray(np.broadcast_to(pa[None, :], (128, 4096))),
            "pos_tm": np.ascontiguousarray(pa[2048:].reshape(16, 128).T),
            "cst": cs,
            "cst16": _consts16(c),
        })
        maps.append(m)
    return maps


_NC_CACHE = {}


def kernel(**inputs):
    if "nc" not in _NC_CACHE:
        _NC_CACHE["nc"] = build()
    nc = _NC_CACHE["nc"]
    maps = make_in_maps(inputs)
    res = run_bass_kernel_spmd(nc, maps, core_ids=list(range(NCORES)))
    outs = [np.asarray(r["out"], np.float32) for r in res.results]
    return np.concatenate(outs, 0)[None]
```
